# Optimizing a Trainium2 kernel written in Bass

```python
import math
import jax, jax.numpy as jnp
from jax import lax
import numpy as np

D_MODEL = 1024
BATCH = 8
SEQ = 4096
DEPTH = 2

D_ATTN = D_MODEL // 2
HEAD_DIM = 64
N_HEADS_A = D_ATTN // HEAD_DIM
ROT_DIM = HEAD_DIM // 4
ROPE_THETA = 500000.0
DILATED_PATTERNS = ((128, 1), (512, 4), (2048, 16))
D_POOL = D_MODEL - D_ATTN
POOL_WINDOWS = (2, 4, 8, 16)
N_POOL_GROUPS = len(POOL_WINDOWS)
POOL_C = D_POOL // N_POOL_GROUPS
D_IN_EVEN = 3 * D_ATTN + D_POOL
S5_GROUP = 16
S5_GROUPS = D_MODEL // S5_GROUP
S5_STATE = 64
D_FF = ((8 * D_MODEL // 3 + 255) // 256) * 256
N_EVEN = (DEPTH + 1) // 2
N_ODD = DEPTH // 2
EPS = 1e-6

kernel_name = 'hybrid_dilated_pool_s5_macaron'


def _rmsnorm(x, g):
    xf = x.astype(jnp.float32)
    y = xf * lax.rsqrt(jnp.mean(xf * xf, axis=-1, keepdims=True) + EPS)
    return (y * g.astype(jnp.float32)).astype(x.dtype)


def _swiglu(h, w_gate, w_up, w_down):
    return (jax.nn.silu(h @ w_gate) * (h @ w_up)) @ w_down


def _rotary_tables(positions):
    inv_freq = ROPE_THETA ** (-jnp.arange(0, ROT_DIM, 2, dtype=jnp.float32) / ROT_DIM)
    ang = positions.astype(jnp.float32)[..., None] * inv_freq
    return jnp.cos(ang)[:, :, None, :], jnp.sin(ang)[:, :, None, :]


def _partial_rotary(t, cos, sin):
    half = ROT_DIM // 2
    tf = t[..., :ROT_DIM].astype(jnp.float32)
    t1, t2 = tf[..., :half], tf[..., half:]
    rot = jnp.concatenate([t1 * cos - t2 * sin, t2 * cos + t1 * sin], axis=-1).astype(t.dtype)
    return jnp.concatenate([rot, t[..., ROT_DIM:]], axis=-1)


def _dilated_window_attention(q, k, v, window, dilation):
    Bsz, S, H, Dh = q.shape
    n_back = window // dilation
    blk = n_back
    L = S // dilation
    nb = -(-L // blk)
    Lp = nb * blk

    def to_sub(t):
        return t.reshape(Bsz, L, dilation, H, Dh).transpose(0, 2, 3, 1, 4)

    qs = jnp.pad(to_sub(q), ((0, 0), (0, 0), (0, 0), (0, Lp - L), (0, 0)))
    qs = qs.reshape(Bsz, dilation, H, nb, blk, Dh)

    def windows(t):
        t = jnp.pad(to_sub(t), ((0, 0), (0, 0), (0, 0), (blk, Lp - L), (0, 0)))
        t = t.reshape(Bsz, dilation, H, nb + 1, blk, Dh)
        return jnp.concatenate([t[:, :, :, :-1], t[:, :, :, 1:]], axis=-2)

    kw, vw = windows(k), windows(v)
    scores = jnp.einsum('bdhnqe,bdhnke->bdhnqk', qs, kw,
                        preferred_element_type=jnp.float32) * (Dh ** -0.5)
    qi = jnp.arange(blk)[:, None]
    kj = jnp.arange(2 * blk)[None, :]
    delta = qi - kj + blk
    blk_idx = jnp.arange(nb)[:, None, None]
    valid = (delta >= 0) & (delta <= n_back) & (blk_idx * blk - blk + kj >= 0)
    scores = jnp.where(valid, scores, -jnp.inf)
    lse = jax.nn.logsumexp(scores, axis=-1)
    probs = jnp.exp(scores - lse[..., None])
    out = jnp.einsum('bdhnqk,bdhnke->bdhnqe', probs.astype(vw.dtype), vw)
    out = out.reshape(Bsz, dilation, H, Lp, Dh)[:, :, :, :L]
    out = out.transpose(0, 3, 1, 2, 4).reshape(Bsz, S, H, Dh)
    lse = lse.reshape(Bsz, dilation, H, Lp)[..., :L].transpose(0, 3, 1, 2).reshape(Bsz, S, H)
    return out, lse


def _multiscale_pool(p, pool_w, pool_scale):
    Bsz, S, _ = p.shape
    pf = p.astype(jnp.float32).reshape(Bsz, S, N_POOL_GROUPS, POOL_C)
    cs0 = jnp.pad(jnp.cumsum(pf, axis=1), ((0, 0), (1, 0), (0, 0), (0, 0)))
    t1 = jnp.arange(1, S + 1)
    pooled = []
    for g, w in enumerate(POOL_WINDOWS):
        upper = cs0[:, 1:, g]
        lower = jnp.pad(cs0[:, :S + 1 - w, g], ((0, 0), (w - 1, 0), (0, 0)))
        count = jnp.minimum(t1, w).astype(jnp.float32)[None, :, None]
        pooled.append((upper - lower) / count - pf[:, :, g])
    pooled = jnp.stack(pooled, axis=2)
    y = jnp.einsum('bsgc,gce->bsge', pooled, pool_w.astype(jnp.float32)).reshape(Bsz, S, D_POOL)
    return (y * pool_scale.astype(jnp.float32)).astype(p.dtype)


def _even_mixer(h, cos, sin, w_in, q_norm, k_norm, pool_w, pool_scale, w_out):
    Bsz, S, _ = h.shape
    proj = h @ w_in
    q, k, v, p = jnp.split(proj, [D_ATTN, 2 * D_ATTN, 3 * D_ATTN], axis=-1)
    shp = (Bsz, S, N_HEADS_A, HEAD_DIM)
    q = _partial_rotary(_rmsnorm(q.reshape(shp), q_norm), cos, sin)
    k = _partial_rotary(_rmsnorm(k.reshape(shp), k_norm), cos, sin)
    v = v.reshape(shp)
    outs, lses = [], []
    for window, dilation in DILATED_PATTERNS:
        o, l = _dilated_window_attention(q, k, v, window, dilation)
        outs.append(o)
        lses.append(l)
    wts = jax.nn.softmax(jnp.stack(lses, axis=0), axis=0)
    attn = jnp.einsum('pbsh,pbshe->bshe', wts, jnp.stack(outs, axis=0).astype(jnp.float32))
    attn = attn.reshape(Bsz, S, D_ATTN).astype(h.dtype)
    pool = _multiscale_pool(p, pool_w, pool_scale)
    return jnp.concatenate([attn, pool], axis=-1) @ w_out


def _s5_ssm(u, a_re, a_im, log_dt, b_re, b_im, c_re, c_im, d_skip):
    Bsz, S, _ = u.shape
    f32 = jnp.float32
    uf = u.astype(f32).reshape(Bsz, S, S5_GROUPS, S5_GROUP)
    lam_re = jnp.minimum(a_re.astype(f32), -1e-4)
    lam_im = a_im.astype(f32)
    dt = jnp.exp(log_dt.astype(f32))[:, None]
    mag = jnp.exp(lam_re * dt)
    lbar_re = mag * jnp.cos(lam_im * dt)
    lbar_im = mag * jnp.sin(lam_im * dt)
    den = lam_re * lam_re + lam_im * lam_im
    nre = lbar_re - 1.0
    coef_re = (nre * lam_re + lbar_im * lam_im) / den
    coef_im = (lbar_im * lam_re - nre * lam_im) / den
    br, bi = b_re.astype(f32), b_im.astype(f32)
    bb_re = coef_re[..., None] * br - coef_im[..., None] * bi
    bb_im = coef_re[..., None] * bi + coef_im[..., None] * br
    bu_re = jnp.einsum('bsgc,gpc->bsgp', uf, bb_re)
    bu_im = jnp.einsum('bsgc,gpc->bsgp', uf, bb_im)
    a_seq_re = jnp.broadcast_to(lbar_re, (1, S, S5_GROUPS, S5_STATE))
    a_seq_im = jnp.broadcast_to(lbar_im, (1, S, S5_GROUPS, S5_STATE))

    def combine(e1, e2):
        ar1, ai1, br1, bi1 = e1
        ar2, ai2, br2, bi2 = e2
        return (ar2 * ar1 - ai2 * ai1,
                ar2 * ai1 + ai2 * ar1,
                ar2 * br1 - ai2 * bi1 + br2,
                ar2 * bi1 + ai2 * br1 + bi2)

    _, _, h_re, h_im = lax.associative_scan(combine, (a_seq_re, a_seq_im, bu_re, bu_im), axis=1)
    y = (jnp.einsum('bsgp,gcp->bsgc', h_re, c_re.astype(f32))
         - jnp.einsum('bsgp,gcp->bsgc', h_im, c_im.astype(f32)))
    return y.reshape(Bsz, S, D_MODEL) + d_skip.astype(f32) * uf.reshape(Bsz, S, D_MODEL)


def _odd_mixer(h, a_re, a_im, log_dt, b_re, b_im, c_re, c_im, d_skip, w_glu):
    y = _s5_ssm(h, a_re, a_im, log_dt, b_re, b_im, c_re, c_im, d_skip)
    z = jax.nn.gelu(y).astype(h.dtype)
    val, gate = jnp.split(z @ w_glu, 2, axis=-1)
    return val * jax.nn.sigmoid(gate)


def setup_inputs(seed: int = 0) -> dict:
    key = jax.random.key(seed)
    ks = jax.random.split(key, 24)
    f32 = jnp.float32

    def nrm(k, shape, scale):
        return scale * jax.random.normal(k, shape, f32)

    x = nrm(ks[0], (BATCH, SEQ, D_MODEL), 1.0)
    positions = (jax.random.randint(ks[1], (BATCH, 1), 0, 1024, jnp.int32)
                 + jnp.arange(SEQ, dtype=jnp.int32)[None, :])
    ffn_norm = 1.0 + nrm(ks[2], (DEPTH, 2, D_MODEL), 0.02)
    ffn_w_gate = nrm(ks[3], (DEPTH, 2, D_MODEL, D_FF), D_MODEL ** -0.5)
    ffn_w_up = nrm(ks[4], (DEPTH, 2, D_MODEL, D_FF), D_MODEL ** -0.5)
    ffn_w_down = nrm(ks[5], (DEPTH, 2, D_FF, D_MODEL), D_FF ** -0.5)
    mix_norm = 1.0 + nrm(ks[6], (DEPTH, D_MODEL), 0.02)
    ev_w_in = nrm(ks[7], (N_EVEN, D_MODEL, D_IN_EVEN), D_MODEL ** -0.5)
    ev_q_norm = 1.0 + nrm(ks[8], (N_EVEN, HEAD_DIM), 0.02)
    ev_k_norm = 1.0 + nrm(ks[9], (N_EVEN, HEAD_DIM), 0.02)
    ev_pool_w = nrm(ks[10], (N_EVEN, N_POOL_GROUPS, POOL_C, POOL_C), POOL_C ** -0.5)
    ev_pool_scale = 1.0 + nrm(ks[11], (N_EVEN, D_POOL), 0.02)
    ev_w_out = nrm(ks[12], (N_EVEN, D_MODEL, D_MODEL), D_MODEL ** -0.5)
    s5_a_re = -0.5 + nrm(ks[13], (N_ODD, S5_GROUPS, S5_STATE), 0.01)
    s5_a_im = (math.pi * jnp.arange(S5_STATE, dtype=f32))[None, None, :] + nrm(ks[14], (N_ODD, S5_GROUPS, S5_STATE), 0.01)
    s5_log_dt = jax.random.uniform(ks[15], (N_ODD, S5_GROUPS), f32, math.log(0.001), math.log(0.1))
    s5_b_re = nrm(ks[16], (N_ODD, S5_GROUPS, S5_STATE, S5_GROUP), (2 * S5_GROUP) ** -0.5)
    s5_b_im = nrm(ks[17], (N_ODD, S5_GROUPS, S5_STATE, S5_GROUP), (2 * S5_GROUP) ** -0.5)
    s5_c_re = nrm(ks[18], (N_ODD, S5_GROUPS, S5_GROUP, S5_STATE), S5_STATE ** -0.5)
    s5_c_im = nrm(ks[19], (N_ODD, S5_GROUPS, S5_GROUP, S5_STATE), S5_STATE ** -0.5)
    s5_d = nrm(ks[20], (N_ODD, D_MODEL), 1.0)
    s5_w_glu = nrm(ks[21], (N_ODD, D_MODEL, 2 * D_MODEL), D_MODEL ** -0.5)
    return {'x': x, 'positions': positions,
            'ffn_norm': ffn_norm, 'ffn_w_gate': ffn_w_gate, 'ffn_w_up': ffn_w_up, 'ffn_w_down': ffn_w_down,
            'mix_norm': mix_norm,
            'ev_w_in': ev_w_in, 'ev_q_norm': ev_q_norm, 'ev_k_norm': ev_k_norm,
            'ev_pool_w': ev_pool_w, 'ev_pool_scale': ev_pool_scale, 'ev_w_out': ev_w_out,
            's5_a_re': s5_a_re, 's5_a_im': s5_a_im, 's5_log_dt': s5_log_dt,
            's5_b_re': s5_b_re, 's5_b_im': s5_b_im, 's5_c_re': s5_c_re, 's5_c_im': s5_c_im,
            's5_d': s5_d, 's5_w_glu': s5_w_glu}


def reference(x, positions, ffn_norm, ffn_w_gate, ffn_w_up, ffn_w_down, mix_norm,
              ev_w_in, ev_q_norm, ev_k_norm, ev_pool_w, ev_pool_scale, ev_w_out,
              s5_a_re, s5_a_im, s5_log_dt, s5_b_re, s5_b_im, s5_c_re, s5_c_im,
              s5_d, s5_w_glu):
    cos, sin = _rotary_tables(positions)
    for layer in range(DEPTH):
        x = x + 0.5 * _swiglu(_rmsnorm(x, ffn_norm[layer, 0]), ffn_w_gate[layer, 0],
                              ffn_w_up[layer, 0], ffn_w_down[layer, 0])
        h = _rmsnorm(x, mix_norm[layer])
        j = layer // 2
        if layer % 2 == 0:
            mixed = _even_mixer(h, cos, sin, ev_w_in[j], ev_q_norm[j], ev_k_norm[j],
                                ev_pool_w[j], ev_pool_scale[j], ev_w_out[j])
        else:
            mixed = _odd_mixer(h, s5_a_re[j], s5_a_im[j], s5_log_dt[j], s5_b_re[j], s5_b_im[j],
                               s5_c_re[j], s5_c_im[j], s5_d[j], s5_w_glu[j])
        x = x + mixed.astype(x.dtype)
        x = x + 0.5 * _swiglu(_rmsnorm(x, ffn_norm[layer, 1]), ffn_w_gate[layer, 1],
                              ffn_w_up[layer, 1], ffn_w_down[layer, 1])
    return x
```

```python
import contextlib
import math
import numpy as np
import concourse.bass as bass
import concourse.mybir as mybir
from concourse.bass_utils import run_bass_kernel_spmd

F32 = mybir.dt.float32
BF16 = mybir.dt.bfloat16
I32 = mybir.dt.int32
AF = mybir.ActivationFunctionType
ALU = mybir.AluOpType

S = 4096
D = 1024
DFF = 2816
NFC = DFF // 128
EPS = 1e-6


class Sem:
    _n = 0

    def __init__(self, h):
        self.h = h
        Sem._n += 1
        self.k = Sem._n
        self.v = 0


class Eng:
    def __init__(self, prog, eng, name):
        self.e = eng
        self.sem = prog.newsem("e_" + name)
        self.waited = {}

    def wait(self, deps):
        for ev in deps:
            if ev is None:
                continue
            sem, val = ev
            if self.waited.get(sem.k, 0) >= val:
                continue
            self.e.wait_ge(sem.h, val)
            self.waited[sem.k] = val

    def done(self, ins):
        self.sem.v += 1
        ins.then_inc(self.sem.h, 1)
        return (self.sem, self.sem.v)


class Prog:
    def __init__(self):
        self.nc = bass.Bass("TRN2", target_bir_lowering=False)
        self.es = contextlib.ExitStack()
        self._nm = 0
        self.sem_pool = []
        self.phase_sems = None
        nc = self.nc
        self.pe = Eng(self, nc.tensor, "pe")
        self.act = Eng(self, nc.scalar, "act")
        self.dve = Eng(self, nc.vector, "dve")
        self.pool = Eng(self, nc.gpsimd, "pool")
        self.sp = Eng(self, nc.sync, "sp")

    def newsem(self, name):
        if self.sem_pool:
            sm = self.sem_pool.pop()
        else:
            self._nm += 1
            sm = Sem(self.es.enter_context(self.nc.semaphore("%s_s%d" % (name, self._nm))))
        if self.phase_sems is not None:
            self.phase_sems.append(sm)
        return sm

    def begin_phase(self):
        self.phase_sems = []

    def end_phase(self):
        self.sem_pool.extend(self.phase_sems)
        self.phase_sems = None

    def name(self, p):
        self._nm += 1
        return "%s_%d" % (p, self._nm)

    def sbuf(self, shape, dt, name="t", stack=None):
        return (stack or self.es).enter_context(self.nc.sbuf_tensor(self.name(name), list(shape), dt))

    def psum(self, shape, dt, name="p", stack=None):
        return (stack or self.es).enter_context(self.nc.psum_tensor(self.name(name), list(shape), dt))

    def op(self, eng, fn, deps=()):
        eng.wait(deps)
        return eng.done(fn(eng.e))

    def op_nosig(self, eng, fn, deps=()):
        eng.wait(deps)
        fn(eng.e)

    def barrier(self, extra=()):
        evs = [(e.sem, e.sem.v) for e in (self.pe, self.act, self.dve, self.pool) if e.sem.v > 0] + list(extra)
        for e in (self.pe, self.act, self.dve, self.pool, self.sp):
            e.wait(evs)

    def dma(self, q, sem, out, in_, deps=(), **kw):
        q.wait(deps)
        sem.v += 16
        q.e.dma_start(out=out, in_=in_, **kw).then_inc(sem.h, 16)
        return (sem, sem.v)


class Ring:
    def __init__(self, prog, n, shape, dt, name, stack=None):
        self.n = n
        self.bufs = [prog.sbuf(shape, dt, name, stack) for _ in range(n)]
        self.sems = [prog.newsem(name + "_ld%d" % i) for i in range(n)]
        self.free = [[] for _ in range(n)]
        self.i = 0

    def next(self):
        s = self.i % self.n
        self.i += 1
        fr = self.free[s]
        self.free[s] = []
        return s, self.bufs[s], self.sems[s], fr


class Chain:
    def __init__(self, P):
        self.P = P
        self.last = None

    def _do(self, eng, fn, extra=()):
        self.last = self.P.op(eng, fn, deps=[self.last] + list(extra))
        return self.last

    def dve(self, fn, extra=()):
        return self._do(self.P.dve, fn, extra)

    def act(self, fn, extra=()):
        return self._do(self.P.act, fn, extra)

    def pool(self, fn, extra=()):
        return self._do(self.P.pool, fn, extra)

    def pe(self, fn, extra=()):
        return self._do(self.P.pe, fn, extra)


def _lay_wgu(wg, wu):
    a = wg.reshape(8, 128, NFC, 128).transpose(2, 1, 0, 3)
    b = wu.reshape(8, 128, NFC, 128).transpose(2, 1, 0, 3)
    return np.ascontiguousarray(np.stack([a, b], axis=2)).reshape(NFC, 128, 2048)


def _lay_wd(wd):
    return np.ascontiguousarray(wd.reshape(NFC, 128, 8, 128).transpose(2, 1, 0, 3)).reshape(8, 128, DFF)


def _lay_gain(g):
    return np.ascontiguousarray(g.reshape(8, 128).T)


class Builder:
    def __init__(self, cfg):
        self.cfg = cfg
        global S
        S = cfg.get("S", 4096)
        self.P = Prog()
        P = self.P
        nc = P.nc
        self.nc = nc
        dt = nc.dram_tensor
        self.x_in = dt("x_in", [D, S], F32, kind="ExternalInput").ap()
        self.xout = dt("xout", [D, S], F32, kind="ExternalOutput").ap()
        self.gains = dt("gains", [128, 6, 8], F32, kind="ExternalInput").ap()
        self.wgu_f = dt("wgu_f", [4, NFC, 128, 2048], F32, kind="ExternalInput").ap()
        self.wd_f = dt("wd_f", [4, 8, 128, DFF], F32, kind="ExternalInput").ap()
        self.wgu_b = dt("wgu_b", [4, NFC, 128, 2048], BF16, kind="Internal").ap()
        self.wd_b = dt("wd_b", [4, 8, 128, DFF], BF16, kind="Internal").ap()
        self.conv_ev = {}

    def setup_consts(self):
        P = self.P
        nc = self.nc
        self.ones_bf = P.sbuf([128, 128], BF16, "ones")
        self.gain_sb = P.sbuf([128, 6, 8], F32, "gain")
        self.sem_misc = P.newsem("misc")
        ev1 = P.op(P.pool, lambda e: e.memset(self.ones_bf[:], 1.0))
        ev2 = P.dma(P.sp, self.sem_misc, self.gain_sb[:], self.gains)
        self.const_ev = [ev1, ev2]

    def convert_ffn(self, i):
        P = self.P
        sem = P.newsem("cv%d" % i)
        ev = None
        for fc in range(NFC):
            ev = P.dma(P.pool, sem, self.wgu_b[i, fc], self.wgu_f[i, fc])
        for dc in range(8):
            for h in range(2):
                ev = P.dma(P.pool, sem, self.wd_b[i, dc, :, h * 1408:(h + 1) * 1408],
                           self.wd_f[i, dc, :, h * 1408:(h + 1) * 1408])
        self.conv_ev[i] = ev

    def alloc_ffn(self, st):
        P = self.P
        b = {}
        b["xa"] = Ring(P, 2, [128, 8, 512], F32, "xa", st)
        b["sq"] = P.sbuf([128, 8, 512], BF16, "sq", st)
        b["rt"] = P.sbuf([128, 512], F32, "rt", st)
        b["rstd"] = [P.sbuf([128, 512], F32, "rstd", st) for _ in range(2)]
        b["hT"] = P.sbuf([128, 8, 1024], BF16, "hT", st)
        b["actT"] = P.sbuf([128, NFC, 1024], BF16, "actT", st)
        b["wgu"] = Ring(P, 4, [128, 2048], BF16, "wgu", st)
        b["wd"] = Ring(P, 3, [128, DFF], BF16, "wd", st)
        b["xc"] = Ring(P, 4, [128, 512], F32, "xc", st)
        b["sg"] = [P.sbuf([128, 512], F32, "sg", st) for _ in range(2)]
        b["st_sems"] = [P.newsem("ffn_st%d" % j) for j in range(4)]
        return b

    def ffn(self, b, i, gidx, x_src, x_dst, src_ready, ps):
        P = self.P
        nc = self.nc
        pe, act, dve, pool, sp = P.pe, P.act, P.dve, P.pool, P.sp
        NT = 1024
        ntile = S // NT
        xs = x_src.rearrange("(dc p) t -> p dc t", p=128)
        xd = x_dst.rearrange("(dc p) t -> p dc t", p=128)
        units = ps["units"]
        misc = ps["misc"]
        st = b.setdefault("state", {"unit_i": 0, "misc_i": 0, "unit_free": [[] for _ in units],
                                    "misc_free": [[] for _ in misc], "hT_free": [], "act_free": [],
                                    "sg_free": [[], []], "sg_i": 0, "rstd_free": [[], []], "sq_free": [], "rt_free": []})
        store_evs = []
        A_state = {}

        def stage_A1(k):
            t0 = k * NT
            res = []
            for ns in range(2):
                s, xa, sem, fr = b["xa"].next()
                ev_ld = P.dma(sp, sem, xa[:], xs[:, :, t0 + ns * 512:t0 + (ns + 1) * 512], deps=list(src_ready) + fr)
                ev_sq = P.op(act, lambda e: e.activation(out=b["sq"][:], in_=xa[:], func=AF.Square),
                             deps=[ev_ld] + st["sq_free"])
                mi = st["misc_i"] % 2
                st["misc_i"] += 1
                bank = misc[mi]
                deps = [ev_sq] + st["misc_free"][mi] + self.const_ev
                for dc in range(8):
                    f = lambda e, dc=dc: e.matmul(bank[:], lhsT=self.ones_bf[:], rhs=b["sq"][:, dc, :],
                                                  start=(dc == 0), stop=(dc == 7))
                    if dc < 7:
                        P.op_nosig(pe, f, deps=deps if dc == 0 else ())
                    else:
                        ev_mm = P.op(pe, f)
                st["sq_free"] = [ev_mm]
                ev_rt = P.op(act, lambda e: e.activation(out=b["rt"][:], in_=bank[:], func=AF.Sqrt,
                                                         scale=1.0 / D, bias=self.eps_sb[:]),
                             deps=[ev_mm] + st["rt_free"])
                st["misc_free"][mi] = [ev_rt]
                rstd = b["rstd"][ns]
                ev_rs = P.op(dve, lambda e: e.reciprocal(out=rstd[:], in_=b["rt"][:]),
                             deps=[ev_rt] + st["rstd_free"][ns])
                st["rt_free"] = [ev_rs]
                res.append((s, xa, ev_ld, ev_rs))
            A_state[k] = res

        def stage_A2(k):
            evs = []
            for ns in range(2):
                s, xa, ev_ld, ev_rs = A_state[k][ns]
                rstd = b["rstd"][ns]
                for dc in range(8):
                    ev = P.op(dve, lambda e, dc=dc: e.scalar_tensor_tensor(
                        out=b["hT"][:, dc, ns * 512:(ns + 1) * 512], in0=xa[:, dc, :],
                        scalar=self.gain_sb[:, gidx, dc:dc + 1], in1=rstd[:],
                        op0=ALU.mult, op1=ALU.mult),
                        deps=[ev_ld, ev_rs] + st["hT_free"] + self.const_ev)
                evs.append(ev)
                b["xa"].free[s] = [ev]
                st["rstd_free"][ns] = [ev]
            A_state[k] = evs

        def stage_B(k, hook=None):
            hT_ready = A_state[k]
            last_mm = None
            act_evs = []
            for fc in range(NFC):
                if hook is not None and fc == 14:
                    hook()
                s, w, sem, fr = b["wgu"].next()
                ev_w = P.dma(pool, sem, w[:], self.wgu_b[i, fc], deps=fr + [self.conv_ev[i]])
                for ns in range(2):
                    ui = st["unit_i"] % len(units)
                    st["unit_i"] += 1
                    pg, pu = units[ui]
                    deps = [ev_w] + hT_ready + st["unit_free"][ui]
                    first = True
                    for half, bank in ((0, pg), (1, pu)):
                        for dc in range(8):
                            f = lambda e, half=half, bank=bank, dc=dc: e.matmul(
                                bank[:], lhsT=w[:, (half * 8 + dc) * 128:(half * 8 + dc + 1) * 128],
                                rhs=b["hT"][:, dc, ns * 512:(ns + 1) * 512], start=(dc == 0), stop=(dc == 7))
                            if dc < 7:
                                P.op_nosig(pe, f, deps=deps if first else ())
                            else:
                                ev = P.op(pe, f)
                            first = False
                        if half == 0:
                            ev_g = ev
                        else:
                            ev_u = ev
                    last_mm = ev_u
                    si = st["sg_i"] % 2
                    st["sg_i"] += 1
                    sg = b["sg"][si]
                    ev_s = P.op(act, lambda e: e.activation(out=sg[:], in_=pg[:], func=AF.Silu),
                                deps=[ev_g] + st["sg_free"][si])
                    ev_a = P.op(dve, lambda e: e.tensor_tensor(out=b["actT"][:, fc, ns * 512:(ns + 1) * 512],
                                                               in0=sg[:], in1=pu[:], op=ALU.mult),
                                deps=[ev_s, ev_u] + st["act_free"])
                    st["sg_free"][si] = [ev_a]
                    st["unit_free"][ui] = [ev_a]
                    act_evs.append(ev_a)
                b["wgu"].free[s] = [last_mm]
            st["hT_free"] = [last_mm]
            return act_evs

        def stage_C(k, act_evs):
            t0 = k * NT
            last_mm = None
            for dc in range(8):
                s, w, sem, fr = b["wd"].next()
                ev_w = None
                for h in range(2):
                    ev_w = P.dma(pool, sem, w[:, h * 1408:(h + 1) * 1408], self.wd_b[i, dc, :, h * 1408:(h + 1) * 1408],
                                 deps=fr + [self.conv_ev[i]])
                for ns in range(2):
                    mi = st["misc_i"] % 2
                    st["misc_i"] += 1
                    bank = misc[mi]
                    deps = [ev_w] + act_evs[-2:] + st["misc_free"][mi]
                    for fc in range(NFC):
                        f = lambda e, fc=fc: e.matmul(bank[:], lhsT=w[:, fc * 128:(fc + 1) * 128],
                                                      rhs=b["actT"][:, fc, ns * 512:(ns + 1) * 512],
                                                      start=(fc == 0), stop=(fc == NFC - 1))
                        if fc < NFC - 1:
                            P.op_nosig(pe, f, deps=deps if fc == 0 else ())
                        else:
                            ev_mm = P.op(pe, f)
                    last_mm = ev_mm
                    sx, xc, semx, frx = b["xc"].next()
                    ev_ld = P.dma(sp, semx, xc[:], xs[:, dc, t0 + ns * 512:t0 + (ns + 1) * 512],
                                  deps=list(src_ready) + frx)
                    ev_r = P.op(dve, lambda e: e.scalar_tensor_tensor(out=xc[:], in0=bank[:], scalar=0.5, in1=xc[:],
                                                                      op0=ALU.mult, op1=ALU.add),
                                deps=[ev_mm, ev_ld])
                    st["misc_free"][mi] = [ev_r]
                    ev_st = P.dma(sp, b["st_sems"][sx], xd[:, dc, t0 + ns * 512:t0 + (ns + 1) * 512], xc[:], deps=[ev_r])
                    b["xc"].free[sx] = [ev_st]
                    store_evs.append(ev_st)
                b["wd"].free[s] = [last_mm]
            st["act_free"] = [last_mm]

        stage_A1(0)
        stage_A2(0)
        for k in range(ntile):
            nxt = (lambda k=k: stage_A1(k + 1)) if k + 1 < ntile else None
            act_evs = stage_B(k, nxt)
            if k + 1 < ntile:
                stage_A2(k + 1)
            stage_C(k, act_evs)
        return store_evs[-4:]

    def decl_even(self):
        dt = self.nc.dram_tensor
        self.pos = dt("pos", [128, S // 128], I32, kind="ExternalInput").ap()
        self.win_f = dt("win_f", [1024, 2048], F32, kind="ExternalInput").ap()
        self.wout_f = dt("wout_f", [512, 2048], F32, kind="ExternalInput").ap()
        self.poolw_f = dt("poolw_f", [128, 512], F32, kind="ExternalInput").ap()
        self.pscale = dt("pscale", [128, 4], F32, kind="ExternalInput").ap()
        self.qkn = dt("qkn", [2, 64], F32, kind="ExternalInput").ap()
        self.cst = dt("cst", [128, 8 + 64], F32, kind="ExternalInput").ap()
        self.win_b = dt("win_b", [1024, 2048], BF16, kind="Internal").ap()
        self.wout_b = dt("wout_b", [512, 2048], BF16, kind="Internal").ap()
        self.poolw_b = dt("poolw_b", [128, 512], BF16, kind="Internal").ap()
        self.catT = dt("catT", [1024, S], BF16, kind="Internal").ap()
        self.vd = dt("vd", [S, 512], BF16, kind="Internal").ap()

    def convert_even(self):
        P = self.P
        sem = P.newsem("cv_even")
        ev = None
        for j in range(8):
            ev = P.dma(P.pool, sem, self.win_b[j * 128:(j + 1) * 128], self.win_f[j * 128:(j + 1) * 128])
        for j in range(4):
            ev = P.dma(P.pool, sem, self.wout_b[j * 128:(j + 1) * 128], self.wout_f[j * 128:(j + 1) * 128])
        ev = P.dma(P.pool, sem, self.poolw_b, self.poolw_f)
        self.conv_ev["even"] = ev

    def norm_tile(self, nb, xs, t0, gidx, hT, src_ready, ps, hT_free):
        P = self.P
        pe, act, dve, sp = P.pe, P.act, P.dve, P.sp
        s, xa, sem, fr = nb["xa"].next()
        ev_ld = P.dma(sp, sem, xa[:], xs[:, :, t0:t0 + 512], deps=list(src_ready) + fr)
        ev_sq = P.op(act, lambda e: e.activation(out=nb["sq"][:], in_=xa[:], func=AF.Square),
                     deps=[ev_ld] + nb["sq_free"])
        bank, bfree, bi = self.bank_next(ps)
        deps = [ev_sq] + bfree + self.const_ev
        for dc in range(8):
            f = lambda e, dc=dc: e.matmul(bank[:], lhsT=self.ones_bf[:], rhs=nb["sq"][:, dc, :],
                                          start=(dc == 0), stop=(dc == 7))
            if dc < 7:
                P.op_nosig(pe, f, deps=deps if dc == 0 else ())
            else:
                ev_mm = P.op(pe, f)
        nb["sq_free"] = [ev_mm]
        ev_rt = P.op(act, lambda e: e.activation(out=nb["rt"][:], in_=bank[:], func=AF.Sqrt,
                                                 scale=1.0 / D, bias=self.eps_sb[:]),
                     deps=[ev_mm] + nb["rt_free"])
        ps["free"][bi] = [ev_rt]
        ev_rs = P.op(dve, lambda e: e.reciprocal(out=nb["rstd"][:], in_=nb["rt"][:]),
                     deps=[ev_rt] + nb["rstd_free"])
        nb["rt_free"] = [ev_rs]
        for dc in range(8):
            ev = P.op(dve, lambda e, dc=dc: e.scalar_tensor_tensor(
                out=hT[:, dc, :], in0=xa[:, dc, :], scalar=self.gain_sb[:, gidx, dc:dc + 1], in1=nb["rstd"][:],
                op0=ALU.mult, op1=ALU.mult), deps=[ev_ld, ev_rs] + hT_free + self.const_ev)
        nb["xa"].free[s] = [ev]
        nb["rstd_free"] = [ev]
        return ev

    def alloc_norm(self, st):
        P = self.P
        return {"xa": Ring(P, 2, [128, 8, 512], F32, "nxa", st), "sq": P.sbuf([128, 8, 512], BF16, "nsq", st),
                "rt": P.sbuf([128, 512], F32, "nrt", st), "rstd": P.sbuf([128, 512], F32, "nrstd", st),
                "sq_free": [], "rt_free": [], "rstd_free": []}

    def bank_next(self, ps):
        bi = ps["i"] % len(ps["banks"])
        ps["i"] += 1
        fr = ps["free"][bi]
        ps["free"][bi] = []
        return ps["banks"][bi], fr, bi

    def even_consts(self, st):
        P = self.P
        pool, dve, act, sp = P.pool, P.dve, P.act, P.sp
        c = {}
        nblk = S // 128
        c["ident"] = P.sbuf([128, 128], BF16, "ident", st)
        c["mask"] = P.sbuf([128, 256], BF16, "mask", st)
        c["ones64"] = P.sbuf([128, 64], BF16, "ones64", st)
        c["gqk"] = P.sbuf([128, 2, 64], F32, "gqk", st)
        c["pscale"] = P.sbuf([128, 4], F32, "pscale", st)
        c["cst"] = P.sbuf([128, 72], F32, "cst", st)
        c["cos"] = P.sbuf([128, nblk, 8], F32, "cos", st)
        c["sin"] = P.sbuf([128, nblk, 8], F32, "sin", st)
        posi = P.sbuf([128, nblk], I32, "posi", st)
        posf = P.sbuf([128, nblk], F32, "posf", st)
        ang = P.sbuf([128, nblk, 8], F32, "ang", st)
        kf = P.sbuf([128, nblk, 8], F32, "kf", st)
        ki = P.sbuf([128, nblk, 8], I32, "ki", st)
        red = P.sbuf([128, nblk, 8], F32, "red", st)
        sem = P.newsem("ec")
        evs = []
        e1 = P.op(pool, lambda e: e.memset(c["ident"][:], 1.0))
        e1 = P.op(pool, lambda e: e.affine_select(out=c["ident"][:], in_=c["ident"][:], pattern=[[-1, 128]],
                                                  compare_op=ALU.is_equal, fill=0.0, base=0, channel_multiplier=1), deps=[e1])
        evs.append(e1)
        e2 = P.op(pool, lambda e: e.memset(c["mask"][:], 1.0))
        e3 = P.op(pool, lambda e: e.affine_select(out=c["mask"][:, 0:128], in_=c["mask"][:, 0:128], pattern=[[-1, 128]],
                                                  compare_op=ALU.is_ge, fill=0.0, base=0, channel_multiplier=1), deps=[e2])
        e4 = P.op(pool, lambda e: e.affine_select(out=c["mask"][:, 128:256], in_=c["mask"][:, 128:256], pattern=[[1, 128]],
                                                  compare_op=ALU.is_ge, fill=0.0, base=0, channel_multiplier=-1), deps=[e2])
        evs += [e3, e4]
        evs.append(P.op(pool, lambda e: e.memset(c["ones64"][:], 1.0)))
        d1 = P.dma(sp, sem, c["gqk"][:, 0, :], self.qkn[0].partition_broadcast(128))
        d1 = P.dma(sp, sem, c["gqk"][:, 1, :], self.qkn[1].partition_broadcast(128))
        d1 = P.dma(sp, sem, c["pscale"][:], self.pscale)
        d1 = P.dma(sp, sem, c["cst"][:], self.cst)
        d1 = P.dma(sp, sem, posi[:], self.pos)
        evs.append(d1)
        ev = P.op(dve, lambda e: e.tensor_copy(out=posf[:], in_=posi[:]), deps=[d1])
        ev = P.op(dve, lambda e: e.tensor_tensor(out=ang[:], in0=posf[:].unsqueeze(2).to_broadcast([128, nblk, 8]),
                                                 in1=c["cst"][:, 0:8].unsqueeze(1).to_broadcast([128, nblk, 8]), op=ALU.mult), deps=[ev])
        C1 = 6.28125
        C2 = 2.0 * math.pi - C1
        for which, shift in (("sin", 0.0), ("cos", math.pi / 2)):
            ev = P.op(dve, lambda e: e.tensor_scalar(out=kf[:], in0=ang[:], scalar1=shift, scalar2=1.0 / (2 * math.pi),
                                                     op0=ALU.add, op1=ALU.mult), deps=[ev])
            ev = P.op(dve, lambda e: e.tensor_copy(out=ki[:], in_=kf[:]), deps=[ev])
            ev = P.op(dve, lambda e: e.tensor_copy(out=kf[:], in_=ki[:]), deps=[ev])
            ev = P.op(dve, lambda e: e.scalar_tensor_tensor(out=red[:], in0=kf[:], scalar=-C1, in1=ang[:],
                                                            op0=ALU.mult, op1=ALU.add), deps=[ev])
            ev = P.op(dve, lambda e: e.scalar_tensor_tensor(out=red[:], in0=kf[:], scalar=-C2, in1=red[:],
                                                            op0=ALU.mult, op1=ALU.add), deps=[ev])
            ev = P.op(dve, lambda e: e.tensor_scalar(out=red[:], in0=red[:], scalar1=shift, scalar2=None, op0=ALU.add), deps=[ev])
            ev = P.op(dve, lambda e: e.tensor_scalar(out=kf[:], in0=red[:], scalar1=math.pi, scalar2=-2 * math.pi, op0=ALU.is_gt, op1=ALU.mult), deps=[ev])
            ev = P.op(dve, lambda e: e.tensor_tensor(out=red[:], in0=red[:], in1=kf[:], op=ALU.add), deps=[ev])
            ev = P.op(dve, lambda e: e.tensor_scalar(out=kf[:], in0=red[:], scalar1=-math.pi, scalar2=2 * math.pi, op0=ALU.is_lt, op1=ALU.mult), deps=[ev])
            ev = P.op(dve, lambda e: e.tensor_tensor(out=red[:], in0=red[:], in1=kf[:], op=ALU.add), deps=[ev])
            ev = P.op(act, lambda e, which=which: e.activation(out=c[which][:], in_=red[:], func=AF.Sin), deps=[ev])
        evs.append(ev)
        c["evs"] = evs
        return c

    def even_E1(self, st, c, qT, kT, ps, src_ready):
        P = self.P
        pe, act, dve, pool, sp = P.pe, P.act, P.dve, P.pool, P.sp
        xs = self.xout.rearrange("(dc p) t -> p dc t", p=128)
        nb = self.alloc_norm(st)
        Win = P.sbuf([128, 8, 2048], BF16, "Win", st)
        poolw = P.sbuf([128, 4, 128], BF16, "poolw", st)
        hT = P.sbuf([128, 8, 512], BF16, "ehT", st)
        sqt = P.sbuf([128, 512], F32, "sqt", st)
        ss = P.sbuf([128, 8], F32, "ss", st)
        rt8 = P.sbuf([128, 8], F32, "rt8", st)
        rs8 = P.sbuf([128, 8], F32, "rs8", st)
        qn = P.sbuf([128, 8, 64], F32, "qn", st)
        tA = P.sbuf([128, 8, 8], F32, "tA", st)
        tB = P.sbuf([128, 8, 8], F32, "tB", st)
        qb = [P.sbuf([128, 512], BF16, "qb", st) for _ in range(2)]
        vb = Ring(P, 2, [128, 512], BF16, "vb", st)
        pbuf = P.sbuf([128, 4, 528], F32, "pbuf", st)
        sA = P.sbuf([128, 528], F32, "sA", st)
        sB = P.sbuf([128, 528], F32, "sB", st)
        pooled = P.sbuf([128, 512], BF16, "pooled", st)
        tmp16 = P.sbuf([128, 16], F32, "tmp16", st)
        pooled_free = []
        catp = Ring(P, 2, [128, 512], BF16, "catp", st)
        sem = P.newsem("e1w")
        ev_w = P.dma(pool, sem, Win[:], self.win_b.rearrange("(p dc) f -> p dc f", dc=8), deps=[self.conv_ev["even"]])
        ev_w = P.dma(pool, sem, poolw[:], self.poolw_b.rearrange("c (g e) -> c g e", g=4), deps=[self.conv_ev["even"]])
        ev0 = P.op(dve, lambda e: e.memset(sA[:], 0.0))
        ev0 = P.op(dve, lambda e: e.memset(sB[:], 0.0))
        ev0 = P.op(dve, lambda e: e.memset(pbuf[:, :, 0:16], 0.0))
        cev = c["evs"]
        hT_free = []
        qb_free = [[], []]
        qbi = 0
        halo_ev = [ev0] * 4
        small_free = []
        stores = []
        ntile = S // 512
        windows = (2, 4, 8, 16)
        for n in range(ntile):
            t0 = n * 512
            ev_h = self.norm_tile(nb, xs, t0, 4, hT, src_ready, ps, hT_free)
            last_pe = None
            for tb in range(4):
                blk = n * 4 + tb
                for sec in range(3):
                    bank, bfree, bi = self.bank_next(ps)
                    deps = [ev_h, ev_w] + bfree
                    for dc in range(8):
                        f = lambda e, dc=dc: e.matmul(bank[:], lhsT=hT[:, dc, tb * 128:(tb + 1) * 128],
                                                      rhs=Win[:, dc, sec * 512:(sec + 1) * 512], start=(dc == 0), stop=(dc == 7))
                        if dc < 7:
                            P.op_nosig(pe, f, deps=deps if dc == 0 else ())
                        else:
                            ev_mm = P.op(pe, f)
                    last_pe = ev_mm
                    if sec == 2:
                        s, vbt, vsem, vfr = vb.next()
                        ev_v = P.op(act, lambda e: e.activation(out=vbt[:], in_=bank[:], func=AF.Copy), deps=[ev_mm] + vfr)
                        ps["free"][bi] = [ev_v]
                        ev_st = P.dma(sp, vsem, self.vd[blk * 128:(blk + 1) * 128, :], vbt[:], deps=[ev_v])
                        vb.free[s] = [ev_st]
                        stores.append(ev_st)
                        continue
                    bank3 = bank[:].rearrange("p (h e) -> p h e", h=8)
                    ev = P.op(act, lambda e: e.activation(out=sqt[:], in_=bank[:], func=AF.Square), deps=[ev_mm] + small_free)
                    ev = P.op(dve, lambda e: e.tensor_reduce(out=ss[:], in_=sqt[:].rearrange("p (h e) -> p h e", h=8),
                                                             axis=mybir.AxisListType.X, op=ALU.add), deps=[ev] + small_free)
                    ev_sqdone = ev
                    ev = P.op(act, lambda e: e.activation(out=rt8[:], in_=ss[:], func=AF.Sqrt, scale=1.0 / 64, bias=self.eps_sb[:]),
                              deps=[ev])
                    ev = P.op(dve, lambda e: e.reciprocal(out=rs8[:], in_=rt8[:]), deps=[ev])
                    ev = P.op(dve, lambda e: e.tensor_tensor(out=qn[:], in0=bank3, in1=rs8[:].unsqueeze(2).to_broadcast([128, 8, 64]),
                                                             op=ALU.mult), deps=[ev])
                    ps["free"][bi] = [ev]
                    ev = P.op(dve, lambda e: e.tensor_tensor(out=qn[:], in0=qn[:],
                                                             in1=c["gqk"][:, sec, :].unsqueeze(1).to_broadcast([128, 8, 64]),
                                                             op=ALU.mult), deps=[ev] + cev)
                    q_b = qb[qbi % 2]
                    qfr = qb_free[qbi % 2]
                    ev_c = P.op(act, lambda e: e.activation(out=q_b[:], in_=qn[:].rearrange("p h e -> p (h e)"), func=AF.Copy),
                                deps=[ev] + qfr)
                    q3 = q_b[:].rearrange("p (h e) -> p h e", h=8)
                    cosb = c["cos"][:, blk, :].unsqueeze(1).to_broadcast([128, 8, 8])
                    sinb = c["sin"][:, blk, :].unsqueeze(1).to_broadcast([128, 8, 8])
                    t1 = qn[:, :, 0:8]
                    t2 = qn[:, :, 8:16]
                    e_a = P.op(dve, lambda e: e.tensor_tensor(out=tA[:], in0=t1, in1=cosb, op=ALU.mult), deps=[ev])
                    e_b = P.op(dve, lambda e: e.tensor_tensor(out=tB[:], in0=t2, in1=sinb, op=ALU.mult), deps=[ev])
                    e_r1 = P.op(dve, lambda e: e.tensor_tensor(out=q3[:, :, 0:8], in0=tA[:], in1=tB[:], op=ALU.subtract),
                                deps=[e_a, e_b, ev_c])
                    e_a = P.op(dve, lambda e: e.tensor_tensor(out=tA[:], in0=t2, in1=cosb, op=ALU.mult), deps=[e_r1])
                    e_b = P.op(dve, lambda e: e.tensor_tensor(out=tB[:], in0=t1, in1=sinb, op=ALU.mult), deps=[e_r1])
                    e_r2 = P.op(dve, lambda e: e.tensor_tensor(out=q3[:, :, 8:16], in0=tA[:], in1=tB[:], op=ALU.add),
                                deps=[e_a, e_b])
                    small_free = [e_r2]
                    tbank, tfree, ti = self.bank_next(ps)
                    tb16 = tbank[:].bitcast(BF16)
                    for cc in range(4):
                        f = lambda e, cc=cc: e.transpose(tb16[:, cc * 128:(cc + 1) * 128], q_b[:, cc * 128:(cc + 1) * 128], c["ident"][:])
                        if cc < 3:
                            P.op_nosig(pe, f, deps=([e_r2] + tfree + cev) if cc == 0 else ())
                        else:
                            ev_t = P.op(pe, f)
                    dst = (qT if sec == 0 else kT)[:, :, blk * 128:(blk + 1) * 128]
                    ev_e = P.op(act, lambda e: e.activation(out=dst, in_=tb16[:, 0:512].rearrange("p (c t) -> p c t", c=4), func=AF.Copy),
                                deps=[ev_t])
                    ps["free"][ti] = [ev_e]
                    qb_free[qbi % 2] = [ev_t]
                    qbi += 1
            for pc in range(4):
                w = windows[pc]
                bank, bfree, bi = self.bank_next(ps)
                deps = [ev_h, ev_w] + bfree
                for dc in range(8):
                    f = lambda e, dc=dc: e.matmul(bank[:], lhsT=Win[:, dc, 1536 + pc * 128:1536 + (pc + 1) * 128],
                                                  rhs=hT[:, dc, :], start=(dc == 0), stop=(dc == 7))
                    if dc < 7:
                        P.op_nosig(pe, f, deps=deps if dc == 0 else ())
                    else:
                        ev_mm = P.op(pe, f)
                last_pe = ev_mm
                ev_p = P.op(act, lambda e: e.activation(out=pbuf[:, pc, 16:528], in_=bank[:], func=AF.Copy),
                            deps=[ev_mm, halo_ev[pc]])
                ps["free"][bi] = [ev_p]
                cur = pbuf[:, pc, :]
                ev = ev_p
                lvl = 1
                bufs2 = [sA, sB]
                k = 0
                while lvl < w:
                    dstb = bufs2[k % 2]
                    src = cur
                    ev = P.op(dve, lambda e, src=src, dstb=dstb, lvl=lvl: e.tensor_tensor(
                        out=dstb[:, lvl:528], in0=src[:, lvl:528], in1=src[:, 0:528 - lvl], op=ALU.add), deps=[ev])
                    cur = dstb
                    lvl *= 2
                    k += 1
                ev_pl = P.op(dve, lambda e, cur=cur: e.scalar_tensor_tensor(out=pooled[:], in0=cur[:, 16:528], scalar=1.0 / w,
                                                                           in1=pbuf[:, pc, 16:528], op0=ALU.mult, op1=ALU.subtract),
                             deps=[ev] + pooled_free)
                if n == 0:
                    ev = P.op(dve, lambda e, cur=cur: e.tensor_tensor(out=tmp16[:], in0=cur[:, 16:32],
                                                                     in1=c["cst"][:, 8 + pc * 16:8 + (pc + 1) * 16], op=ALU.mult), deps=[ev_pl] + cev)
                    ev_pl = P.op(dve, lambda e: e.tensor_tensor(out=pooled[:, 0:16], in0=tmp16[:], in1=pbuf[:, pc, 16:32], op=ALU.subtract), deps=[ev])
                halo_ev[pc] = P.op(dve, lambda e: e.tensor_copy(out=pbuf[:, pc, 0:16], in_=pbuf[:, pc, 512:528]), deps=[ev_pl])
                bank2, bfree2, bi2 = self.bank_next(ps)
                ev_m2 = P.op(pe, lambda e: e.matmul(bank2[:], lhsT=poolw[:, pc, :], rhs=pooled[:], start=True, stop=True),
                             deps=[ev_pl, ev_w] + bfree2)
                last_pe = ev_m2
                s, cp, csem, cfr = catp.next()
                ev_cp = P.op(act, lambda e: e.activation(out=cp[:], in_=bank2[:], func=AF.Identity, scale=c["pscale"][:, pc:pc + 1]),
                             deps=[ev_m2] + cfr + cev)
                ps["free"][bi2] = [ev_cp]
                ev_st = P.dma(sp, csem, self.catT[512 + pc * 128:512 + (pc + 1) * 128, t0:t0 + 512], cp[:], deps=[ev_cp])
                catp.free[s] = [ev_st]
                stores.append(ev_st)
                pooled_free = [ev_m2]
            hT_free = [last_pe]
        return stores

    def even_E2(self, st, c, qT, kT, ps, vd_ready):
        P = self.P
        pe, act, dve, pool, sp = P.pe, P.act, P.dve, P.pool, P.sp
        nblk = S // 128
        acc = P.sbuf([128, 2, S], F32, "acc", st)
        rcp = P.sbuf([128, S], F32, "rcp", st)
        att = Ring(P, 2, [128, S], BF16, "att", st)
        vring = Ring(P, 2, [128, nblk, 128], BF16, "vring", st)
        pt = [[P.sbuf([128, 256], BF16, "pt", st) for _ in range(2)] for _ in range(2)]
        pt_free = [[[], []], [[], []]]
        ui = 0
        cev = c["evs"]
        stores = []
        acc_ev = None
        att_last = []
        patterns = ((128, 1), (512, 4), (2048, 16))
        for hp in range(4):
            first_pat = True
            for (_, d) in patterns:
                L = S // d
                nb = L // 128
                s, vt, vsem, vfr = vring.next()
                src = self.vd[:, hp * 128:(hp + 1) * 128].rearrange("(b i r) f -> i r b f", i=128, r=d)
                ev_v = None
                for r in range(d):
                    ev_v = P.dma(pool, vsem, vt[:, r * nb:(r + 1) * nb, :], src[:, r], deps=list(vd_ready) + vfr)
                last_uz = None
                for r in range(d):
                    for b in range(nb):
                        par = ui % 2
                        ui += 1
                        q0 = r + d * 128 * b
                        qsl = slice(q0, q0 + d * 127 + 1, d)
                        ksl_prev = slice(q0 - d * 128, q0 - d * 128 + d * 127 + 1, d)
                        bankA, frA, iA = self.bank_next(ps)
                        bankB, frB, iB = self.bank_next(ps)
                        evS = []
                        for h, bank, fr in ((0, bankA, frA), (1, bankB, frB)):
                            rows = slice(h * 64, (h + 1) * 64)
                            if b > 0:
                                P.op_nosig(pe, lambda e, bank=bank, rows=rows: e.matmul(
                                    bank[:, 0:128], lhsT=kT[rows, hp, ksl_prev], rhs=qT[rows, hp, qsl], start=True, stop=True), deps=fr)
                            evS.append(P.op(pe, lambda e, bank=bank, rows=rows: e.matmul(
                                bank[:, 128:256], lhsT=kT[rows, hp, qsl], rhs=qT[rows, hp, qsl], start=True, stop=True), deps=fr))
                        lo = 0 if b > 0 else 128
                        evP = []
                        for h, bank, bi in ((0, bankA, iA), (1, bankB, iB)):
                            p_t = pt[par][h]
                            ev = P.op(act, lambda e, bank=bank, p_t=p_t: e.activation(out=p_t[:, lo:256], in_=bank[:, lo:256], func=AF.Exp, scale=0.125),
                                      deps=[evS[h]] + pt_free[par][h])
                            ps["free"][bi] = [ev]
                            eng = dve if h == 0 else pool
                            ev = P.op(eng, lambda e, p_t=p_t: e.tensor_tensor(out=p_t[:, lo:256], in0=p_t[:, lo:256], in1=c["mask"][:, lo:256], op=ALU.mult),
                                      deps=[ev] + cev)
                            evP.append(ev)
                        bankU, frU, iU = self.bank_next(ps)
                        blk = r * nb + b
                        first = True
                        for h in range(2):
                            p_t = pt[par][h]
                            osl = slice(h * 64, (h + 1) * 64)
                            vcols = slice(h * 64, (h + 1) * 64)
                            for kind in range(2):
                                col = slice(kind * 128, (kind + 1) * 128)
                                if b > 0:
                                    lt = vt[:, blk - 1, vcols] if kind == 0 else c["ones64"][:]
                                    P.op_nosig(pe, lambda e, lt=lt, p_t=p_t, osl=osl, col=col: e.matmul(
                                        bankU[osl, col], lhsT=lt, rhs=p_t[:, 0:128], start=True, stop=False),
                                        deps=(frU + [ev_v] + evP + cev) if first else evP)
                                    first = False
                                lt = vt[:, blk, vcols] if kind == 0 else c["ones64"][:]
                                ev_uz = P.op(pe, lambda e, lt=lt, p_t=p_t, osl=osl, col=col: e.matmul(
                                    bankU[osl, col], lhsT=lt, rhs=p_t[:, 128:256], start=(b == 0), stop=True),
                                    deps=(frU + [ev_v] + evP + cev) if first else evP)
                                first = False
                            pt_free[par][h] = [ev_uz]
                        last_uz = ev_uz
                        a_view = acc[:, :, qsl]
                        u_view = bankU[:, 0:256].rearrange("p (k q) -> p k q", k=2)
                        if first_pat:
                            acc_ev = P.op(dve, lambda e: e.tensor_copy(out=a_view, in_=u_view), deps=[ev_uz] + att_last)
                        else:
                            acc_ev = P.op(dve, lambda e: e.tensor_tensor(out=a_view, in0=a_view, in1=u_view, op=ALU.add), deps=[ev_uz])
                        ps["free"][iU] = [acc_ev]
                vring.free[s] = [last_uz]
                first_pat = False
            ev = P.op(dve, lambda e: e.reciprocal(out=rcp[:], in_=acc[:, 1, :]), deps=[acc_ev])
            s, at, asem, afr = att.next()
            ev = P.op(dve, lambda e: e.tensor_tensor(out=at[:], in0=acc[:, 0, :], in1=rcp[:], op=ALU.mult), deps=[ev] + afr)
            att_last = [ev]
            ev_st = P.dma(sp, asem, self.catT[hp * 128:(hp + 1) * 128, :], at[:], deps=[ev])
            att.free[s] = [ev_st]
            stores.append(ev_st)
        return stores

    def even_E3(self, st, ps, cat_ready, src_ready):
        P = self.P
        pe, act, dve, pool, sp = P.pe, P.act, P.dve, P.pool, P.sp
        xs = self.xout.rearrange("(dc p) t -> p dc t", p=128)
        Wout = P.sbuf([128, 8, 1024], BF16, "Wout", st)
        cat = Ring(P, 2, [128, 8, 512], BF16, "cat", st)
        xc = Ring(P, 4, [128, 512], F32, "e3xc", st)
        st_sems = [P.newsem("e3st%d" % j) for j in range(4)]
        sem = P.newsem("e3w")
        ev_w = P.dma(pool, sem, Wout[:], self.wout_b.rearrange("(p a) (b f) -> p (a b) f", p=128, b=2), deps=[self.conv_ev["even"]])
        catv = self.catT.rearrange("(c p) t -> p c t", p=128)
        stores = []
        for n in range(S // 512):
            t0 = n * 512
            s, ct, csem, cfr = cat.next()
            ev_c = P.dma(sp, csem, ct[:], catv[:, :, t0:t0 + 512], deps=list(cat_ready) + cfr)
            last = None
            for dc in range(8):
                bank, bfree, bi = self.bank_next(ps)
                deps = [ev_c, ev_w] + bfree
                for cc in range(8):
                    f = lambda e, cc=cc: e.matmul(bank[:], lhsT=Wout[:, cc, dc * 128:(dc + 1) * 128], rhs=ct[:, cc, :],
                                                  start=(cc == 0), stop=(cc == 7))
                    if cc < 7:
                        P.op_nosig(pe, f, deps=deps if cc == 0 else ())
                    else:
                        ev_mm = P.op(pe, f)
                last = ev_mm
                sx, xt, xsem, xfr = xc.next()
                ev_ld = P.dma(sp, xsem, xt[:], xs[:, dc, t0:t0 + 512], deps=list(src_ready) + xfr)
                ev_r = P.op(dve, lambda e: e.tensor_tensor(out=xt[:], in0=bank[:], in1=xt[:], op=ALU.add), deps=[ev_mm, ev_ld])
                ps["free"][bi] = [ev_r]
                ev_st = P.dma(sp, st_sems[sx], xs[:, dc, t0:t0 + 512], xt[:], deps=[ev_r])
                xc.free[sx] = [ev_st]
                stores.append(ev_st)
            cat.free[s] = [last]
        return stores[-4:]

    def even_mixer(self, ps, src_ready):
        P = self.P
        with contextlib.ExitStack() as st0:
            c = self.even_consts(st0)
            qT = P.sbuf([128, 4, S], BF16, "qT", st0)
            kT = P.sbuf([128, 4, S], BF16, "kT", st0)
            with contextlib.ExitStack() as st1:
                ev1 = self.even_E1(st1, c, qT, kT, ps, src_ready)
            P.barrier(ev1)
            ps["free"] = [[] for _ in ps["banks"]]
            with contextlib.ExitStack() as st2:
                ev2 = self.even_E2(st2, c, qT, kT, ps, ev1)
            P.barrier(ev2)
            ps["free"] = [[] for _ in ps["banks"]]
        with contextlib.ExitStack() as st3:
            ev3 = self.even_E3(st3, ps, ev1 + ev2, src_ready)
        P.barrier(ev3)
        ps["free"] = [[] for _ in ps["banks"]]
        return ev3

    def decl_odd(self):
        dt = self.nc.dram_tensor
        NQ = S // 8
        self.s5_a1 = dt("s5_a1", [2, 4096], F32, kind="ExternalInput").ap()
        self.s5_a2 = dt("s5_a2", [128, 2, 64], F32, kind="ExternalInput").ap()
        self.s5_ldt = dt("s5_ldt", [64], F32, kind="ExternalInput").ap()
        self.s5_b1 = dt("s5_b1", [128, 2, 4096], F32, kind="ExternalInput").ap()
        self.s5_b2 = dt("s5_b2", [128, 2, 1024], F32, kind="ExternalInput").ap()
        self.s5_c2 = dt("s5_c2", [128, 2, 1024], F32, kind="ExternalInput").ap()
        self.s5_dv = dt("s5_dv", [1024], F32, kind="ExternalInput").ap()
        self.s5_cst = dt("s5_cst", [128, 12], F32, kind="ExternalInput").ap()
        self.s5_iota = dt("s5_iota", [128, 512], F32, kind="ExternalInput").ap()
        self.wglu_f = dt("wglu_f", [1024, 2048], F32, kind="ExternalInput").ap()
        self.wglu_b = dt("wglu_b", [1024, 2048], BF16, kind="Internal").ap()
        self.Ud = dt("Ud", [64, 128, NQ], BF16, kind="Internal").ap()
        self.Tabd = dt("Tabd", [32, 128, 5, 256], BF16, kind="Internal").ap()

    def convert_odd(self):
        P = self.P
        sem = P.newsem("cv_odd")
        ev = None
        for j in range(8):
            ev = P.dma(P.pool, sem, self.wglu_b[j * 128:(j + 1) * 128], self.wglu_f[j * 128:(j + 1) * 128])
        self.conv_ev["odd"] = ev

    def trig(self, ch, ang, kf, ki, red, out_sin, out_cos):
        C1 = 6.28125
        C2 = 2.0 * math.pi - C1
        for out, shift in ((out_sin, 0.0), (out_cos, math.pi / 2)):
            if out is None:
                continue
            ch.dve(lambda e: e.tensor_scalar(out=kf, in0=ang, scalar1=shift, scalar2=1.0 / (2 * math.pi), op0=ALU.add, op1=ALU.mult))
            ch.dve(lambda e: e.tensor_copy(out=ki, in_=kf))
            ch.dve(lambda e: e.tensor_copy(out=kf, in_=ki))
            ch.dve(lambda e: e.scalar_tensor_tensor(out=red, in0=kf, scalar=-C1, in1=ang, op0=ALU.mult, op1=ALU.add))
            ch.dve(lambda e: e.scalar_tensor_tensor(out=red, in0=kf, scalar=-C2, in1=red, op0=ALU.mult, op1=ALU.add))
            ch.dve(lambda e: e.tensor_scalar(out=red, in0=red, scalar1=shift, scalar2=-math.pi, op0=ALU.add, op1=ALU.max))
            ch.dve(lambda e: e.tensor_scalar(out=red, in0=red, scalar1=math.pi, scalar2=None, op0=ALU.min))
            ch.act(lambda e, out=out: e.activation(out=out, in_=red, func=AF.Sin))

    def odd_tables(self, st0, ps, small):
        P = self.P
        stA = contextlib.ExitStack()
        st = stA
        pe, act, dve, pool, sp = P.pe, P.act, P.dve, P.pool, P.sp
        ch = Chain(P)
        sem = P.newsem("o2ld")
        f32 = lambda shape, nm: P.sbuf(shape, F32, nm, st)
        cst = P.sbuf([128, 12], F32, "ocst", st0)
        dt2 = P.sbuf([128, 64], F32, "dt2", st0)
        a2 = f32([128, 2, 64], "a2")
        ldt = f32([128, 64], "ldt")
        b2 = f32([128, 2, 1024], "b2")
        c2 = f32([128, 2, 1024], "c2")
        dv = f32([128, 1024], "dv")
        evs = [P.dma(sp, sem, cst[:], self.s5_cst), P.dma(sp, sem, a2[:], self.s5_a2),
               P.dma(sp, sem, ldt[:], self.s5_ldt.partition_broadcast(128)), P.dma(sp, sem, b2[:], self.s5_b2),
               P.dma(sp, sem, c2[:], self.s5_c2), P.dma(sp, sem, dv[:], self.s5_dv.partition_broadcast(128))]
        ch.last = evs[-1]
        ncol, n1col, mlo, mhi = cst[:, 0:1], cst[:, 1:2], cst[:, 2:3], cst[:, 3:4]
        T = [None, None] + [P.sbuf([128, 64, 128], BF16, "tab%d" % t, st) for t in range(2, 5)]
        lr2 = f32([128, 64], "lr2"); li2 = a2[:, 1, :]
        lrdt2 = f32([128, 64], "lrdt2"); th2 = f32([128, 64], "th2")
        Lre = f32([128, 64, 9], "Lre"); Lim = f32([128, 64, 9], "Lim")
        mag = f32([128, 64], "mag"); ang = f32([128, 64], "ang2"); kf = f32([128, 64], "kf2")
        ki = P.sbuf([128, 64], I32, "ki2", st); red = f32([128, 64], "red2")
        cs = f32([128, 64], "cs2"); sn = f32([128, 64], "sn2")
        ch.act(lambda e: e.activation(out=dt2[:], in_=ldt[:], func=AF.Exp))
        ch.dve(lambda e: e.tensor_scalar(out=lr2[:], in0=a2[:, 0, :], scalar1=-1e-4, scalar2=None, op0=ALU.min))
        ch.dve(lambda e: e.tensor_tensor(out=lrdt2[:], in0=lr2[:], in1=dt2[:], op=ALU.mult))
        ch.dve(lambda e: e.tensor_tensor(out=th2[:], in0=li2, in1=dt2[:], op=ALU.mult))
        for ex in range(9):
            ch.act(lambda e, ex=ex: e.activation(out=mag[:], in_=lrdt2[:], func=AF.Exp, scale=float(ex)))
            ch.dve(lambda e, ex=ex: e.tensor_scalar(out=ang[:], in0=th2[:], scalar1=float(ex), scalar2=None, op0=ALU.mult))
            self.trig(ch, ang[:], kf[:], ki[:], red[:], sn[:], cs[:])
            ch.dve(lambda e, ex=ex: e.tensor_tensor(out=Lre[:, :, ex], in0=mag[:], in1=cs[:], op=ALU.mult))
            ch.dve(lambda e, ex=ex: e.tensor_tensor(out=Lim[:, :, ex], in0=mag[:], in1=sn[:], op=ALU.mult))
            if ex == 8:
                ch.dve(lambda e: e.tensor_scalar(out=kf[:, 0:32], in0=mag[:, 0::2], scalar1=mlo, scalar2=None, op0=ALU.mult))
                ch.dve(lambda e: e.scalar_tensor_tensor(out=small["r8p"][:], in0=mag[:, 1::2], scalar=mhi, in1=kf[:, 0:32], op0=ALU.mult, op1=ALU.add))
                ch.dve(lambda e: e.tensor_scalar(out=kf[:, 0:32], in0=ang[:, 0::2], scalar1=mlo, scalar2=None, op0=ALU.mult))
                ch.dve(lambda e: e.scalar_tensor_tensor(out=small["php"][:], in0=ang[:, 1::2], scalar=mhi, in1=kf[:, 0:32], op0=ALU.mult, op1=ALU.add))
        den = f32([128, 64], "den2"); cre = f32([128, 64], "cre2"); cim = f32([128, 64], "cim2"); nre = f32([128, 64], "nre2")
        t1 = f32([128, 64], "t12")
        ch.dve(lambda e: e.tensor_tensor(out=den[:], in0=lr2[:], in1=lr2[:], op=ALU.mult))
        ch.dve(lambda e: e.tensor_tensor(out=t1[:], in0=li2, in1=li2, op=ALU.mult))
        ch.dve(lambda e: e.tensor_tensor(out=den[:], in0=den[:], in1=t1[:], op=ALU.add))
        ch.dve(lambda e: e.reciprocal(out=den[:], in_=den[:]))
        ch.dve(lambda e: e.tensor_scalar(out=nre[:], in0=Lre[:, :, 1], scalar1=-1.0, scalar2=None, op0=ALU.add))
        ch.dve(lambda e: e.tensor_tensor(out=cre[:], in0=nre[:], in1=lr2[:], op=ALU.mult))
        ch.dve(lambda e: e.tensor_tensor(out=t1[:], in0=Lim[:, :, 1], in1=li2, op=ALU.mult))
        ch.dve(lambda e: e.tensor_tensor(out=cre[:], in0=cre[:], in1=t1[:], op=ALU.add))
        ch.dve(lambda e: e.tensor_tensor(out=cre[:], in0=cre[:], in1=den[:], op=ALU.mult))
        ch.dve(lambda e: e.tensor_tensor(out=cim[:], in0=Lim[:, :, 1], in1=lr2[:], op=ALU.mult))
        ch.dve(lambda e: e.tensor_tensor(out=t1[:], in0=nre[:], in1=li2, op=ALU.mult))
        ch.dve(lambda e: e.tensor_tensor(out=cim[:], in0=cim[:], in1=t1[:], op=ALU.subtract))
        ch.dve(lambda e: e.tensor_tensor(out=cim[:], in0=cim[:], in1=den[:], op=ALU.mult))
        w1 = f32([128, 64, 16], "w1"); w2 = f32([128, 64, 16], "w2"); w3 = f32([128, 64, 16], "w3")
        Brep = P.sbuf([128, 64, 8, 16], BF16, "Brep", st)
        bre = b2[:, 0, :].rearrange("r (g c) -> r g c", g=64); bim = b2[:, 1, :].rearrange("r (g c) -> r g c", g=64)
        creb = cre[:].unsqueeze(2).to_broadcast([128, 64, 16]); cimb = cim[:].unsqueeze(2).to_broadcast([128, 64, 16])
        ch.dve(lambda e: e.tensor_tensor(out=w1[:], in0=bre, in1=creb, op=ALU.mult))
        ch.dve(lambda e: e.tensor_tensor(out=w2[:], in0=bim, in1=cimb, op=ALU.mult))
        ch.dve(lambda e: e.tensor_tensor(out=w1[:], in0=w1[:], in1=w2[:], op=ALU.subtract))
        ch.dve(lambda e: e.tensor_tensor(out=w2[:], in0=bim, in1=creb, op=ALU.mult))
        ch.dve(lambda e: e.tensor_tensor(out=w3[:], in0=bre, in1=cimb, op=ALU.mult))
        ch.dve(lambda e: e.tensor_tensor(out=w2[:], in0=w2[:], in1=w3[:], op=ALU.add))
        ch.dve(lambda e: e.tensor_scalar(out=w1[:], in0=w1[:], scalar1=mlo, scalar2=None, op0=ALU.mult))
        ch.dve(lambda e: e.tensor_scalar(out=w2[:], in0=w2[:], scalar1=mhi, scalar2=None, op0=ALU.mult))
        ch.dve(lambda e: e.tensor_tensor(out=w1[:], in0=w1[:], in1=w2[:], op=ALU.subtract))
        ch.dve(lambda e: e.tensor_copy(out=Brep[:], in_=w1[:].unsqueeze(2).to_broadcast([128, 64, 8, 16])))
        pmask = f32([128, 64], "pmask")
        ch.dve(lambda e: e.tensor_copy(out=pmask[:, 0::2], in_=mlo.to_broadcast([128, 32])))
        ch.dve(lambda e: e.tensor_copy(out=pmask[:, 1::2], in_=mhi.to_broadcast([128, 32])))
        pmb = pmask[:].unsqueeze(2).to_broadcast([128, 64, 16])
        X = P.sbuf([128, 64, 8, 16], BF16, "Xcl", st)
        cre3 = c2[:, 0, :].rearrange("r (g c) -> r g c", g=64); cim3 = c2[:, 1, :].rearrange("r (g c) -> r g c", g=64)
        MI_re = T[3][:].rearrange("r g (j c) -> r g j c", j=8); MI_im = T[4][:].rearrange("r g (j c) -> r g j c", j=8)
        for ex in range(9):
            lre = Lre[:, :, ex].unsqueeze(2).to_broadcast([128, 64, 16]); lim = Lim[:, :, ex].unsqueeze(2).to_broadcast([128, 64, 16])
            ch.dve(lambda e, lre=lre: e.tensor_tensor(out=w1[:], in0=cre3, in1=lre, op=ALU.mult))
            ch.dve(lambda e, lim=lim: e.tensor_tensor(out=w2[:], in0=cim3, in1=lim, op=ALU.mult))
            ch.dve(lambda e: e.tensor_tensor(out=w1[:], in0=w1[:], in1=w2[:], op=ALU.subtract))
            ch.dve(lambda e, lim=lim: e.tensor_tensor(out=w2[:], in0=cre3, in1=lim, op=ALU.mult))
            ch.dve(lambda e, lre=lre: e.tensor_tensor(out=w3[:], in0=cim3, in1=lre, op=ALU.mult))
            ch.dve(lambda e: e.tensor_tensor(out=w2[:], in0=w2[:], in1=w3[:], op=ALU.add))
            if ex >= 1:
                ch.dve(lambda e, ex=ex: e.tensor_tensor(out=MI_re[:, :, ex - 1, :], in0=w1[:], in1=pmb, op=ALU.mult))
                ch.dve(lambda e, ex=ex: e.scalar_tensor_tensor(out=MI_im[:, :, ex - 1, :], in0=w2[:], scalar=-1.0, in1=pmb, op0=ALU.mult, op1=ALU.mult))
            if ex <= 7:
                ch.dve(lambda e: e.tensor_scalar(out=w3[:], in0=w1[:], scalar1=mlo, scalar2=None, op0=ALU.mult))
                ch.dve(lambda e, ex=ex: e.scalar_tensor_tensor(out=X[:, :, ex, :], in0=w2[:], scalar=mhi, in1=w3[:], op0=ALU.mult, op1=ALU.add))
        identf = f32([128, 128], "identf")
        ch.pool(lambda e: e.memset(identf[:], 1.0))
        ch.pool(lambda e: e.affine_select(out=identf[:], in_=identf[:], pattern=[[-1, 128]], compare_op=ALU.is_equal, fill=0.0, base=0, channel_multiplier=1))
        macc = f32([128, 4, 128], "macc"); dtmp = f32([128, 4, 8, 16], "dtmp")
        for gb in range(16):
            bank, bfree, bi = self.bank_next(ps)
            for gg in range(4):
                g = gb * 4 + gg
                ch.pe(lambda e, g=g, gg=gg: e.matmul(bank[:, gg * 128:(gg + 1) * 128], lhsT=Brep[:, g, :, :].rearrange("r i c -> r (i c)"),
                                                    rhs=X[:, g, :, :].rearrange("r j c -> r (j c)"), start=True, stop=True), extra=bfree)
            b3 = bank[:].rearrange("r (g x) -> r g x", g=4)
            ch.dve(lambda e: e.tensor_scalar(out=macc[:], in0=b3, scalar1=cst[:, 4:5], scalar2=None, op0=ALU.mult))
            for s in range(1, 8):
                ch.dve(lambda e, s=s: e.scalar_tensor_tensor(out=macc[:, :, 16 * s:128], in0=b3[:, :, 0:128 - 16 * s], scalar=cst[:, 4 + s:5 + s],
                                                             in1=macc[:, :, 16 * s:128], op0=ALU.mult, op1=ALU.add))
            ps["free"][bi] = [ch.last]
            dsl = dv[:, gb * 64:(gb + 1) * 64].rearrange("r (g c) -> r g c", g=4).unsqueeze(2).to_broadcast([128, 4, 8, 16])
            idb = identf[:].rearrange("r (j c) -> r j c", j=8).unsqueeze(1).to_broadcast([128, 4, 8, 16])
            ch.dve(lambda e: e.tensor_tensor(out=dtmp[:], in0=idb, in1=dsl, op=ALU.mult))
            ch.dve(lambda e, gb=gb: e.tensor_tensor(out=T[2][:, gb * 4:(gb + 1) * 4, :], in0=macc[:], in1=dtmp[:].rearrange("r g j c -> r g (j c)"), op=ALU.add))
        outs = []
        for t in range(2, 5):
            dst = self.Tabd[:, :, t, :].rearrange("k r (g x) -> r k g x", g=2)
            outs.append(P.dma(sp, sem, dst, T[t][:].rearrange("r (k g) x -> r k g x", g=2), deps=[ch.last]))
        P.barrier(outs[-1:])
        ps["free"] = [[] for _ in ps["banks"]]
        stA.close()
        stB = contextlib.ExitStack()
        st = stB
        T[0] = P.sbuf([128, 64, 128], BF16, "tab0", st)
        T[1] = P.sbuf([128, 64, 128], BF16, "tab1", st)
        ch.last = outs[-1]
        ch.pool(lambda e: e.memset(T[0][:], 0.0))
        ch.pool(lambda e: e.memset(T[1][:], 0.0))
        QG = 16
        NE = QG * 64
        a1 = f32([128, 2, NE], "a1"); b1 = f32([128, 2, NE], "b1")
        lr1 = f32([128, NE], "lr1"); lrdt1 = f32([128, NE], "lrdt1"); th1 = f32([128, NE], "th1")
        mg1 = f32([128, NE], "mg1"); an1 = f32([128, NE], "an1"); kf1 = f32([128, NE], "kf1"); ki1 = P.sbuf([128, NE], I32, "ki1", st)
        rd1 = f32([128, NE], "rd1"); cs1 = f32([128, NE], "cs1"); sn1 = f32([128, NE], "sn1")
        pa_re = f32([128, NE], "pa_re"); pa_im = f32([128, NE], "pa_im"); dn1 = f32([128, NE], "dn1")
        for qc in range(64 // QG):
            g0 = qc * QG
            sl = slice(g0 * 64, (g0 + QG) * 64)
            d1 = P.dma(sp, sem, a1[:, 0, :], self.s5_a1[0, sl].partition_broadcast(128), deps=[ch.last])
            d1 = P.dma(sp, sem, a1[:, 1, :], self.s5_a1[1, sl].partition_broadcast(128), deps=[ch.last])
            d1 = P.dma(sp, sem, b1[:], self.s5_b1[:, :, sl], deps=[ch.last])
            ch.last = d1
            dtb = dt2[:, g0:g0 + QG].unsqueeze(2).to_broadcast([128, QG, 64])
            v3 = lambda t: t[:].rearrange("r (g p) -> r g p", g=QG)
            ch.dve(lambda e: e.tensor_scalar(out=lr1[:], in0=a1[:, 0, :], scalar1=-1e-4, scalar2=None, op0=ALU.min))
            ch.dve(lambda e: e.tensor_tensor(out=v3(lrdt1), in0=v3(lr1), in1=dtb, op=ALU.mult))
            ch.dve(lambda e: e.tensor_tensor(out=v3(th1), in0=a1[:, 1, :].rearrange("r (g p) -> r g p", g=QG), in1=dtb, op=ALU.mult))
            for k_, col in enumerate((ncol, n1col)):
                ch.act(lambda e, col=col: e.activation(out=mg1[:], in_=lrdt1[:], func=AF.Exp, scale=col))
                ch.dve(lambda e, col=col: e.tensor_scalar(out=an1[:], in0=th1[:], scalar1=col, scalar2=None, op0=ALU.mult))
                self.trig(ch, an1[:], kf1[:], ki1[:], rd1[:], sn1[:], cs1[:])
                if k_ == 0:
                    ch.dve(lambda e: e.tensor_tensor(out=pa_re[:], in0=mg1[:], in1=cs1[:], op=ALU.mult))
                    ch.dve(lambda e: e.tensor_tensor(out=pa_im[:], in0=mg1[:], in1=sn1[:], op=ALU.mult))
                else:
                    ch.dve(lambda e: e.tensor_tensor(out=cs1[:], in0=mg1[:], in1=cs1[:], op=ALU.mult))
                    ch.dve(lambda e: e.tensor_tensor(out=sn1[:], in0=mg1[:], in1=sn1[:], op=ALU.mult))
                    ch.dve(lambda e: e.tensor_tensor(out=pa_re[:], in0=cs1[:], in1=pa_re[:], op=ALU.subtract))
                    ch.dve(lambda e: e.tensor_tensor(out=pa_im[:], in0=sn1[:], in1=pa_im[:], op=ALU.subtract))
            li1 = a1[:, 1, :]
            ch.dve(lambda e: e.tensor_tensor(out=dn1[:], in0=lr1[:], in1=lr1[:], op=ALU.mult))
            ch.dve(lambda e: e.tensor_tensor(out=kf1[:], in0=li1, in1=li1, op=ALU.mult))
            ch.dve(lambda e: e.tensor_tensor(out=dn1[:], in0=dn1[:], in1=kf1[:], op=ALU.add))
            ch.dve(lambda e: e.reciprocal(out=dn1[:], in_=dn1[:]))
            ch.dve(lambda e: e.tensor_tensor(out=cs1[:], in0=pa_re[:], in1=lr1[:], op=ALU.mult))
            ch.dve(lambda e: e.tensor_tensor(out=kf1[:], in0=pa_im[:], in1=li1, op=ALU.mult))
            ch.dve(lambda e: e.tensor_tensor(out=cs1[:], in0=cs1[:], in1=kf1[:], op=ALU.add))
            ch.dve(lambda e: e.tensor_tensor(out=cs1[:], in0=cs1[:], in1=dn1[:], op=ALU.mult))
            ch.dve(lambda e: e.tensor_tensor(out=sn1[:], in0=pa_im[:], in1=lr1[:], op=ALU.mult))
            ch.dve(lambda e: e.tensor_tensor(out=kf1[:], in0=pa_re[:], in1=li1, op=ALU.mult))
            ch.dve(lambda e: e.tensor_tensor(out=sn1[:], in0=sn1[:], in1=kf1[:], op=ALU.subtract))
            ch.dve(lambda e: e.tensor_tensor(out=sn1[:], in0=sn1[:], in1=dn1[:], op=ALU.mult))
            ch.dve(lambda e: e.tensor_tensor(out=mg1[:], in0=cs1[:], in1=b1[:, 0, :], op=ALU.mult))
            ch.dve(lambda e: e.tensor_tensor(out=kf1[:], in0=sn1[:], in1=b1[:, 1, :], op=ALU.mult))
            ch.dve(lambda e: e.tensor_tensor(out=mg1[:], in0=mg1[:], in1=kf1[:], op=ALU.subtract))
            ch.dve(lambda e: e.tensor_tensor(out=an1[:], in0=cs1[:], in1=b1[:, 1, :], op=ALU.mult))
            ch.dve(lambda e: e.tensor_tensor(out=kf1[:], in0=sn1[:], in1=b1[:, 0, :], op=ALU.mult))
            ch.dve(lambda e: e.tensor_tensor(out=an1[:], in0=an1[:], in1=kf1[:], op=ALU.add))
            for t, srcb in ((0, mg1), (1, an1)):
                s3 = srcb[:].rearrange("r (g p) -> r g p", g=QG)
                ch.dve(lambda e, t=t, s3=s3: e.tensor_copy(out=T[t][:, g0:g0 + QG:2, 0:64], in_=s3[:, 0::2, :]))
                ch.dve(lambda e, t=t, s3=s3: e.tensor_copy(out=T[t][:, g0 + 1:g0 + QG:2, 64:128], in_=s3[:, 1::2, :]))
        outs = []
        for t in range(2):
            dst = self.Tabd[:, :, t, :].rearrange("k r (g x) -> r k g x", g=2)
            outs.append(P.dma(sp, sem, dst, T[t][:].rearrange("r (k g) x -> r k g x", g=2), deps=[ch.last]))
        P.barrier(outs[-1:])
        stB.close()
        return outs[-1:], small

    def odd_O1(self, st, ps, src_ready):
        P = self.P
        pe, act, dve, pool, sp = P.pe, P.act, P.dve, P.pool, P.sp
        NQ = S // 8
        xs = self.xout.rearrange("(dc p) t -> p dc t", p=128)
        nb = self.alloc_norm(st)
        hT = P.sbuf([128, 8, 512], BF16, "ohT", st)
        Sel = P.sbuf([128, 8, 8, 128], BF16, "Sel", st)
        U = P.sbuf([128, 64, NQ], BF16, "Usb", st)
        ev = P.op(pool, lambda e: e.memset(Sel[:], 0.0))
        evs = []
        for m in range(8):
            for i in range(8):
                evs.append(P.op(pool, lambda e, m=m, i=i: e.affine_select(
                    out=Sel[:, m, i, 16 * i:16 * i + 16], in_=self.ones_bf[:, 0:16], pattern=[[-1, 16]],
                    compare_op=ALU.is_equal, fill=0.0, base=-16 * m, channel_multiplier=1), deps=[ev] + self.const_ev))
        sel_ev = [evs[-1]]
        hT_free = []
        last_ev = None
        for n in range(S // 512):
            ev_h = self.norm_tile(nb, xs, n * 512, 5, hT, src_ready, ps, hT_free)
            last_pe = None
            for gc in range(8):
                bank, bfree, bi = self.bank_next(ps)
                first = True
                for m in range(8):
                    for i in range(8):
                        f = lambda e, m=m, i=i: e.matmul(bank[:, m * 64:(m + 1) * 64], lhsT=Sel[:, m, i, :], rhs=hT[:, gc, i::8],
                                                         start=(i == 0), stop=(i == 7))
                        if i < 7:
                            P.op_nosig(pe, f, deps=([ev_h] + bfree + sel_ev) if first else ())
                        else:
                            last_pe = P.op(pe, f)
                        first = False
                eng = act if gc % 2 == 0 else dve
                if eng is act:
                    last_ev = P.op(act, lambda e: e.activation(out=U[:, gc * 8:(gc + 1) * 8, n * 64:(n + 1) * 64],
                                                               in_=bank[:].rearrange("r (m q) -> r m q", m=8), func=AF.Copy), deps=[last_pe])
                else:
                    last_ev = P.op(dve, lambda e: e.tensor_copy(out=U[:, gc * 8:(gc + 1) * 8, n * 64:(n + 1) * 64],
                                                                in_=bank[:].rearrange("r (m q) -> r m q", m=8)), deps=[last_pe])
                ps["free"][bi] = [last_ev]
            hT_free = [last_pe]
        sem = P.newsem("o1st")
        ev_a = (P.act.sem, P.act.sem.v)
        ev_d = (P.dve.sem, P.dve.sem.v)
        ev_st = P.dma(sp, sem, self.Ud.rearrange("g r q -> r g q"), U[:], deps=[ev_a, ev_d])
        return [ev_st]

    def odd_O3(self, st, ps, small, ready, Ztm):
        P = self.P
        pe, act, dve, pool, sp = P.pe, P.act, P.dve, P.pool, P.sp
        NQ = S // 8
        NQT = NQ // 128
        f32 = lambda shape, nm: P.sbuf(shape, F32, nm, st)
        up = Ring(P, 2, [128, 2, NQ], BF16, "up", st)
        tabr = Ring(P, 2, [128, 5, 256], BF16, "tabr", st)
        iota = f32([128, 512], "iota")
        sem = P.newsem("o3c")
        ev_io = P.dma(sp, sem, iota[:], self.s5_iota)
        ang = f32([128, NQ], "o3ang"); kf = f32([128, NQ], "o3kf"); ki = P.sbuf([128, NQ], I32, "o3ki", st); red = f32([128, NQ], "o3red")
        cs = f32([128, NQ], "o3cs"); sn = f32([128, NQ], "o3sn")
        ta = f32([128, NQ], "o3ta"); tb = f32([128, NQ], "o3tb")
        wre = f32([128, NQ], "o3wre"); wim = f32([128, NQ], "o3wim")
        zre = f32([128, NQ], "o3zre"); zim = f32([128, NQ], "o3zim")
        Hs = [[P.sbuf([128, NQ + 1], BF16, "Hs", st) for _ in range(2)] for _ in range(2)]
        hs_free = [[], []]
        ch = Chain(P)
        for par in range(2):
            for c_ in range(2):
                ch.pool(lambda e, par=par, c_=c_: e.memset(Hs[par][c_][:, 0:1], 0.0))
        ch.last = ev_io if ch.last is None else ch.last
        init_ev = [ch.last, ev_io]
        last = None
        for k in range(32):
            par = k % 2
            s, u, usem, ufr = up.next()
            ev_u = P.dma(sp, usem, u[:], self.Ud[2 * k:2 * k + 2].rearrange("g r q -> r g q"), deps=list(ready) + ufr)
            s2, tab, tsem, tfr = tabr.next()
            ev_t = P.dma(sp, tsem, tab[:], self.Tabd[k], deps=list(ready) + tfr)
            bEr, frr, ir = self.bank_next(ps)
            bEi, fri, ii = self.bank_next(ps)
            evE = []
            for t, bank, fr in ((0, bEr, frr), (1, bEi, fri)):
                P.op_nosig(pe, lambda e, t=t, bank=bank: e.matmul(bank[:, 0:NQ], lhsT=tab[:, t, 0:128], rhs=u[:, 0, :], start=True, stop=False),
                           deps=[ev_u, ev_t] + fr)
                evE.append(P.op(pe, lambda e, t=t, bank=bank: e.matmul(bank[:, 0:NQ], lhsT=tab[:, t, 128:256], rhs=u[:, 1, :], start=False, stop=True)))
            ch.dve(lambda e: e.tensor_scalar(out=ang[:], in0=iota[:, 0:NQ], scalar1=small["php"][:, k:k + 1], scalar2=None, op0=ALU.mult), extra=init_ev)
            self.trig(ch, ang[:], kf[:], ki[:], red[:], sn[:], cs[:])
            ch.dve(lambda e: e.tensor_tensor(out=ta[:], in0=cs[:], in1=bEr[:, 0:NQ], op=ALU.mult), extra=evE)
            ch.dve(lambda e: e.tensor_tensor(out=tb[:], in0=sn[:], in1=bEi[:, 0:NQ], op=ALU.mult))
            ch.dve(lambda e: e.tensor_tensor(out=wre[:], in0=ta[:], in1=tb[:], op=ALU.add))
            ch.dve(lambda e: e.tensor_tensor(out=ta[:], in0=cs[:], in1=bEi[:, 0:NQ], op=ALU.mult))
            ch.dve(lambda e: e.tensor_tensor(out=tb[:], in0=sn[:], in1=bEr[:, 0:NQ], op=ALU.mult))
            ps["free"][ir] = [ch.last]
            ps["free"][ii] = [ch.last]
            ch.dve(lambda e: e.tensor_tensor(out=wim[:], in0=ta[:], in1=tb[:], op=ALU.subtract))
            r8 = small["r8p"][:, k:k + 1].to_broadcast([128, NQ])
            ch.dve(lambda e: e.tensor_tensor_scan(out=zre[:], data0=r8, data1=wre[:], initial=0.0, op0=ALU.mult, op1=ALU.add))
            ch.dve(lambda e: e.tensor_tensor_scan(out=zim[:], data0=r8, data1=wim[:], initial=0.0, op0=ALU.mult, op1=ALU.add))
            hre, him = Hs[par]
            ch.dve(lambda e: e.tensor_tensor(out=ta[:], in0=cs[:], in1=zre[:], op=ALU.mult))
            ch.dve(lambda e: e.tensor_tensor(out=tb[:], in0=sn[:], in1=zim[:], op=ALU.mult))
            ch.dve(lambda e: e.tensor_tensor(out=hre[:, 1:NQ + 1], in0=ta[:], in1=tb[:], op=ALU.subtract), extra=hs_free[par])
            ch.dve(lambda e: e.tensor_tensor(out=ta[:], in0=cs[:], in1=zim[:], op=ALU.mult))
            ch.dve(lambda e: e.tensor_tensor(out=tb[:], in0=sn[:], in1=zre[:], op=ALU.mult))
            ev_H = ch.dve(lambda e: e.tensor_tensor(out=him[:, 1:NQ + 1], in0=ta[:], in1=tb[:], op=ALU.add))
            for qt in range(NQT):
                by, fry, iy = self.bank_next(ps)
                qs = slice(qt * 128, (qt + 1) * 128)
                P.op_nosig(pe, lambda e: e.matmul(by[:, 0:256], lhsT=hre[:, qs], rhs=tab[:, 3, :], start=True, stop=False), deps=[ev_H] + fry)
                P.op_nosig(pe, lambda e: e.matmul(by[:, 0:256], lhsT=him[:, qs], rhs=tab[:, 4, :], start=False, stop=False))
                P.op_nosig(pe, lambda e: e.matmul(by[:, 0:128], lhsT=u[:, 0, qs], rhs=tab[:, 2, 0:128], start=False, stop=False))
                ev_y = P.op(pe, lambda e: e.matmul(by[:, 128:256], lhsT=u[:, 1, qs], rhs=tab[:, 2, 128:256], start=False, stop=True))
                dst = Ztm[:, qt, :, 32 * k:32 * k + 32].rearrange("q j (g c) -> q g j c", g=2)
                src = by[:, 0:256].rearrange("q (g j c) -> q g j c", g=2, j=8)
                last = P.op(act, lambda e: e.activation(out=dst, in_=src, func=AF.Gelu_apprx_tanh), deps=[ev_y])
                ps["free"][iy] = [last]
            hs_free[par] = [ev_y]
            up.free[s] = [ev_y]
            tabr.free[s2] = [ev_y]
        return [last]

    def odd_O4(self, st, ps, Ztm, z_ready, src_ready):
        P = self.P
        pe, act, dve, pool, sp = P.pe, P.act, P.dve, P.pool, P.sp
        NQT = S // 1024
        xs = self.xout.rearrange("(dc p) t -> p dc t", p=128)
        Wg = P.sbuf([128, 8, 2048], BF16, "Wglu", st)
        ident = P.sbuf([128, 128], BF16, "oident", st)
        zT = [P.sbuf([128, 8, 1024], BF16, "zT", st) for _ in range(2)]
        zT_free = [[], []]
        sg = [P.sbuf([128, 512], F32, "osg", st) for _ in range(2)]
        sg_free = [[], []]
        gt = [P.sbuf([128, 512], F32, "ogt", st) for _ in range(2)]
        xc = Ring(P, 4, [128, 512], F32, "o4xc", st)
        st_sems = [P.newsem("o4st%d" % j) for j in range(4)]
        sem = P.newsem("o4w")
        ev_w = P.dma(pool, sem, Wg[:], self.wglu_b.rearrange("(p dc) f -> p dc f", dc=8), deps=[self.conv_ev["odd"]])
        e1 = P.op(pool, lambda e: e.memset(ident[:], 1.0))
        e1 = P.op(pool, lambda e: e.affine_select(out=ident[:], in_=ident[:], pattern=[[-1, 128]], compare_op=ALU.is_equal, fill=0.0,
                                                  base=0, channel_multiplier=1), deps=[e1])
        stores = []
        ui = 0
        for qt in range(NQT):
            z = zT[qt % 2]
            evz = None
            for j in range(8):
                for half in range(2):
                    tbank, tfree, ti = self.bank_next(ps)
                    t16 = tbank[:].bitcast(BF16)
                    for cc in range(4):
                        chn = half * 4 + cc
                        f = lambda e, cc=cc, chn=chn: e.transpose(t16[:, cc * 128:(cc + 1) * 128], Ztm[:, qt, j, chn * 128:(chn + 1) * 128], ident[:])
                        if cc < 3:
                            P.op_nosig(pe, f, deps=(list(z_ready) + tfree + [e1]) if cc == 0 else ())
                        else:
                            ev_t = P.op(pe, f)
                    evz = P.op(act if half == 0 else dve,
                               (lambda e: e.activation(out=z[:, half * 4:(half + 1) * 4, j::8], in_=t16[:, 0:512].rearrange("p (c t) -> p c t", c=4), func=AF.Copy))
                               if half == 0 else
                               (lambda e: e.tensor_copy(out=z[:, half * 4:(half + 1) * 4, j::8], in_=t16[:, 0:512].rearrange("p (c t) -> p c t", c=4))),
                               deps=[ev_t] + zT_free[qt % 2])
                    ps["free"][ti] = [evz]
            z_evs = [(P.act.sem, P.act.sem.v), (P.dve.sem, P.dve.sem.v)]
            last_mm = None
            for ns in range(2):
                t0 = qt * 1024 + ns * 512
                for dc in range(8):
                    bv, frv, iv = self.bank_next(ps)
                    bg, frg, ig = self.bank_next(ps)
                    for bank, off, fr in ((bv, 0, frv), (bg, 1024, frg)):
                        for cc in range(8):
                            f = lambda e, bank=bank, off=off, cc=cc: e.matmul(bank[:], lhsT=Wg[:, cc, off + dc * 128:off + (dc + 1) * 128],
                                                                             rhs=z[:, cc, ns * 512:(ns + 1) * 512], start=(cc == 0), stop=(cc == 7))
                            if cc < 7:
                                P.op_nosig(pe, f, deps=(z_evs + [ev_w] + fr) if cc == 0 else ())
                            else:
                                ev_mm = P.op(pe, f)
                    last_mm = ev_mm
                    si = ui % 2
                    ui += 1
                    ev_s = P.op(act, lambda e: e.activation(out=sg[si][:], in_=bg[:], func=AF.Sigmoid), deps=[ev_mm] + sg_free[si])
                    ps["free"][ig] = [ev_s]
                    ev_m = P.op(dve, lambda e: e.tensor_tensor(out=gt[si][:], in0=bv[:], in1=sg[si][:], op=ALU.mult), deps=[ev_s, ev_mm])
                    ps["free"][iv] = [ev_m]
                    sx, xt, xsem, xfr = xc.next()
                    ev_ld = P.dma(sp, xsem, xt[:], xs[:, dc, t0:t0 + 512], deps=list(src_ready) + xfr)
                    ev_r = P.op(dve, lambda e: e.tensor_tensor(out=xt[:], in0=gt[si][:], in1=xt[:], op=ALU.add), deps=[ev_m, ev_ld])
                    sg_free[si] = [ev_r]
                    ev_st = P.dma(sp, st_sems[sx], xs[:, dc, t0:t0 + 512], xt[:], deps=[ev_r])
                    xc.free[sx] = [ev_st]
                    stores.append(ev_st)
            zT_free[qt % 2] = [last_mm]
        return stores[-4:]

    def odd_mixer(self, ps, src_ready):
        P = self.P
        NQ = S // 8
        reset = lambda: ps.__setitem__("free", [[] for _ in ps["banks"]])
        so = contextlib.ExitStack()
        small = {"r8p": P.sbuf([128, 32], F32, "r8p", so), "php": P.sbuf([128, 32], F32, "php", so)}
        with contextlib.ExitStack() as st:
            ev1 = self.odd_O1(st, ps, src_ready)
        P.barrier(ev1)
        reset()
        with contextlib.ExitStack() as st:
            ev2, small = self.odd_tables(st, ps, small)
        reset()
        with contextlib.ExitStack() as st0:
            Ztm = P.sbuf([128, S // 1024, 8, 1024], BF16, "Ztm", st0)
            with contextlib.ExitStack() as st:
                ev3 = self.odd_O3(st, ps, small, ev1 + ev2, Ztm)
            P.barrier(ev3)
            reset()
            with contextlib.ExitStack() as st:
                ev4 = self.odd_O4(st, ps, Ztm, ev3, src_ready)
            P.barrier(ev4)
            reset()
        so.close()
        return ev4

    def build(self):
        P = self.P
        nc = self.nc
        cfg = self.cfg
        phases = cfg["phases"]
        self.setup_consts()
        self.eps_sb = P.sbuf([128, 1], F32, "eps")
        ev = P.op(P.pool, lambda e: e.memset(self.eps_sb[:], EPS))
        self.const_ev.append(ev)
        if "even" in phases:
            self.decl_even()
        if "odd" in phases:
            self.decl_odd()
        for ph in phases:
            if ph.startswith("ffn"):
                self.convert_ffn(int(ph[3:]))
            elif ph == "even":
                self.convert_even()
            elif ph == "odd":
                self.convert_odd()
        banks = [P.psum([128, 512], F32, "bank") for _ in range(8)]
        psf = {"units": [(banks[0], banks[1]), (banks[2], banks[3]), (banks[4], banks[5])], "misc": [banks[6], banks[7]]}
        psm = {"banks": banks, "free": [[] for _ in banks], "i": 0}
        ready = []
        src = self.x_in
        for ph in phases:
            P.begin_phase()
            if ph.startswith("ffn"):
                i = int(ph[3:])
                with contextlib.ExitStack() as st:
                    b = self.alloc_ffn(st)
                    ready = self.ffn(b, i, i, src, self.xout, ready, psf)
                P.barrier(ready)
            elif ph == "copy":
                csem = P.newsem("copy")
                ready = [P.dma(P.sp, csem, self.xout, self.x_in)]
            elif ph == "even":
                ready = self.even_mixer(psm, ready)
            elif ph == "odd":
                ready = self.odd_mixer(psm, ready)
            P.end_phase()
            src = self.xout
        P.sp.wait(ready)
        P.es.close()
        return nc


POOL_WINDOWS = (2, 4, 8, 16)


def prep_shared(inp, phases):
    m = {}
    g = [inp["ffn_norm"][l, s] for l in range(2) for s in range(2)] + [inp["mix_norm"][0], inp["mix_norm"][1]]
    m["gains"] = np.ascontiguousarray(np.stack([_lay_gain(np.asarray(x, np.float32)) for x in g], axis=1))
    wg, wu, wd = inp["ffn_w_gate"], inp["ffn_w_up"], inp["ffn_w_down"]
    m["wgu_f"] = np.stack([_lay_wgu(wg[l, s], wu[l, s]) for l in range(2) for s in range(2)])
    m["wd_f"] = np.stack([_lay_wd(wd[l, s]) for l in range(2) for s in range(2)])
    if "even" in phases:
        m["win_f"] = np.ascontiguousarray(inp["ev_w_in"][0].reshape(8, 128, 2048).transpose(1, 0, 2)).reshape(1024, 2048)
        m["wout_f"] = np.ascontiguousarray(inp["ev_w_out"][0].reshape(8, 128, 1024).transpose(1, 0, 2)).reshape(512, 2048)
        m["poolw_f"] = np.ascontiguousarray(inp["ev_pool_w"][0].transpose(1, 0, 2)).reshape(128, 512)
        m["pscale"] = np.ascontiguousarray(inp["ev_pool_scale"][0].reshape(4, 128).T)
        m["qkn"] = np.ascontiguousarray(np.stack([inp["ev_q_norm"][0], inp["ev_k_norm"][0]]))
        cst = np.zeros((128, 72), np.float32)
        cst[:, 0:8] = (500000.0 ** (-np.arange(0, 16, 2, dtype=np.float32) / 16.0)).astype(np.float32)[None, :]
        for pc, w in enumerate(POOL_WINDOWS):
            cst[:, 8 + pc * 16:8 + (pc + 1) * 16] = (1.0 / np.minimum(np.arange(1, 17), w)).astype(np.float32)[None, :]
        m["cst"] = cst
    if "odd" in phases:
        f = lambda a: np.asarray(a, np.float32)
        a_re, a_im = f(inp["s5_a_re"][0]), f(inp["s5_a_im"][0])
        m["s5_a1"] = np.ascontiguousarray(np.stack([a_re.reshape(-1), a_im.reshape(-1)]))
        m["s5_a2"] = np.ascontiguousarray(np.stack([np.tile(a_re.T, (2, 1)), np.tile(a_im.T, (2, 1))], axis=1))
        m["s5_ldt"] = np.ascontiguousarray(f(inp["s5_log_dt"][0]))
        b_re, b_im = f(inp["s5_b_re"][0]), f(inp["s5_b_im"][0])
        lay1 = lambda b: np.tile(b.transpose(2, 0, 1).reshape(16, 4096), (8, 1))
        m["s5_b1"] = np.ascontiguousarray(np.stack([lay1(b_re), lay1(b_im)], axis=1))
        lay2 = lambda b: np.tile(b.transpose(1, 0, 2).reshape(64, 1024), (2, 1))
        m["s5_b2"] = np.ascontiguousarray(np.stack([lay2(b_re), lay2(b_im)], axis=1))
        c_re, c_im = f(inp["s5_c_re"][0]), f(inp["s5_c_im"][0])
        lay3 = lambda c: np.tile(c.transpose(2, 0, 1).reshape(64, 1024), (2, 1))
        m["s5_c2"] = np.ascontiguousarray(np.stack([lay3(c_re), lay3(c_im)], axis=1))
        m["s5_dv"] = np.ascontiguousarray(f(inp["s5_d"][0]))
        cst = np.zeros((128, 12), np.float32)
        ii = np.arange(128) // 16
        cst[:, 0] = 7 - ii
        cst[:, 1] = 8 - ii
        cst[:, 2] = (np.arange(128) < 64)
        cst[:, 3] = (np.arange(128) >= 64)
        for s_ in range(8):
            cst[:, 4 + s_] = (ii == s_)
        m["s5_cst"] = cst
        m["s5_iota"] = np.ascontiguousarray(np.tile(np.arange(512, dtype=np.float32)[None, :], (128, 1)))
        m["wglu_f"] = np.ascontiguousarray(f(inp["s5_w_glu"][0]).reshape(8, 128, 2048).transpose(1, 0, 2)).reshape(1024, 2048)
    return m


def prep_core(inp, b, phases):
    m = {"x_in": np.ascontiguousarray(np.asarray(inp["x"][b], np.float32).T)}
    if "even" in phases:
        m["pos"] = np.ascontiguousarray(np.asarray(inp["positions"][b], np.int32).reshape(-1, 128).T)
    return m


PHASES = ["ffn0", "even", "ffn1", "ffn2", "odd", "ffn3"]


_NC_CACHE = {}


def kernel(**inputs):
    inp = {k: np.asarray(v) for k, v in inputs.items()}
    phases = PHASES
    if "nc" not in _NC_CACHE:
        _NC_CACHE["nc"] = Builder({"phases": phases, "S": 4096}).build()
    nc = _NC_CACHE["nc"]
    shared = prep_shared(inp, phases)
    nb = inp["x"].shape[0]
    in_maps = []
    for b in range(nb):
        m = dict(shared)
        m.update(prep_core(inp, b, phases))
        in_maps.append(m)
    res = run_bass_kernel_spmd(nc, in_maps, core_ids=list(range(nb)))
    out = np.stack([np.ascontiguousarray(np.asarray(res.results[b]["xout"]).T) for b in range(nb)], axis=0)
    return out.astype(np.float32)
```

```python
import contextlib
import math
import numpy as np
import concourse.bass as bass
import concourse.mybir as mybir
from concourse.bass_utils import run_bass_kernel_spmd

F32 = mybir.dt.float32
BF16 = mybir.dt.bfloat16
I32 = mybir.dt.int32
AF = mybir.ActivationFunctionType
ALU = mybir.AluOpType

S = 4096
D = 1024
DFF = 2816
NFC = DFF // 128
EPS = 1e-6


class Sem:
    _n = 0

    def __init__(self, h):
        self.h = h
        Sem._n += 1
        self.k = Sem._n
        self.v = 0
        self.sw = False


class Eng:
    def __init__(self, prog, eng, name):
        self.e = eng
        self.sem = prog.newsem("e_" + name)
        self.waited = {}

    def wait(self, deps):
        for ev in deps:
            if ev is None:
                continue
            sem, val = ev
            if self.waited.get(sem.k, 0) >= val:
                continue
            self.e.wait_ge(sem.h, val)
            self.waited[sem.k] = val

    def done(self, ins):
        self.sem.v += 1
        ins.then_inc(self.sem.h, 1)
        return (self.sem, self.sem.v)


class Prog:
    def __init__(self):
        self.nc = bass.Bass("TRN2", target_bir_lowering=False)
        self.es = contextlib.ExitStack()
        self._nm = 0
        self.sem_pool = []
        self.sem_pool_sw = []
        self.phase_sems = None
        nc = self.nc
        self.pe = Eng(self, nc.tensor, "pe")
        self.act = Eng(self, nc.scalar, "act")
        self.dve = Eng(self, nc.vector, "dve")
        self.pool = Eng(self, nc.gpsimd, "pool")
        self.sp = Eng(self, nc.sync, "sp")

    def newsem(self, name, sw=False):
        pool_ = self.sem_pool_sw if sw else self.sem_pool
        if pool_:
            sm = pool_.pop()
        else:
            self._nm += 1
            sm = Sem(self.es.enter_context(self.nc.semaphore("%s_s%d" % (name, self._nm))))
            sm.sw = sw
        if self.phase_sems is not None:
            self.phase_sems.append(sm)
        return sm

    def begin_phase(self):
        self.phase_sems = []

    def end_phase(self):
        for sm in self.phase_sems:
            (self.sem_pool_sw if sm.sw else self.sem_pool).append(sm)
        self.phase_sems = None

    def name(self, p):
        self._nm += 1
        return "%s_%d" % (p, self._nm)

    def sbuf(self, shape, dt, name="t", stack=None):
        return (stack or self.es).enter_context(self.nc.sbuf_tensor(self.name(name), list(shape), dt))

    def psum(self, shape, dt, name="p", stack=None):
        return (stack or self.es).enter_context(self.nc.psum_tensor(self.name(name), list(shape), dt))

    def op(self, eng, fn, deps=()):
        eng.wait(deps)
        return eng.done(fn(eng.e))

    def op_nosig(self, eng, fn, deps=()):
        eng.wait(deps)
        fn(eng.e)

    def barrier(self, extra=()):
        evs = [(e.sem, e.sem.v) for e in (self.pe, self.act, self.dve, self.pool) if e.sem.v > 0] + list(extra)
        for e in (self.pe, self.act, self.dve, self.pool, self.sp):
            e.wait(evs)

    def dma(self, q, sem, out, in_, deps=(), **kw):
        q.wait(deps)
        sem.v += 16
        q.e.dma_start(out=out, in_=in_, **kw).then_inc(sem.h, 16)
        return (sem, sem.v)


class Ring:
    def __init__(self, prog, n, shape, dt, name, stack=None, sw=False):
        self.n = n
        self.bufs = [prog.sbuf(shape, dt, name, stack) for _ in range(n)]
        self.sems = [prog.newsem(name + "_ld%d" % i, sw) for i in range(n)]
        self.free = [[] for _ in range(n)]
        self.i = 0

    def next(self):
        s = self.i % self.n
        self.i += 1
        fr = self.free[s]
        self.free[s] = []
        return s, self.bufs[s], self.sems[s], fr


class Chain:
    def __init__(self, P):
        self.P = P
        self.last = None

    def _do(self, eng, fn, extra=()):
        self.last = self.P.op(eng, fn, deps=[self.last] + list(extra))
        return self.last

    def dve(self, fn, extra=()):
        return self._do(self.P.dve, fn, extra)

    def act(self, fn, extra=()):
        return self._do(self.P.act, fn, extra)

    def pool(self, fn, extra=()):
        return self._do(self.P.pool, fn, extra)

    def pe(self, fn, extra=()):
        return self._do(self.P.pe, fn, extra)


def _lay_wgu(wg, wu):
    a = wg.reshape(8, 128, NFC, 128).transpose(2, 1, 0, 3)
    b = wu.reshape(8, 128, NFC, 128).transpose(2, 1, 0, 3)
    return np.ascontiguousarray(np.stack([a, b], axis=2)).reshape(NFC, 128, 2048)


def _lay_wd(wd):
    return np.ascontiguousarray(wd.reshape(NFC, 128, 8, 128).transpose(2, 1, 0, 3)).reshape(8, 128, DFF)


def _lay_gain(g):
    return np.ascontiguousarray(g.reshape(8, 128).T)


class Builder:
    def __init__(self, cfg):
        self.cfg = cfg
        global S
        S = cfg.get("S", 4096)
        self.P = Prog()
        P = self.P
        nc = P.nc
        self.nc = nc
        dt = nc.dram_tensor
        self.x_in = dt("x_in", [D, S], F32, kind="ExternalInput").ap()
        self.xout = dt("xout", [D, S], F32, kind="ExternalOutput").ap()
        self.gains = dt("gains", [128, 6, 8], F32, kind="ExternalInput").ap()
        self.wgu_f = dt("wgu_f", [4, NFC, 128, 2048], F32, kind="ExternalInput").ap()
        self.wd_f = dt("wd_f", [4, 8, 128, DFF], F32, kind="ExternalInput").ap()
        self.wgu_b = dt("wgu_b", [4, NFC, 128, 2048], BF16, kind="Internal").ap()
        self.wd_b = dt("wd_b", [4, 8, 128, DFF], BF16, kind="Internal").ap()
        self.conv_ev = {}

    def setup_consts(self):
        P = self.P
        nc = self.nc
        self.ones_bf = P.sbuf([128, 128], BF16, "ones")
        self.gain_sb = P.sbuf([128, 6, 8], F32, "gain")
        self.sem_misc = P.newsem("misc")
        ev1 = P.op(P.pool, lambda e: e.memset(self.ones_bf[:], 1.0))
        ev2 = P.dma(P.sp, self.sem_misc, self.gain_sb[:], self.gains)
        self.const_ev = [ev1, ev2]

    def convert_ffn(self, i):
        P = self.P
        sem = P.newsem("cv%d" % i, sw=True)
        ev = None
        for fc in range(NFC):
            ev = P.dma(P.pool, sem, self.wgu_b[i, fc], self.wgu_f[i, fc])
        for dc in range(8):
            for h in range(2):
                ev = P.dma(P.pool, sem, self.wd_b[i, dc, :, h * 1408:(h + 1) * 1408],
                           self.wd_f[i, dc, :, h * 1408:(h + 1) * 1408])
        self.conv_ev[i] = ev

    def alloc_ffn(self, st):
        P = self.P
        b = {}
        b["xa"] = Ring(P, 2, [128, 8, 512], F32, "xa", st)
        b["sq"] = P.sbuf([128, 8, 512], BF16, "sq", st)
        b["rt"] = P.sbuf([128, 512], F32, "rt", st)
        b["rstd"] = [P.sbuf([128, 512], F32, "rstd", st) for _ in range(2)]
        b["hT"] = P.sbuf([128, 8, 1024], BF16, "hT", st)
        b["actT"] = P.sbuf([128, NFC, 1024], BF16, "actT", st)
        b["wgu"] = Ring(P, 4, [128, 2048], BF16, "wgu", st, sw=True)
        b["wd"] = Ring(P, 3, [128, DFF], BF16, "wd", st, sw=True)
        b["xc"] = Ring(P, 4, [128, 512], F32, "xc", st)
        b["sg"] = [P.sbuf([128, 512], F32, "sg", st) for _ in range(2)]
        b["st_sems"] = [P.newsem("ffn_st%d" % j) for j in range(4)]
        return b

    def ffn(self, b, i, gidx, x_src, x_dst, src_ready, ps):
        P = self.P
        nc = self.nc
        pe, act, dve, pool, sp = P.pe, P.act, P.dve, P.pool, P.sp
        NT = 1024
        ntile = S // NT
        xs = x_src.rearrange("(dc p) t -> p dc t", p=128)
        xd = x_dst.rearrange("(dc p) t -> p dc t", p=128)
        units = ps["units"]
        misc = ps["misc"]
        st = b.setdefault("state", {"unit_i": 0, "misc_i": 0, "unit_free": [[] for _ in units],
                                    "misc_free": [[] for _ in misc], "hT_free": [], "act_free": [],
                                    "sg_free": [[], []], "sg_i": 0, "rstd_free": [[], []], "sq_free": [], "rt_free": []})
        store_evs = []
        A_state = {}

        def stage_A1(k):
            t0 = k * NT
            res = []
            for ns in range(2):
                s, xa, sem, fr = b["xa"].next()
                ev_ld = P.dma(sp, sem, xa[:], xs[:, :, t0 + ns * 512:t0 + (ns + 1) * 512], deps=list(src_ready) + fr)
                ev_sq = P.op(act, lambda e: e.activation(out=b["sq"][:], in_=xa[:], func=AF.Square),
                             deps=[ev_ld] + st["sq_free"])
                mi = st["misc_i"] % 2
                st["misc_i"] += 1
                bank = misc[mi]
                deps = [ev_sq] + st["misc_free"][mi] + self.const_ev
                for dc in range(8):
                    f = lambda e, dc=dc: e.matmul(bank[:], lhsT=self.ones_bf[:], rhs=b["sq"][:, dc, :],
                                                  start=(dc == 0), stop=(dc == 7))
                    if dc < 7:
                        P.op_nosig(pe, f, deps=deps if dc == 0 else ())
                    else:
                        ev_mm = P.op(pe, f)
                st["sq_free"] = [ev_mm]
                ev_rt = P.op(act, lambda e: e.activation(out=b["rt"][:], in_=bank[:], func=AF.Sqrt,
                                                         scale=1.0 / D, bias=self.eps_sb[:]),
                             deps=[ev_mm] + st["rt_free"])
                st["misc_free"][mi] = [ev_rt]
                rstd = b["rstd"][ns]
                ev_rs = P.op(dve, lambda e: e.reciprocal(out=rstd[:], in_=b["rt"][:]),
                             deps=[ev_rt] + st["rstd_free"][ns])
                st["rt_free"] = [ev_rs]
                res.append((s, xa, ev_ld, ev_rs))
            A_state[k] = res

        def stage_A2(k):
            evs = []
            for ns in range(2):
                s, xa, ev_ld, ev_rs = A_state[k][ns]
                rstd = b["rstd"][ns]
                for dc in range(8):
                    ev = P.op(dve, lambda e, dc=dc: e.scalar_tensor_tensor(
                        out=b["hT"][:, dc, ns * 512:(ns + 1) * 512], in0=xa[:, dc, :],
                        scalar=self.gain_sb[:, gidx, dc:dc + 1], in1=rstd[:],
                        op0=ALU.mult, op1=ALU.mult),
                        deps=[ev_ld, ev_rs] + st["hT_free"] + self.const_ev)
                evs.append(ev)
                b["xa"].free[s] = [ev]
                st["rstd_free"][ns] = [ev]
            A_state[k] = evs

        def stage_B(k, hook=None):
            hT_ready = A_state[k]
            last_mm = None
            act_evs = []
            for fc in range(NFC):
                if hook is not None and fc == 14:
                    hook()
                s, w, sem, fr = b["wgu"].next()
                ev_w = P.dma(pool, sem, w[:], self.wgu_b[i, fc], deps=fr + [self.conv_ev[i]])
                for ns in range(2):
                    ui = st["unit_i"] % len(units)
                    st["unit_i"] += 1
                    pg, pu = units[ui]
                    deps = [ev_w] + hT_ready + st["unit_free"][ui]
                    first = True
                    for half, bank in ((0, pg), (1, pu)):
                        for dc in range(8):
                            f = lambda e, half=half, bank=bank, dc=dc: e.matmul(
                                bank[:], lhsT=w[:, (half * 8 + dc) * 128:(half * 8 + dc + 1) * 128],
                                rhs=b["hT"][:, dc, ns * 512:(ns + 1) * 512], start=(dc == 0), stop=(dc == 7))
                            if dc < 7:
                                P.op_nosig(pe, f, deps=deps if first else ())
                            else:
                                ev = P.op(pe, f)
                            first = False
                        if half == 0:
                            ev_g = ev
                        else:
                            ev_u = ev
                    last_mm = ev_u
                    si = st["sg_i"] % 2
                    st["sg_i"] += 1
                    sg = b["sg"][si]
                    ev_s = P.op(act, lambda e: e.activation(out=sg[:], in_=pg[:], func=AF.Silu),
                                deps=[ev_g] + st["sg_free"][si])
                    ev_a = P.op(dve, lambda e: e.tensor_tensor(out=b["actT"][:, fc, ns * 512:(ns + 1) * 512],
                                                               in0=sg[:], in1=pu[:], op=ALU.mult),
                                deps=[ev_s, ev_u] + st["act_free"])
                    st["sg_free"][si] = [ev_a]
                    st["unit_free"][ui] = [ev_a]
                    act_evs.append(ev_a)
                b["wgu"].free[s] = [last_mm]
            st["hT_free"] = [last_mm]
            return act_evs

        def stage_C(k, act_evs):
            t0 = k * NT
            last_mm = None
            for dc in range(8):
                s, w, sem, fr = b["wd"].next()
                ev_w = None
                for h in range(2):
                    ev_w = P.dma(pool, sem, w[:, h * 1408:(h + 1) * 1408], self.wd_b[i, dc, :, h * 1408:(h + 1) * 1408],
                                 deps=fr + [self.conv_ev[i]])
                for ns in range(2):
                    mi = st["misc_i"] % 2
                    st["misc_i"] += 1
                    bank = misc[mi]
                    deps = [ev_w] + act_evs[-2:] + st["misc_free"][mi]
                    for fc in range(NFC):
                        f = lambda e, fc=fc: e.matmul(bank[:], lhsT=w[:, fc * 128:(fc + 1) * 128],
                                                      rhs=b["actT"][:, fc, ns * 512:(ns + 1) * 512],
                                                      start=(fc == 0), stop=(fc == NFC - 1))
                        if fc < NFC - 1:
                            P.op_nosig(pe, f, deps=deps if fc == 0 else ())
                        else:
                            ev_mm = P.op(pe, f)
                    last_mm = ev_mm
                    sx, xc, semx, frx = b["xc"].next()
                    ev_ld = P.dma(sp, semx, xc[:], xs[:, dc, t0 + ns * 512:t0 + (ns + 1) * 512],
                                  deps=list(src_ready) + frx)
                    ev_r = P.op(dve, lambda e: e.scalar_tensor_tensor(out=xc[:], in0=bank[:], scalar=0.5, in1=xc[:],
                                                                      op0=ALU.mult, op1=ALU.add),
                                deps=[ev_mm, ev_ld])
                    st["misc_free"][mi] = [ev_r]
                    ev_st = P.dma(sp, b["st_sems"][sx], xd[:, dc, t0 + ns * 512:t0 + (ns + 1) * 512], xc[:], deps=[ev_r])
                    b["xc"].free[sx] = [ev_st]
                    store_evs.append(ev_st)
                b["wd"].free[s] = [last_mm]
            st["act_free"] = [last_mm]

        stage_A1(0)
        stage_A2(0)
        for k in range(ntile):
            nxt = (lambda k=k: stage_A1(k + 1)) if k + 1 < ntile else None
            act_evs = stage_B(k, nxt)
            if k + 1 < ntile:
                stage_A2(k + 1)
            stage_C(k, act_evs)
        return store_evs[-4:]

    def decl_even(self):
        dt = self.nc.dram_tensor
        self.pos = dt("pos", [128, S // 128], I32, kind="ExternalInput").ap()
        self.win_f = dt("win_f", [1024, 2048], F32, kind="ExternalInput").ap()
        self.wout_f = dt("wout_f", [512, 2048], F32, kind="ExternalInput").ap()
        self.poolw_f = dt("poolw_f", [128, 512], F32, kind="ExternalInput").ap()
        self.pscale = dt("pscale", [128, 4], F32, kind="ExternalInput").ap()
        self.qkn = dt("qkn", [2, 64], F32, kind="ExternalInput").ap()
        self.cst = dt("cst", [128, 8 + 64], F32, kind="ExternalInput").ap()
        self.win_b = dt("win_b", [1024, 2048], BF16, kind="Internal").ap()
        self.wout_b = dt("wout_b", [512, 2048], BF16, kind="Internal").ap()
        self.poolw_b = dt("poolw_b", [128, 512], BF16, kind="Internal").ap()
        self.catT = dt("catT", [1024, S], BF16, kind="Internal").ap()
        self.vd = dt("vd", [S, 512], BF16, kind="Internal").ap()

    def convert_even(self):
        P = self.P
        sem = P.newsem("cv_even", sw=True)
        ev = None
        for j in range(8):
            ev = P.dma(P.pool, sem, self.win_b[j * 128:(j + 1) * 128], self.win_f[j * 128:(j + 1) * 128])
        for j in range(4):
            ev = P.dma(P.pool, sem, self.wout_b[j * 128:(j + 1) * 128], self.wout_f[j * 128:(j + 1) * 128])
        ev = P.dma(P.pool, sem, self.poolw_b, self.poolw_f)
        self.conv_ev["even"] = ev

    def norm_tile(self, nb, xs, t0, gidx, hT, src_ready, ps, hT_free, octet=None):
        P = self.P
        pe, act, dve, sp = P.pe, P.act, P.dve, P.sp
        s, xa, sem, fr = nb["xa"].next()
        ev_ld = P.dma(sp, sem, xa[:], xs[:, :, t0:t0 + 512], deps=list(src_ready) + fr)
        ev_sq = P.op(act, lambda e: e.activation(out=nb["sq"][:], in_=xa[:], func=AF.Square),
                     deps=[ev_ld] + nb["sq_free"])
        bank, bfree, bi = self.bank_next(ps)
        deps = [ev_sq] + bfree + self.const_ev
        for dc in range(8):
            f = lambda e, dc=dc: e.matmul(bank[:], lhsT=self.ones_bf[:], rhs=nb["sq"][:, dc, :],
                                          start=(dc == 0), stop=(dc == 7))
            if dc < 7:
                P.op_nosig(pe, f, deps=deps if dc == 0 else ())
            else:
                ev_mm = P.op(pe, f)
        nb["sq_free"] = [ev_mm]
        ev_rt = P.op(act, lambda e: e.activation(out=nb["rt"][:], in_=bank[:], func=AF.Sqrt,
                                                 scale=1.0 / D, bias=self.eps_sb[:]),
                     deps=[ev_mm] + nb["rt_free"])
        ps["free"][bi] = [ev_rt]
        ev_rs = P.op(dve, lambda e: e.reciprocal(out=nb["rstd"][:], in_=nb["rt"][:]),
                     deps=[ev_rt] + nb["rstd_free"])
        nb["rt_free"] = [ev_rs]
        for dc in range(8):
            if octet is None:
                o_, i0_, i1_ = hT[:, dc, :], xa[:, dc, :], nb["rstd"][:]
            else:
                o_ = hT[:, dc, :, octet * 64:(octet + 1) * 64].rearrange("p i q -> p q i")
                i0_ = xa[:, dc, :].rearrange("p (q i) -> p q i", i=8)
                i1_ = nb["rstd"][:].rearrange("p (q i) -> p q i", i=8)
            ev = P.op(dve, lambda e, dc=dc: e.scalar_tensor_tensor(
                out=o_, in0=i0_, scalar=self.gain_sb[:, gidx, dc:dc + 1], in1=i1_,
                op0=ALU.mult, op1=ALU.mult), deps=[ev_ld, ev_rs] + hT_free + self.const_ev)
        nb["xa"].free[s] = [ev]
        nb["rstd_free"] = [ev]
        return ev

    def alloc_norm(self, st):
        P = self.P
        return {"xa": Ring(P, 2, [128, 8, 512], F32, "nxa", st), "sq": P.sbuf([128, 8, 512], BF16, "nsq", st),
                "rt": P.sbuf([128, 512], F32, "nrt", st), "rstd": P.sbuf([128, 512], F32, "nrstd", st),
                "sq_free": [], "rt_free": [], "rstd_free": []}

    def bank_next(self, ps):
        bi = ps["i"] % len(ps["banks"])
        ps["i"] += 1
        fr = ps["free"][bi]
        ps["free"][bi] = []
        return ps["banks"][bi], fr, bi

    def even_consts(self, st):
        P = self.P
        pool, dve, act, sp = P.pool, P.dve, P.act, P.sp
        c = {}
        nblk = S // 128
        c["ident"] = P.sbuf([128, 128], BF16, "ident", st)
        c["mask"] = P.sbuf([128, 256], BF16, "mask", st)
        c["ones64"] = P.sbuf([128, 64], BF16, "ones64", st)
        c["gqk"] = P.sbuf([128, 2, 64], F32, "gqk", st)
        c["pscale"] = P.sbuf([128, 4], F32, "pscale", st)
        c["cst"] = P.sbuf([128, 72], F32, "cst", st)
        c["cos"] = P.sbuf([128, nblk, 8], F32, "cos", st)
        c["sin"] = P.sbuf([128, nblk, 8], F32, "sin", st)
        posi = P.sbuf([128, nblk], I32, "posi", st)
        posf = P.sbuf([128, nblk], F32, "posf", st)
        ang = P.sbuf([128, nblk, 8], F32, "ang", st)
        kf = P.sbuf([128, nblk, 8], F32, "kf", st)
        ki = P.sbuf([128, nblk, 8], I32, "ki", st)
        red = P.sbuf([128, nblk, 8], F32, "red", st)
        sem = P.newsem("ec")
        evs = []
        e1 = P.op(pool, lambda e: e.memset(c["ident"][:], 1.0))
        e1 = P.op(pool, lambda e: e.affine_select(out=c["ident"][:], in_=c["ident"][:], pattern=[[-1, 128]],
                                                  compare_op=ALU.is_equal, fill=0.0, base=0, channel_multiplier=1), deps=[e1])
        evs.append(e1)
        e2 = P.op(pool, lambda e: e.memset(c["mask"][:], 1.0))
        e3 = P.op(pool, lambda e: e.affine_select(out=c["mask"][:, 0:128], in_=c["mask"][:, 0:128], pattern=[[-1, 128]],
                                                  compare_op=ALU.is_ge, fill=0.0, base=0, channel_multiplier=1), deps=[e2])
        e4 = P.op(pool, lambda e: e.affine_select(out=c["mask"][:, 128:256], in_=c["mask"][:, 128:256], pattern=[[1, 128]],
                                                  compare_op=ALU.is_ge, fill=0.0, base=0, channel_multiplier=-1), deps=[e2])
        evs += [e3, e4]
        evs.append(P.op(pool, lambda e: e.memset(c["ones64"][:], 1.0)))
        d1 = P.dma(sp, sem, c["gqk"][:, 0, :], self.qkn[0].partition_broadcast(128))
        d1 = P.dma(sp, sem, c["gqk"][:, 1, :], self.qkn[1].partition_broadcast(128))
        d1 = P.dma(sp, sem, c["pscale"][:], self.pscale)
        d1 = P.dma(sp, sem, c["cst"][:], self.cst)
        d1 = P.dma(sp, sem, posi[:], self.pos)
        evs.append(d1)
        ev = P.op(dve, lambda e: e.tensor_copy(out=posf[:], in_=posi[:]), deps=[d1])
        ev = P.op(dve, lambda e: e.tensor_tensor(out=ang[:], in0=posf[:].unsqueeze(2).to_broadcast([128, nblk, 8]),
                                                 in1=c["cst"][:, 0:8].unsqueeze(1).to_broadcast([128, nblk, 8]), op=ALU.mult), deps=[ev])
        C1 = 6.28125
        C2 = 2.0 * math.pi - C1
        for which, shift in (("sin", 0.0), ("cos", math.pi / 2)):
            ev = P.op(dve, lambda e: e.tensor_scalar(out=kf[:], in0=ang[:], scalar1=shift, scalar2=1.0 / (2 * math.pi),
                                                     op0=ALU.add, op1=ALU.mult), deps=[ev])
            ev = P.op(dve, lambda e: e.tensor_copy(out=ki[:], in_=kf[:]), deps=[ev])
            ev = P.op(dve, lambda e: e.tensor_copy(out=kf[:], in_=ki[:]), deps=[ev])
            ev = P.op(dve, lambda e: e.scalar_tensor_tensor(out=red[:], in0=kf[:], scalar=-C1, in1=ang[:],
                                                            op0=ALU.mult, op1=ALU.add), deps=[ev])
            ev = P.op(dve, lambda e: e.scalar_tensor_tensor(out=red[:], in0=kf[:], scalar=-C2, in1=red[:],
                                                            op0=ALU.mult, op1=ALU.add), deps=[ev])
            ev = P.op(dve, lambda e: e.tensor_scalar(out=red[:], in0=red[:], scalar1=shift, scalar2=None, op0=ALU.add), deps=[ev])
            ev = P.op(dve, lambda e: e.tensor_scalar(out=kf[:], in0=red[:], scalar1=math.pi, scalar2=-2 * math.pi, op0=ALU.is_gt, op1=ALU.mult), deps=[ev])
            ev = P.op(dve, lambda e: e.tensor_tensor(out=red[:], in0=red[:], in1=kf[:], op=ALU.add), deps=[ev])
            ev = P.op(dve, lambda e: e.tensor_scalar(out=kf[:], in0=red[:], scalar1=-math.pi, scalar2=2 * math.pi, op0=ALU.is_lt, op1=ALU.mult), deps=[ev])
            ev = P.op(dve, lambda e: e.tensor_tensor(out=red[:], in0=red[:], in1=kf[:], op=ALU.add), deps=[ev])
            ev = P.op(act, lambda e, which=which: e.activation(out=c[which][:], in_=red[:], func=AF.Sin), deps=[ev])
        evs.append(ev)
        c["evs"] = evs
        return c

    def even_E1(self, st, c, qT, kT, ps, src_ready):
        P = self.P
        pe, act, dve, pool, sp = P.pe, P.act, P.dve, P.pool, P.sp
        xs = self.xout.rearrange("(dc p) t -> p dc t", p=128)
        nb = self.alloc_norm(st)
        Win = P.sbuf([128, 8, 2048], BF16, "Win", st)
        poolw = P.sbuf([128, 4, 128], BF16, "poolw", st)
        hT = P.sbuf([128, 8, 512], BF16, "ehT", st)
        lanes = []
        for _l in range(2):
            lanes.append({"sqt": P.sbuf([128, 512], F32, "sqt", st), "ss": P.sbuf([128, 8], F32, "ss", st),
                          "rt8": P.sbuf([128, 8], F32, "rt8", st), "rs8": P.sbuf([128, 8], F32, "rs8", st),
                          "qn": P.sbuf([128, 8, 64], F32, "qn", st), "tA": P.sbuf([128, 8, 8], F32, "tA", st),
                          "tB": P.sbuf([128, 8, 8], F32, "tB", st),
                          "qb": [P.sbuf([128, 512], BF16, "qb", st) for _ in range(2)],
                          "qb_free": [[], []], "qbi": 0, "free": []})
        vb = Ring(P, 2, [128, 512], BF16, "vb", st)
        pbuf = P.sbuf([128, 4, 528], F32, "pbuf", st)
        sA = P.sbuf([128, 528], F32, "sA", st)
        sB = P.sbuf([128, 528], F32, "sB", st)
        pooled = P.sbuf([128, 512], BF16, "pooled", st)
        tmp16 = P.sbuf([128, 16], F32, "tmp16", st)
        pooled_free = []
        catp = Ring(P, 2, [128, 512], BF16, "catp", st)
        sem = P.newsem("e1w", sw=True)
        ev_w = P.dma(pool, sem, Win[:], self.win_b.rearrange("(p dc) f -> p dc f", dc=8), deps=[self.conv_ev["even"]])
        ev_w = P.dma(pool, sem, poolw[:], self.poolw_b.rearrange("c (g e) -> c g e", g=4), deps=[self.conv_ev["even"]])
        ev0 = P.op(dve, lambda e: e.memset(sA[:], 0.0))
        ev0 = P.op(dve, lambda e: e.memset(sB[:], 0.0))
        ev0 = P.op(dve, lambda e: e.memset(pbuf[:, :, 0:16], 0.0))
        cev = c["evs"]
        hT_free = []
        qb_free = [[], []]
        qbi = 0
        halo_ev = [ev0] * 4
        small_free = []
        stores = []
        ntile = S // 512
        windows = (2, 4, 8, 16)
        for n in range(ntile):
            t0 = n * 512
            ev_h = self.norm_tile(nb, xs, t0, 4, hT, src_ready, ps, hT_free)
            last_pe = None
            for tb in range(4):
                blk = n * 4 + tb
                bk = {}
                for sec in range(3):
                    bank, bfree, bi = self.bank_next(ps)
                    deps = [ev_h, ev_w] + bfree
                    for dc in range(8):
                        f = lambda e, dc=dc: e.matmul(bank[:], lhsT=hT[:, dc, tb * 128:(tb + 1) * 128],
                                                      rhs=Win[:, dc, sec * 512:(sec + 1) * 512], start=(dc == 0), stop=(dc == 7))
                        if dc < 7:
                            P.op_nosig(pe, f, deps=deps if dc == 0 else ())
                        else:
                            ev_mm = P.op(pe, f)
                    last_pe = ev_mm
                    bk[sec] = (bank, bi, ev_mm)
                bank, bi, ev_mm = bk[2]
                s, vbt, vsem, vfr = vb.next()
                ev_v = P.op(act, lambda e: e.activation(out=vbt[:], in_=bank[:], func=AF.Copy), deps=[ev_mm] + vfr)
                ps["free"][bi] = [ev_v]
                ev_st = P.dma(sp, vsem, self.vd[blk * 128:(blk + 1) * 128, :], vbt[:], deps=[ev_v])
                vb.free[s] = [ev_st]
                stores.append(ev_st)

                def qk_chain(sec):
                    bank, bi, ev_mm = bk[sec]
                    L = lanes[sec]
                    sqt, ss, rt8, rs8, qn, tA, tB = L["sqt"], L["ss"], L["rt8"], L["rs8"], L["qn"], L["tA"], L["tB"]
                    bank3 = bank[:].rearrange("p (h e) -> p h e", h=8)
                    ev = P.op(act, lambda e: e.activation(out=sqt[:], in_=bank[:], func=AF.Square), deps=[ev_mm] + L["free"])
                    yield
                    ev = P.op(dve, lambda e: e.tensor_reduce(out=ss[:], in_=sqt[:].rearrange("p (h e) -> p h e", h=8),
                                                             axis=mybir.AxisListType.X, op=ALU.add), deps=[ev] + L["free"])
                    yield
                    ev = P.op(act, lambda e: e.activation(out=rt8[:], in_=ss[:], func=AF.Sqrt, scale=1.0 / 64, bias=self.eps_sb[:]),
                              deps=[ev])
                    yield
                    ev = P.op(dve, lambda e: e.reciprocal(out=rs8[:], in_=rt8[:]), deps=[ev])
                    yield
                    ev = P.op(dve, lambda e: e.tensor_tensor(out=qn[:], in0=bank3, in1=rs8[:].unsqueeze(2).to_broadcast([128, 8, 64]),
                                                             op=ALU.mult), deps=[ev])
                    ps["free"][bi] = [ev]
                    yield
                    ev = P.op(dve, lambda e: e.tensor_tensor(out=qn[:], in0=qn[:],
                                                             in1=c["gqk"][:, sec, :].unsqueeze(1).to_broadcast([128, 8, 64]),
                                                             op=ALU.mult), deps=[ev] + cev)
                    yield
                    q_b = L["qb"][L["qbi"] % 2]
                    qfr = L["qb_free"][L["qbi"] % 2]
                    ev_c = P.op(act, lambda e: e.activation(out=q_b[:], in_=qn[:].rearrange("p h e -> p (h e)"), func=AF.Copy),
                                deps=[ev] + qfr)
                    yield
                    q3 = q_b[:].rearrange("p (h e) -> p h e", h=8)
                    cosb = c["cos"][:, blk, :].unsqueeze(1).to_broadcast([128, 8, 8])
                    sinb = c["sin"][:, blk, :].unsqueeze(1).to_broadcast([128, 8, 8])
                    t1 = qn[:, :, 0:8]
                    t2 = qn[:, :, 8:16]
                    e_a = P.op(dve, lambda e: e.tensor_tensor(out=tA[:], in0=t1, in1=cosb, op=ALU.mult), deps=[ev])
                    e_b = P.op(dve, lambda e: e.tensor_tensor(out=tB[:], in0=t2, in1=sinb, op=ALU.mult), deps=[ev])
                    yield
                    e_r1 = P.op(dve, lambda e: e.tensor_tensor(out=q3[:, :, 0:8], in0=tA[:], in1=tB[:], op=ALU.subtract),
                                deps=[e_a, e_b, ev_c])
                    yield
                    e_a = P.op(dve, lambda e: e.tensor_tensor(out=tA[:], in0=t2, in1=cosb, op=ALU.mult), deps=[e_r1])
                    e_b = P.op(dve, lambda e: e.tensor_tensor(out=tB[:], in0=t1, in1=sinb, op=ALU.mult), deps=[e_r1])
                    yield
                    e_r2 = P.op(dve, lambda e: e.tensor_tensor(out=q3[:, :, 8:16], in0=tA[:], in1=tB[:], op=ALU.add),
                                deps=[e_a, e_b])
                    L["free"] = [e_r2]
                    yield
                    tbank, tfree, ti = self.bank_next(ps)
                    tb16 = tbank[:].bitcast(BF16)
                    for cc in range(4):
                        f = lambda e, cc=cc: e.transpose(tb16[:, cc * 128:(cc + 1) * 128], q_b[:, cc * 128:(cc + 1) * 128], c["ident"][:])
                        if cc < 3:
                            P.op_nosig(pe, f, deps=([e_r2] + tfree + cev) if cc == 0 else ())
                        else:
                            ev_t = P.op(pe, f)
                    yield
                    dst = (qT if sec == 0 else kT)[:, :, blk * 128:(blk + 1) * 128]
                    ev_e = P.op(act, lambda e: e.activation(out=dst, in_=tb16[:, 0:512].rearrange("p (c t) -> p c t", c=4), func=AF.Copy),
                                deps=[ev_t])
                    ps["free"][ti] = [ev_e]
                    L["qb_free"][L["qbi"] % 2] = [ev_t]
                    L["qbi"] += 1

                gens = [qk_chain(0), qk_chain(1)]
                while gens:
                    for g_ in list(gens):
                        try:
                            next(g_)
                        except StopIteration:
                            gens.remove(g_)
            for pc in range(4):
                w = windows[pc]
                bank, bfree, bi = self.bank_next(ps)
                deps = [ev_h, ev_w] + bfree
                for dc in range(8):
                    f = lambda e, dc=dc: e.matmul(bank[:], lhsT=Win[:, dc, 1536 + pc * 128:1536 + (pc + 1) * 128],
                                                  rhs=hT[:, dc, :], start=(dc == 0), stop=(dc == 7))
                    if dc < 7:
                        P.op_nosig(pe, f, deps=deps if dc == 0 else ())
                    else:
                        ev_mm = P.op(pe, f)
                last_pe = ev_mm
                ev_p = P.op(act, lambda e: e.activation(out=pbuf[:, pc, 16:528], in_=bank[:], func=AF.Copy),
                            deps=[ev_mm, halo_ev[pc]])
                ps["free"][bi] = [ev_p]
                cur = pbuf[:, pc, :]
                ev = ev_p
                lvl = 1
                bufs2 = [sA, sB]
                k = 0
                while lvl < w:
                    dstb = bufs2[k % 2]
                    src = cur
                    ev = P.op(dve, lambda e, src=src, dstb=dstb, lvl=lvl: e.tensor_tensor(
                        out=dstb[:, lvl:528], in0=src[:, lvl:528], in1=src[:, 0:528 - lvl], op=ALU.add), deps=[ev])
                    cur = dstb
                    lvl *= 2
                    k += 1
                ev_pl = P.op(dve, lambda e, cur=cur: e.scalar_tensor_tensor(out=pooled[:], in0=cur[:, 16:528], scalar=1.0 / w,
                                                                           in1=pbuf[:, pc, 16:528], op0=ALU.mult, op1=ALU.subtract),
                             deps=[ev] + pooled_free)
                if n == 0:
                    ev = P.op(dve, lambda e, cur=cur: e.tensor_tensor(out=tmp16[:], in0=cur[:, 16:32],
                                                                     in1=c["cst"][:, 8 + pc * 16:8 + (pc + 1) * 16], op=ALU.mult), deps=[ev_pl] + cev)
                    ev_pl = P.op(dve, lambda e: e.tensor_tensor(out=pooled[:, 0:16], in0=tmp16[:], in1=pbuf[:, pc, 16:32], op=ALU.subtract), deps=[ev])
                halo_ev[pc] = P.op(dve, lambda e: e.tensor_copy(out=pbuf[:, pc, 0:16], in_=pbuf[:, pc, 512:528]), deps=[ev_pl])
                bank2, bfree2, bi2 = self.bank_next(ps)
                ev_m2 = P.op(pe, lambda e: e.matmul(bank2[:], lhsT=poolw[:, pc, :], rhs=pooled[:], start=True, stop=True),
                             deps=[ev_pl, ev_w] + bfree2)
                last_pe = ev_m2
                s, cp, csem, cfr = catp.next()
                ev_cp = P.op(act, lambda e: e.activation(out=cp[:], in_=bank2[:], func=AF.Identity, scale=c["pscale"][:, pc:pc + 1]),
                             deps=[ev_m2] + cfr + cev)
                ps["free"][bi2] = [ev_cp]
                ev_st = P.dma(sp, csem, self.catT[512 + pc * 128:512 + (pc + 1) * 128, t0:t0 + 512], cp[:], deps=[ev_cp])
                catp.free[s] = [ev_st]
                stores.append(ev_st)
                pooled_free = [ev_m2]
            hT_free = [last_pe]
        return stores

    def even_E2(self, st, c, qT, kT, ps, vd_ready):
        P = self.P
        pe, act, dve, pool, sp = P.pe, P.act, P.dve, P.pool, P.sp
        nblk = S // 128
        acc = P.sbuf([128, 2, S], F32, "acc", st)
        rcp = P.sbuf([128, S], F32, "rcp", st)
        att = Ring(P, 2, [128, S], BF16, "att", st)
        vring = Ring(P, 2, [128, nblk, 128], BF16, "vring", st, sw=True)
        NPT = 3
        pt = [[P.sbuf([128, 256], BF16, "pt", st) for _ in range(2)] for _ in range(NPT)]
        pt_free = [[[], []] for _ in range(NPT)]
        cev = c["evs"]
        stores = []
        patterns = (1, 4, 16)
        units = []
        for hp in range(4):
            for di, d in enumerate(patterns):
                nb = (S // d) // 128
                for r in range(d):
                    for b in range(nb):
                        units.append((hp, di, d, nb, r, b))
        vstate = {}
        state = {"acc_ev": None, "att_last": [], "last_uz": {}}

        def stage1(u):
            hp, di, d, nb, r, b = units[u]
            if (hp, di) not in vstate:
                s, vt, vsem, vfr = vring.next()
                src = self.vd[:, hp * 128:(hp + 1) * 128].rearrange("(b i r) f -> i r b f", i=128, r=d)
                ev_v = None
                for rr in range(d):
                    ev_v = P.dma(pool, vsem, vt[:, rr * nb:(rr + 1) * nb, :], src[:, rr], deps=list(vd_ready) + vfr)
                vstate[(hp, di)] = (s, vt, ev_v)
            slot = u % NPT
            q0 = r + d * 128 * b
            qsl = slice(q0, q0 + d * 127 + 1, d)
            ksl_prev = slice(q0 - d * 128, q0 - d * 128 + d * 127 + 1, d)
            bankA, frA, iA = self.bank_next(ps)
            bankB, frB, iB = self.bank_next(ps)
            evS = []
            for h, bank, fr in ((0, bankA, frA), (1, bankB, frB)):
                rows = slice(h * 64, (h + 1) * 64)
                if b > 0:
                    P.op_nosig(pe, lambda e: e.matmul(bank[:, 0:128], lhsT=kT[rows, hp, ksl_prev], rhs=qT[rows, hp, qsl], start=True, stop=True), deps=fr)
                evS.append(P.op(pe, lambda e: e.matmul(bank[:, 128:256], lhsT=kT[rows, hp, qsl], rhs=qT[rows, hp, qsl], start=True, stop=True), deps=fr))
            lo = 0 if b > 0 else 128
            evP = []
            for h, bank, bi in ((0, bankA, iA), (1, bankB, iB)):
                p_t = pt[slot][h]
                ev = P.op(act, lambda e: e.activation(out=p_t[:, lo:256], in_=bank[:, lo:256], func=AF.Exp, scale=0.125),
                          deps=[evS[h]] + pt_free[slot][h])
                ps["free"][bi] = [ev]
                eng = dve if h == 0 else pool
                ev = P.op(eng, lambda e: e.tensor_tensor(out=p_t[:, lo:256], in0=p_t[:, lo:256], in1=c["mask"][:, lo:256], op=ALU.mult),
                          deps=[ev] + cev)
                evP.append(ev)
            return evP

        def stage2(u, evP):
            hp, di, d, nb, r, b = units[u]
            s, vt, ev_v = vstate[(hp, di)]
            slot = u % NPT
            q0 = r + d * 128 * b
            qsl = slice(q0, q0 + d * 127 + 1, d)
            bankU, frU, iU = self.bank_next(ps)
            blk = r * nb + b
            first = True
            ev_uz = None
            for h in range(2):
                p_t = pt[slot][h]
                osl = slice(h * 64, (h + 1) * 64)
                vcols = slice(h * 64, (h + 1) * 64)
                for kind in range(2):
                    col = slice(kind * 128, (kind + 1) * 128)
                    dps = (frU + [ev_v] + evP + cev) if first else ()
                    if b > 0:
                        lt = vt[:, blk - 1, vcols] if kind == 0 else c["ones64"][:]
                        P.op_nosig(pe, lambda e: e.matmul(bankU[osl, col], lhsT=lt, rhs=p_t[:, 0:128], start=True, stop=False), deps=dps)
                        dps = ()
                    lt = vt[:, blk, vcols] if kind == 0 else c["ones64"][:]
                    ev_uz = P.op(pe, lambda e: e.matmul(bankU[osl, col], lhsT=lt, rhs=p_t[:, 128:256], start=(b == 0), stop=True), deps=dps)
                    first = False
                pt_free[slot][h] = [ev_uz]
            a_view = acc[:, :, qsl]
            u_view = bankU[:, 0:256].rearrange("p (k q) -> p k q", k=2)
            if di == 0:
                acc_ev = P.op(dve, lambda e: e.tensor_copy(out=a_view, in_=u_view), deps=[ev_uz] + state["att_last"])
            else:
                acc_ev = P.op(dve, lambda e: e.tensor_tensor(out=a_view, in0=a_view, in1=u_view, op=ALU.add), deps=[ev_uz])
            ps["free"][iU] = [acc_ev]
            last_in_group = (u + 1 == len(units)) or (units[u + 1][0:2] != (hp, di))
            if last_in_group:
                vring.free[s] = [ev_uz]
            last_in_hp = (u + 1 == len(units)) or (units[u + 1][0] != hp)
            if last_in_hp:
                ev = P.op(dve, lambda e: e.reciprocal(out=rcp[:], in_=acc[:, 1, :]), deps=[acc_ev])
                sa, at, asem, afr = att.next()
                ev = P.op(dve, lambda e: e.tensor_tensor(out=at[:], in0=acc[:, 0, :], in1=rcp[:], op=ALU.mult), deps=[ev] + afr)
                state["att_last"] = [ev]
                ev_st = P.dma(sp, asem, self.catT[hp * 128:(hp + 1) * 128, :], at[:], deps=[ev])
                att.free[sa] = [ev_st]
                stores.append(ev_st)

        pend = stage1(0)
        for u in range(len(units)):
            nxt = stage1(u + 1) if u + 1 < len(units) else None
            stage2(u, pend)
            pend = nxt
        return stores

    def even_E3(self, st, ps, cat_ready, src_ready):
        P = self.P
        pe, act, dve, pool, sp = P.pe, P.act, P.dve, P.pool, P.sp
        xs = self.xout.rearrange("(dc p) t -> p dc t", p=128)
        Wout = P.sbuf([128, 8, 1024], BF16, "Wout", st)
        cat = Ring(P, 2, [128, 8, 512], BF16, "cat", st)
        xc = Ring(P, 8, [128, 512], F32, "e3xc", st)
        st_sems = [P.newsem("e3st%d" % j) for j in range(8)]
        sem = P.newsem("e3w", sw=True)
        ev_w = P.dma(pool, sem, Wout[:], self.wout_b.rearrange("(p a) (b f) -> p (a b) f", p=128, b=2), deps=[self.conv_ev["even"]])
        catv = self.catT.rearrange("(c p) t -> p c t", p=128)
        stores = []
        for n in range(S // 512):
            t0 = n * 512
            s, ct, csem, cfr = cat.next()
            ev_c = P.dma(sp, csem, ct[:], catv[:, :, t0:t0 + 512], deps=list(cat_ready) + cfr)
            last = None
            pre = []
            for dc in range(8):
                sx, xt, xsem, xfr = xc.next()
                pre.append((sx, xt, P.dma(sp, xsem, xt[:], xs[:, dc, t0:t0 + 512], deps=list(src_ready) + xfr)))
            for dc in range(8):
                bank, bfree, bi = self.bank_next(ps)
                deps = [ev_c, ev_w] + bfree
                for cc in range(8):
                    f = lambda e, cc=cc: e.matmul(bank[:], lhsT=Wout[:, cc, dc * 128:(dc + 1) * 128], rhs=ct[:, cc, :],
                                                  start=(cc == 0), stop=(cc == 7))
                    if cc < 7:
                        P.op_nosig(pe, f, deps=deps if cc == 0 else ())
                    else:
                        ev_mm = P.op(pe, f)
                last = ev_mm
                sx, xt, ev_ld = pre[dc]
                ev_r = P.op(dve, lambda e: e.tensor_tensor(out=xt[:], in0=bank[:], in1=xt[:], op=ALU.add), deps=[ev_mm, ev_ld])
                ps["free"][bi] = [ev_r]
                ev_st = P.dma(sp, st_sems[sx], xs[:, dc, t0:t0 + 512], xt[:], deps=[ev_r])
                xc.free[sx] = [ev_st]
                stores.append(ev_st)
            cat.free[s] = [last]
        return stores[-8:]

    def even_mixer(self, ps, src_ready):
        P = self.P
        with contextlib.ExitStack() as st0:
            c = self.even_consts(st0)
            qT = P.sbuf([128, 4, S], BF16, "qT", st0)
            kT = P.sbuf([128, 4, S], BF16, "kT", st0)
            with contextlib.ExitStack() as st1:
                ev1 = self.even_E1(st1, c, qT, kT, ps, src_ready)
            P.barrier(ev1)
            ps["free"] = [[] for _ in ps["banks"]]
            with contextlib.ExitStack() as st2:
                ev2 = self.even_E2(st2, c, qT, kT, ps, ev1)
            P.barrier(ev2)
            ps["free"] = [[] for _ in ps["banks"]]
        with contextlib.ExitStack() as st3:
            ev3 = self.even_E3(st3, ps, ev1 + ev2, src_ready)
        P.barrier(ev3)
        ps["free"] = [[] for _ in ps["banks"]]
        return ev3

    def decl_odd(self):
        dt = self.nc.dram_tensor
        NQ = S // 8
        self.s5_a1 = dt("s5_a1", [2, 4096], F32, kind="ExternalInput").ap()
        self.s5_a2 = dt("s5_a2", [128, 2, 64], F32, kind="ExternalInput").ap()
        self.s5_ldt = dt("s5_ldt", [64], F32, kind="ExternalInput").ap()
        self.s5_b1 = dt("s5_b1", [128, 2, 4096], F32, kind="ExternalInput").ap()
        self.s5_b2 = dt("s5_b2", [128, 2, 1024], F32, kind="ExternalInput").ap()
        self.s5_c2 = dt("s5_c2", [128, 2, 1024], F32, kind="ExternalInput").ap()
        self.s5_dv = dt("s5_dv", [1024], F32, kind="ExternalInput").ap()
        self.s5_cst = dt("s5_cst", [128, 12], F32, kind="ExternalInput").ap()
        self.s5_iota = dt("s5_iota", [128, 512], F32, kind="ExternalInput").ap()
        self.wglu_f = dt("wglu_f", [1024, 2048], F32, kind="ExternalInput").ap()
        self.wglu_b = dt("wglu_b", [1024, 2048], BF16, kind="Internal").ap()
        self.Ud = dt("Ud", [64, 128, NQ], BF16, kind="Internal").ap()
        self.Tabd = dt("Tabd", [32, 128, 5, 256], BF16, kind="Internal").ap()

    def convert_odd(self):
        P = self.P
        sem = P.newsem("cv_odd", sw=True)
        ev = None
        for j in range(8):
            ev = P.dma(P.pool, sem, self.wglu_b[j * 128:(j + 1) * 128], self.wglu_f[j * 128:(j + 1) * 128])
        self.conv_ev["odd"] = ev

    def trig(self, ch, ang, kf, ki, red, out_sin, out_cos):
        C1 = 6.28125
        C2 = 2.0 * math.pi - C1
        ch.dve(lambda e: e.tensor_scalar(out=red, in0=ang, scalar1=1.0 / (2 * math.pi), scalar2=None, op0=ALU.mult))
        ch.dve(lambda e: e.tensor_copy(out=ki, in_=red))
        ch.dve(lambda e: e.scalar_tensor_tensor(out=red, in0=ki, scalar=-C1, in1=ang, op0=ALU.mult, op1=ALU.add))
        ch.dve(lambda e: e.scalar_tensor_tensor(out=red, in0=ki, scalar=-C2, in1=red, op0=ALU.mult, op1=ALU.add))
        ev_r = ch.dve(lambda e: e.tensor_scalar(out=red, in0=red, scalar1=-math.pi, scalar2=math.pi, op0=ALU.max, op1=ALU.min))
        if out_sin is not None:
            ch.act(lambda e: e.activation(out=out_sin, in_=red, func=AF.Sin))
        if out_cos is not None:
            ch.dve(lambda e: e.scalar_tensor_tensor(out=kf, in0=red, scalar=-1.0, in1=red, op0=ALU.mult, op1=ALU.max), extra=[ev_r])
            ch.act(lambda e: e.activation(out=out_cos, in_=kf, func=AF.Sin, scale=-1.0, bias=self.pi2_sb[:]))

    def odd_tables(self, st0, ps, small):
        P = self.P
        stA = contextlib.ExitStack()
        st = stA
        pe, act, dve, pool, sp = P.pe, P.act, P.dve, P.pool, P.sp
        ch = Chain(P)
        sem = P.newsem("o2ld")
        f32 = lambda shape, nm: P.sbuf(shape, F32, nm, st)
        cst = P.sbuf([128, 12], F32, "ocst", st0)
        dt2 = P.sbuf([128, 64], F32, "dt2", st0)
        a2 = f32([128, 2, 64], "a2")
        ldt = f32([128, 64], "ldt")
        b2 = f32([128, 2, 1024], "b2")
        c2 = f32([128, 2, 1024], "c2")
        dv = f32([128, 1024], "dv")
        evs = [P.dma(sp, sem, cst[:], self.s5_cst), P.dma(sp, sem, a2[:], self.s5_a2),
               P.dma(sp, sem, ldt[:], self.s5_ldt.partition_broadcast(128)), P.dma(sp, sem, b2[:], self.s5_b2),
               P.dma(sp, sem, c2[:], self.s5_c2), P.dma(sp, sem, dv[:], self.s5_dv.partition_broadcast(128))]
        ch.last = evs[-1]
        ncol, n1col, mlo, mhi = cst[:, 0:1], cst[:, 1:2], cst[:, 2:3], cst[:, 3:4]
        T = [None, None] + [P.sbuf([128, 64, 128], BF16, "tab%d" % t, st) for t in range(2, 5)]
        lr2 = f32([128, 64], "lr2"); li2 = a2[:, 1, :]
        lrdt2 = f32([128, 64], "lrdt2"); th2 = f32([128, 64], "th2")
        Lre = f32([128, 64, 9], "Lre"); Lim = f32([128, 64, 9], "Lim")
        mag = f32([128, 64], "mag"); ang = f32([128, 64], "ang2"); kf = f32([128, 64], "kf2")
        ki = P.sbuf([128, 64], I32, "ki2", st); red = f32([128, 64], "red2")
        cs = f32([128, 64], "cs2"); sn = f32([128, 64], "sn2")
        ch.act(lambda e: e.activation(out=dt2[:], in_=ldt[:], func=AF.Exp))
        ch.dve(lambda e: e.tensor_scalar(out=lr2[:], in0=a2[:, 0, :], scalar1=-1e-4, scalar2=None, op0=ALU.min))
        ch.dve(lambda e: e.tensor_tensor(out=lrdt2[:], in0=lr2[:], in1=dt2[:], op=ALU.mult))
        ch.dve(lambda e: e.tensor_tensor(out=th2[:], in0=li2, in1=dt2[:], op=ALU.mult))
        for ex in range(9):
            ch.act(lambda e, ex=ex: e.activation(out=mag[:], in_=lrdt2[:], func=AF.Exp, scale=float(ex)))
            ch.dve(lambda e, ex=ex: e.tensor_scalar(out=ang[:], in0=th2[:], scalar1=float(ex), scalar2=None, op0=ALU.mult))
            self.trig(ch, ang[:], kf[:], ki[:], red[:], sn[:], cs[:])
            ch.dve(lambda e, ex=ex: e.tensor_tensor(out=Lre[:, :, ex], in0=mag[:], in1=cs[:], op=ALU.mult))
            ch.dve(lambda e, ex=ex: e.tensor_tensor(out=Lim[:, :, ex], in0=mag[:], in1=sn[:], op=ALU.mult))
            if ex == 8:
                ch.dve(lambda e: e.tensor_scalar(out=kf[:, 0:32], in0=mag[:, 0::2], scalar1=mlo, scalar2=None, op0=ALU.mult))
                ch.dve(lambda e: e.scalar_tensor_tensor(out=small["r8p"][:], in0=mag[:, 1::2], scalar=mhi, in1=kf[:, 0:32], op0=ALU.mult, op1=ALU.add))
                ch.dve(lambda e: e.tensor_scalar(out=kf[:, 0:32], in0=ang[:, 0::2], scalar1=mlo, scalar2=None, op0=ALU.mult))
                ch.dve(lambda e: e.scalar_tensor_tensor(out=small["php"][:], in0=ang[:, 1::2], scalar=mhi, in1=kf[:, 0:32], op0=ALU.mult, op1=ALU.add))
        den = f32([128, 64], "den2"); cre = f32([128, 64], "cre2"); cim = f32([128, 64], "cim2"); nre = f32([128, 64], "nre2")
        t1 = f32([128, 64], "t12")
        ch.dve(lambda e: e.tensor_tensor(out=den[:], in0=lr2[:], in1=lr2[:], op=ALU.mult))
        ch.dve(lambda e: e.tensor_tensor(out=t1[:], in0=li2, in1=li2, op=ALU.mult))
        ch.dve(lambda e: e.tensor_tensor(out=den[:], in0=den[:], in1=t1[:], op=ALU.add))
        ch.dve(lambda e: e.reciprocal(out=den[:], in_=den[:]))
        ch.dve(lambda e: e.tensor_scalar(out=nre[:], in0=Lre[:, :, 1], scalar1=-1.0, scalar2=None, op0=ALU.add))
        ch.dve(lambda e: e.tensor_tensor(out=cre[:], in0=nre[:], in1=lr2[:], op=ALU.mult))
        ch.dve(lambda e: e.tensor_tensor(out=t1[:], in0=Lim[:, :, 1], in1=li2, op=ALU.mult))
        ch.dve(lambda e: e.tensor_tensor(out=cre[:], in0=cre[:], in1=t1[:], op=ALU.add))
        ch.dve(lambda e: e.tensor_tensor(out=cre[:], in0=cre[:], in1=den[:], op=ALU.mult))
        ch.dve(lambda e: e.tensor_tensor(out=cim[:], in0=Lim[:, :, 1], in1=lr2[:], op=ALU.mult))
        ch.dve(lambda e: e.tensor_tensor(out=t1[:], in0=nre[:], in1=li2, op=ALU.mult))
        ch.dve(lambda e: e.tensor_tensor(out=cim[:], in0=cim[:], in1=t1[:], op=ALU.subtract))
        ch.dve(lambda e: e.tensor_tensor(out=cim[:], in0=cim[:], in1=den[:], op=ALU.mult))
        w1 = f32([128, 64, 16], "w1"); w2 = f32([128, 64, 16], "w2"); w3 = f32([128, 64, 16], "w3")
        Brep = P.sbuf([128, 64, 8, 16], BF16, "Brep", st)
        bre = b2[:, 0, :].rearrange("r (g c) -> r g c", g=64); bim = b2[:, 1, :].rearrange("r (g c) -> r g c", g=64)
        creb = cre[:].unsqueeze(2).to_broadcast([128, 64, 16]); cimb = cim[:].unsqueeze(2).to_broadcast([128, 64, 16])
        ch.dve(lambda e: e.tensor_tensor(out=w1[:], in0=bre, in1=creb, op=ALU.mult))
        ch.dve(lambda e: e.tensor_tensor(out=w2[:], in0=bim, in1=cimb, op=ALU.mult))
        ch.dve(lambda e: e.tensor_tensor(out=w1[:], in0=w1[:], in1=w2[:], op=ALU.subtract))
        ch.dve(lambda e: e.tensor_tensor(out=w2[:], in0=bim, in1=creb, op=ALU.mult))
        ch.dve(lambda e: e.tensor_tensor(out=w3[:], in0=bre, in1=cimb, op=ALU.mult))
        ch.dve(lambda e: e.tensor_tensor(out=w2[:], in0=w2[:], in1=w3[:], op=ALU.add))
        ch.dve(lambda e: e.tensor_scalar(out=w1[:], in0=w1[:], scalar1=mlo, scalar2=None, op0=ALU.mult))
        ch.dve(lambda e: e.tensor_scalar(out=w2[:], in0=w2[:], scalar1=mhi, scalar2=None, op0=ALU.mult))
        ch.dve(lambda e: e.tensor_tensor(out=w1[:], in0=w1[:], in1=w2[:], op=ALU.subtract))
        ch.dve(lambda e: e.tensor_copy(out=Brep[:], in_=w1[:].unsqueeze(2).to_broadcast([128, 64, 8, 16])))
        pmask = f32([128, 64], "pmask")
        ch.dve(lambda e: e.tensor_copy(out=pmask[:, 0::2], in_=mlo.to_broadcast([128, 32])))
        ch.dve(lambda e: e.tensor_copy(out=pmask[:, 1::2], in_=mhi.to_broadcast([128, 32])))
        pmb = pmask[:].unsqueeze(2).to_broadcast([128, 64, 16])
        X = P.sbuf([128, 64, 8, 16], BF16, "Xcl", st)
        cre3 = c2[:, 0, :].rearrange("r (g c) -> r g c", g=64); cim3 = c2[:, 1, :].rearrange("r (g c) -> r g c", g=64)
        MI_re = T[3][:].rearrange("r g (j c) -> r g j c", j=8); MI_im = T[4][:].rearrange("r g (j c) -> r g j c", j=8)
        for ex in range(9):
            lre = Lre[:, :, ex].unsqueeze(2).to_broadcast([128, 64, 16]); lim = Lim[:, :, ex].unsqueeze(2).to_broadcast([128, 64, 16])
            ch.dve(lambda e, lre=lre: e.tensor_tensor(out=w1[:], in0=cre3, in1=lre, op=ALU.mult))
            ch.dve(lambda e, lim=lim: e.tensor_tensor(out=w2[:], in0=cim3, in1=lim, op=ALU.mult))
            ch.dve(lambda e: e.tensor_tensor(out=w1[:], in0=w1[:], in1=w2[:], op=ALU.subtract))
            ch.dve(lambda e, lim=lim: e.tensor_tensor(out=w2[:], in0=cre3, in1=lim, op=ALU.mult))
            ch.dve(lambda e, lre=lre: e.tensor_tensor(out=w3[:], in0=cim3, in1=lre, op=ALU.mult))
            ch.dve(lambda e: e.tensor_tensor(out=w2[:], in0=w2[:], in1=w3[:], op=ALU.add))
            if ex >= 1:
                ch.dve(lambda e, ex=ex: e.tensor_tensor(out=MI_re[:, :, ex - 1, :], in0=w1[:], in1=pmb, op=ALU.mult))
                ch.dve(lambda e, ex=ex: e.scalar_tensor_tensor(out=MI_im[:, :, ex - 1, :], in0=w2[:], scalar=-1.0, in1=pmb, op0=ALU.mult, op1=ALU.mult))
            if ex <= 7:
                ch.dve(lambda e: e.tensor_scalar(out=w3[:], in0=w1[:], scalar1=mlo, scalar2=None, op0=ALU.mult))
                ch.dve(lambda e, ex=ex: e.scalar_tensor_tensor(out=X[:, :, ex, :], in0=w2[:], scalar=mhi, in1=w3[:], op0=ALU.mult, op1=ALU.add))
        identf = f32([128, 128], "identf")
        ch.pool(lambda e: e.memset(identf[:], 1.0))
        ch.pool(lambda e: e.affine_select(out=identf[:], in_=identf[:], pattern=[[-1, 128]], compare_op=ALU.is_equal, fill=0.0, base=0, channel_multiplier=1))
        macc = f32([128, 4, 128], "macc"); dtmp = f32([128, 4, 8, 16], "dtmp")
        for gb in range(16):
            bank, bfree, bi = self.bank_next(ps)
            for gg in range(4):
                g = gb * 4 + gg
                ch.pe(lambda e, g=g, gg=gg: e.matmul(bank[:, gg * 128:(gg + 1) * 128], lhsT=Brep[:, g, :, :].rearrange("r i c -> r (i c)"),
                                                    rhs=X[:, g, :, :].rearrange("r j c -> r (j c)"), start=True, stop=True), extra=bfree)
            b3 = bank[:].rearrange("r (g x) -> r g x", g=4)
            ch.dve(lambda e: e.tensor_scalar(out=macc[:], in0=b3, scalar1=cst[:, 4:5], scalar2=None, op0=ALU.mult))
            for s in range(1, 8):
                ch.dve(lambda e, s=s: e.scalar_tensor_tensor(out=macc[:, :, 16 * s:128], in0=b3[:, :, 0:128 - 16 * s], scalar=cst[:, 4 + s:5 + s],
                                                             in1=macc[:, :, 16 * s:128], op0=ALU.mult, op1=ALU.add))
            ps["free"][bi] = [ch.last]
            dsl = dv[:, gb * 64:(gb + 1) * 64].rearrange("r (g c) -> r g c", g=4).unsqueeze(2).to_broadcast([128, 4, 8, 16])
            idb = identf[:].rearrange("r (j c) -> r j c", j=8).unsqueeze(1).to_broadcast([128, 4, 8, 16])
            ch.dve(lambda e: e.tensor_tensor(out=dtmp[:], in0=idb, in1=dsl, op=ALU.mult))
            ch.dve(lambda e, gb=gb: e.tensor_tensor(out=T[2][:, gb * 4:(gb + 1) * 4, :], in0=macc[:], in1=dtmp[:].rearrange("r g j c -> r g (j c)"), op=ALU.add))
        outs = []
        for t in range(2, 5):
            dst = self.Tabd[:, :, t, :].rearrange("k r (g x) -> r k g x", g=2)
            outs.append(P.dma(sp, sem, dst, T[t][:].rearrange("r (k g) x -> r k g x", g=2), deps=[ch.last]))
        P.barrier(outs[-1:])
        ps["free"] = [[] for _ in ps["banks"]]
        stA.close()
        stB = contextlib.ExitStack()
        st = stB
        T[0] = P.sbuf([128, 64, 128], BF16, "tab0", st)
        T[1] = P.sbuf([128, 64, 128], BF16, "tab1", st)
        ch.last = outs[-1]
        ch.pool(lambda e: e.memset(T[0][:], 0.0))
        ch.pool(lambda e: e.memset(T[1][:], 0.0))
        QG = 16
        NE = QG * 64
        a1 = f32([128, 2, NE], "a1"); b1 = f32([128, 2, NE], "b1")
        lr1 = f32([128, NE], "lr1"); lrdt1 = f32([128, NE], "lrdt1"); th1 = f32([128, NE], "th1")
        mg1 = f32([128, NE], "mg1"); an1 = f32([128, NE], "an1"); kf1 = f32([128, NE], "kf1"); ki1 = P.sbuf([128, NE], I32, "ki1", st)
        rd1 = f32([128, NE], "rd1"); cs1 = f32([128, NE], "cs1"); sn1 = f32([128, NE], "sn1")
        pa_re = f32([128, NE], "pa_re"); pa_im = f32([128, NE], "pa_im"); dn1 = f32([128, NE], "dn1")
        for qc in range(64 // QG):
            g0 = qc * QG
            sl = slice(g0 * 64, (g0 + QG) * 64)
            d1 = P.dma(sp, sem, a1[:, 0, :], self.s5_a1[0, sl].partition_broadcast(128), deps=[ch.last])
            d1 = P.dma(sp, sem, a1[:, 1, :], self.s5_a1[1, sl].partition_broadcast(128), deps=[ch.last])
            d1 = P.dma(sp, sem, b1[:], self.s5_b1[:, :, sl], deps=[ch.last])
            ch.last = d1
            dtb = dt2[:, g0:g0 + QG].unsqueeze(2).to_broadcast([128, QG, 64])
            v3 = lambda t: t[:].rearrange("r (g p) -> r g p", g=QG)
            ch.dve(lambda e: e.tensor_scalar(out=lr1[:], in0=a1[:, 0, :], scalar1=-1e-4, scalar2=None, op0=ALU.min))
            ch.dve(lambda e: e.tensor_tensor(out=v3(lrdt1), in0=v3(lr1), in1=dtb, op=ALU.mult))
            ch.dve(lambda e: e.tensor_tensor(out=v3(th1), in0=a1[:, 1, :].rearrange("r (g p) -> r g p", g=QG), in1=dtb, op=ALU.mult))
            for k_, col in enumerate((ncol, n1col)):
                ch.act(lambda e, col=col: e.activation(out=mg1[:], in_=lrdt1[:], func=AF.Exp, scale=col))
                ch.dve(lambda e, col=col: e.tensor_scalar(out=an1[:], in0=th1[:], scalar1=col, scalar2=None, op0=ALU.mult))
                self.trig(ch, an1[:], kf1[:], ki1[:], rd1[:], sn1[:], cs1[:])
                if k_ == 0:
                    ch.dve(lambda e: e.tensor_tensor(out=pa_re[:], in0=mg1[:], in1=cs1[:], op=ALU.mult))
                    ch.dve(lambda e: e.tensor_tensor(out=pa_im[:], in0=mg1[:], in1=sn1[:], op=ALU.mult))
                else:
                    ch.dve(lambda e: e.tensor_tensor(out=cs1[:], in0=mg1[:], in1=cs1[:], op=ALU.mult))
                    ch.dve(lambda e: e.tensor_tensor(out=sn1[:], in0=mg1[:], in1=sn1[:], op=ALU.mult))
                    ch.dve(lambda e: e.tensor_tensor(out=pa_re[:], in0=cs1[:], in1=pa_re[:], op=ALU.subtract))
                    ch.dve(lambda e: e.tensor_tensor(out=pa_im[:], in0=sn1[:], in1=pa_im[:], op=ALU.subtract))
            li1 = a1[:, 1, :]
            ch.dve(lambda e: e.tensor_tensor(out=dn1[:], in0=lr1[:], in1=lr1[:], op=ALU.mult))
            ch.dve(lambda e: e.tensor_tensor(out=kf1[:], in0=li1, in1=li1, op=ALU.mult))
            ch.dve(lambda e: e.tensor_tensor(out=dn1[:], in0=dn1[:], in1=kf1[:], op=ALU.add))
            ch.dve(lambda e: e.reciprocal(out=dn1[:], in_=dn1[:]))
            ch.dve(lambda e: e.tensor_tensor(out=cs1[:], in0=pa_re[:], in1=lr1[:], op=ALU.mult))
            ch.dve(lambda e: e.tensor_tensor(out=kf1[:], in0=pa_im[:], in1=li1, op=ALU.mult))
            ch.dve(lambda e: e.tensor_tensor(out=cs1[:], in0=cs1[:], in1=kf1[:], op=ALU.add))
            ch.dve(lambda e: e.tensor_tensor(out=cs1[:], in0=cs1[:], in1=dn1[:], op=ALU.mult))
            ch.dve(lambda e: e.tensor_tensor(out=sn1[:], in0=pa_im[:], in1=lr1[:], op=ALU.mult))
            ch.dve(lambda e: e.tensor_tensor(out=kf1[:], in0=pa_re[:], in1=li1, op=ALU.mult))
            ch.dve(lambda e: e.tensor_tensor(out=sn1[:], in0=sn1[:], in1=kf1[:], op=ALU.subtract))
            ch.dve(lambda e: e.tensor_tensor(out=sn1[:], in0=sn1[:], in1=dn1[:], op=ALU.mult))
            ch.dve(lambda e: e.tensor_tensor(out=mg1[:], in0=cs1[:], in1=b1[:, 0, :], op=ALU.mult))
            ch.dve(lambda e: e.tensor_tensor(out=kf1[:], in0=sn1[:], in1=b1[:, 1, :], op=ALU.mult))
            ch.dve(lambda e: e.tensor_tensor(out=mg1[:], in0=mg1[:], in1=kf1[:], op=ALU.subtract))
            ch.dve(lambda e: e.tensor_tensor(out=an1[:], in0=cs1[:], in1=b1[:, 1, :], op=ALU.mult))
            ch.dve(lambda e: e.tensor_tensor(out=kf1[:], in0=sn1[:], in1=b1[:, 0, :], op=ALU.mult))
            ch.dve(lambda e: e.tensor_tensor(out=an1[:], in0=an1[:], in1=kf1[:], op=ALU.add))
            for t, srcb in ((0, mg1), (1, an1)):
                s3 = srcb[:].rearrange("r (g p) -> r g p", g=QG)
                ch.dve(lambda e, t=t, s3=s3: e.tensor_copy(out=T[t][:, g0:g0 + QG:2, 0:64], in_=s3[:, 0::2, :]))
                ch.dve(lambda e, t=t, s3=s3: e.tensor_copy(out=T[t][:, g0 + 1:g0 + QG:2, 64:128], in_=s3[:, 1::2, :]))
        outs = []
        for t in range(2):
            dst = self.Tabd[:, :, t, :].rearrange("k r (g x) -> r k g x", g=2)
            outs.append(P.dma(sp, sem, dst, T[t][:].rearrange("r (k g) x -> r k g x", g=2), deps=[ch.last]))
        P.barrier(outs[-1:])
        stB.close()
        return outs[-1:], small

    def odd_O1(self, st, ps, src_ready):
        P = self.P
        pe, act, dve, pool, sp = P.pe, P.act, P.dve, P.pool, P.sp
        NQ = S // 8
        xs = self.xout.rearrange("(dc p) t -> p dc t", p=128)
        nb = self.alloc_norm(st)
        hTf = P.sbuf([128, 8, 8, NQ], BF16, "ohTf", st)
        Sel = P.sbuf([128, 8, 8, 128], BF16, "Sel", st)
        uring = Ring(P, 4, [128, NQ], BF16, "uring", st)
        ev = P.op(pool, lambda e: e.memset(Sel[:], 0.0))
        evs = []
        for m in range(8):
            for i in range(8):
                evs.append(P.op(pool, lambda e, m=m, i=i: e.affine_select(
                    out=Sel[:, m, i, 16 * i:16 * i + 16], in_=self.ones_bf[:, 0:16], pattern=[[-1, 16]],
                    compare_op=ALU.is_equal, fill=0.0, base=-16 * m, channel_multiplier=1), deps=[ev] + self.const_ev))
        sel_ev = [evs[-1]]
        ev_h = None
        for n in range(S // 512):
            ev_h = self.norm_tile(nb, xs, n * 512, 5, hTf, src_ready, ps, [], octet=n)
        stores = []
        k = 0
        for gc in range(8):
            for m in range(8):
                g = gc * 8 + m
                bank, bfree, bi = self.bank_next(ps)
                for i in range(8):
                    f = lambda e, i=i: e.matmul(bank[:, 0:NQ], lhsT=Sel[:, m, i, :], rhs=hTf[:, gc, i, :], start=(i == 0), stop=(i == 7))
                    if i < 7:
                        P.op_nosig(pe, f, deps=([ev_h] + bfree + sel_ev) if i == 0 else ())
                    else:
                        ev_mm = P.op(pe, f)
                s_, ub, usem, ufr = uring.next()
                if k % 2 == 0:
                    ev_c = P.op(act, lambda e: e.activation(out=ub[:], in_=bank[:, 0:NQ], func=AF.Copy), deps=[ev_mm] + ufr)
                else:
                    ev_c = P.op(dve, lambda e: e.tensor_copy(out=ub[:], in_=bank[:, 0:NQ]), deps=[ev_mm] + ufr)
                k += 1
                ps["free"][bi] = [ev_c]
                ev_st = P.dma(sp, usem, self.Ud[g], ub[:], deps=[ev_c])
                uring.free[s_] = [ev_st]
                stores.append(ev_st)
        return stores[-4:]

    def odd_O3(self, st, ps, small, ready, Ztm):
        P = self.P
        pe, act, dve, pool, sp = P.pe, P.act, P.dve, P.pool, P.sp
        NQ = S // 8
        NQT = NQ // 128
        f32 = lambda shape, nm: P.sbuf(shape, F32, nm, st)
        up = Ring(P, 3, [128, 2, NQ], BF16, "up", st)
        tabr = Ring(P, 3, [128, 5, 256], BF16, "tabr", st)
        iota = f32([128, 512], "iota")
        sem = P.newsem("o3c")
        ev_io = P.dma(sp, sem, iota[:], self.s5_iota)
        ang = f32([128, NQ], "o3ang"); kf = f32([128, NQ], "o3kf"); ki = P.sbuf([128, NQ], I32, "o3ki", st); red = f32([128, NQ], "o3red")
        cs = [f32([128, NQ], "o3cs") for _ in range(2)]; sn = [f32([128, NQ], "o3sn") for _ in range(2)]
        ta = f32([128, NQ], "o3ta"); tb = f32([128, NQ], "o3tb")
        pa = f32([128, NQ], "o3pa"); pb = f32([128, NQ], "o3pb")
        wre = f32([128, NQ], "o3wre"); wim = f32([128, NQ], "o3wim")
        zre = [f32([128, NQ], "o3zre") for _ in range(2)]; zim = [f32([128, NQ], "o3zim") for _ in range(2)]
        Hs = [[P.sbuf([128, NQ + 1], BF16, "Hs", st) for _ in range(2)] for _ in range(2)]
        hs_free = [[], []]
        rot_free = [[], []]
        ch = Chain(P)
        ev0 = None
        for par in range(2):
            for c_ in range(2):
                ev0 = P.op(pool, lambda e, par=par, c_=c_: e.memset(Hs[par][c_][:, 0:1], 0.0))
        ch.last = ev_io
        A = {}
        last = [None]

        def stageA(k):
            par = k % 2
            s, u, usem, ufr = up.next()
            ev_u = P.dma(sp, usem, u[:], self.Ud[2 * k:2 * k + 2].rearrange("g r q -> r g q"), deps=list(ready) + ufr)
            s2, tab, tsem, tfr = tabr.next()
            ev_t = P.dma(sp, tsem, tab[:], self.Tabd[k], deps=list(ready) + tfr)
            bEr, frr, ir = self.bank_next(ps)
            bEi, fri, ii = self.bank_next(ps)
            evE = []
            for t, bank, fr in ((0, bEr, frr), (1, bEi, fri)):
                P.op_nosig(pe, lambda e: e.matmul(bank[:, 0:NQ], lhsT=tab[:, t, 0:128], rhs=u[:, 0, :], start=True, stop=False),
                           deps=[ev_u, ev_t] + fr)
                evE.append(P.op(pe, lambda e: e.matmul(bank[:, 0:NQ], lhsT=tab[:, t, 128:256], rhs=u[:, 1, :], start=False, stop=True)))
            ch.dve(lambda e: e.tensor_scalar(out=ang[:], in0=iota[:, 0:NQ], scalar1=small["php"][:, k:k + 1], scalar2=None, op0=ALU.mult),
                   extra=rot_free[par])
            self.trig(ch, ang[:], kf[:], ki[:], red[:], sn[par][:], cs[par][:])
            c_, s_ = cs[par], sn[par]
            ch.dve(lambda e: e.tensor_tensor(out=ta[:], in0=c_[:], in1=bEr[:, 0:NQ], op=ALU.mult), extra=evE)
            ch.dve(lambda e: e.tensor_tensor(out=tb[:], in0=s_[:], in1=bEi[:, 0:NQ], op=ALU.mult))
            ch.dve(lambda e: e.tensor_tensor(out=wre[:], in0=ta[:], in1=tb[:], op=ALU.add))
            ch.dve(lambda e: e.tensor_tensor(out=ta[:], in0=c_[:], in1=bEi[:, 0:NQ], op=ALU.mult))
            ch.dve(lambda e: e.tensor_tensor(out=tb[:], in0=s_[:], in1=bEr[:, 0:NQ], op=ALU.mult))
            ps["free"][ir] = [ch.last]
            ps["free"][ii] = [ch.last]
            ch.dve(lambda e: e.tensor_tensor(out=wim[:], in0=ta[:], in1=tb[:], op=ALU.subtract))
            r8 = small["r8p"][:, k:k + 1].to_broadcast([128, NQ])
            ch.dve(lambda e: e.tensor_tensor_scan(out=zre[par][:], data0=r8, data1=wre[:], initial=0.0, op0=ALU.mult, op1=ALU.add))
            ev_z = ch.dve(lambda e: e.tensor_tensor_scan(out=zim[par][:], data0=r8, data1=wim[:], initial=0.0, op0=ALU.mult, op1=ALU.add))
            A[k] = (s, u, s2, tab, ev_z)

        def stageB(k):
            par = k % 2
            s, u, s2, tab, ev_z = A.pop(k)
            c_, s_, zr, zi = cs[par], sn[par], zre[par], zim[par]
            hre, him = Hs[par]
            e1 = P.op(pool, lambda e: e.tensor_tensor(out=pa[:], in0=c_[:], in1=zr[:], op=ALU.mult), deps=[ev_z, ev0] + rot_free[1 - par])
            e2 = P.op(pool, lambda e: e.tensor_tensor(out=pb[:], in0=s_[:], in1=zi[:], op=ALU.mult), deps=[ev_z] + rot_free[1 - par])
            e3 = P.op(pool, lambda e: e.tensor_tensor(out=hre[:, 1:NQ + 1], in0=pa[:], in1=pb[:], op=ALU.subtract), deps=[e1, e2] + hs_free[par])
            e4 = P.op(pool, lambda e: e.tensor_tensor(out=pa[:], in0=c_[:], in1=zi[:], op=ALU.mult), deps=[e3])
            e5 = P.op(pool, lambda e: e.tensor_tensor(out=pb[:], in0=s_[:], in1=zr[:], op=ALU.mult), deps=[e3])
            ev_H = P.op(pool, lambda e: e.tensor_tensor(out=him[:, 1:NQ + 1], in0=pa[:], in1=pb[:], op=ALU.add), deps=[e4, e5])
            rot_free[par] = [ev_H]
            ev_y = None
            for qt in range(NQT):
                by, fry, iy = self.bank_next(ps)
                qs = slice(qt * 128, (qt + 1) * 128)
                P.op_nosig(pe, lambda e: e.matmul(by[:, 0:256], lhsT=hre[:, qs], rhs=tab[:, 3, :], start=True, stop=False), deps=[ev_H] + fry)
                P.op_nosig(pe, lambda e: e.matmul(by[:, 0:256], lhsT=him[:, qs], rhs=tab[:, 4, :], start=False, stop=False))
                P.op_nosig(pe, lambda e: e.matmul(by[:, 0:128], lhsT=u[:, 0, qs], rhs=tab[:, 2, 0:128], start=False, stop=False))
                ev_y = P.op(pe, lambda e: e.matmul(by[:, 128:256], lhsT=u[:, 1, qs], rhs=tab[:, 2, 128:256], start=False, stop=True))
                dst = Ztm[:, qt, :, 32 * k:32 * k + 32].rearrange("q j (g c) -> q g j c", g=2)
                src = by[:, 0:256].rearrange("q (g j c) -> q g j c", g=2, j=8)
                last[0] = P.op(act, lambda e: e.activation(out=dst, in_=src, func=AF.Gelu_apprx_tanh), deps=[ev_y])
                ps["free"][iy] = [last[0]]
            hs_free[par] = [ev_y]
            up.free[s] = [ev_y]
            tabr.free[s2] = [ev_y]

        stageA(0)
        for k in range(32):
            if k + 1 < 32:
                stageA(k + 1)
            stageB(k)
        return [last[0]]

    def odd_O4(self, st, ps, Ztm, z_ready, src_ready):
        P = self.P
        pe, act, dve, pool, sp = P.pe, P.act, P.dve, P.pool, P.sp
        NQT = S // 1024
        xs = self.xout.rearrange("(dc p) t -> p dc t", p=128)
        Wg = P.sbuf([128, 8, 2048], BF16, "Wglu", st)
        ident = P.sbuf([128, 128], BF16, "oident", st)
        zT = [P.sbuf([128, 8, 1024], BF16, "zT", st) for _ in range(2)]
        zT_free = [[], []]
        sg = [P.sbuf([128, 512], F32, "osg", st) for _ in range(2)]
        sg_free = [[], []]
        gt = [P.sbuf([128, 512], F32, "ogt", st) for _ in range(2)]
        xc = Ring(P, 4, [128, 512], F32, "o4xc", st)
        st_sems = [P.newsem("o4st%d" % j) for j in range(4)]
        sem = P.newsem("o4w", sw=True)
        ev_w = P.dma(pool, sem, Wg[:], self.wglu_b.rearrange("(p dc) f -> p dc f", dc=8), deps=[self.conv_ev["odd"]])
        e1 = P.op(pool, lambda e: e.memset(ident[:], 1.0))
        e1 = P.op(pool, lambda e: e.affine_select(out=ident[:], in_=ident[:], pattern=[[-1, 128]], compare_op=ALU.is_equal, fill=0.0,
                                                  base=0, channel_multiplier=1), deps=[e1])
        stores = []
        ui = 0
        for qt in range(NQT):
            z = zT[qt % 2]
            evz = None
            for j in range(8):
                for half in range(2):
                    tbank, tfree, ti = self.bank_next(ps)
                    t16 = tbank[:].bitcast(BF16)
                    for cc in range(4):
                        chn = half * 4 + cc
                        f = lambda e, cc=cc, chn=chn: e.transpose(t16[:, cc * 128:(cc + 1) * 128], Ztm[:, qt, j, chn * 128:(chn + 1) * 128], ident[:])
                        if cc < 3:
                            P.op_nosig(pe, f, deps=(list(z_ready) + tfree + [e1]) if cc == 0 else ())
                        else:
                            ev_t = P.op(pe, f)
                    evz = P.op(act if half == 0 else dve,
                               (lambda e: e.activation(out=z[:, half * 4:(half + 1) * 4, j::8], in_=t16[:, 0:512].rearrange("p (c t) -> p c t", c=4), func=AF.Copy))
                               if half == 0 else
                               (lambda e: e.tensor_copy(out=z[:, half * 4:(half + 1) * 4, j::8], in_=t16[:, 0:512].rearrange("p (c t) -> p c t", c=4))),
                               deps=[ev_t] + zT_free[qt % 2])
                    ps["free"][ti] = [evz]
            z_evs = [(P.act.sem, P.act.sem.v), (P.dve.sem, P.dve.sem.v)]
            last_mm = None
            for ns in range(2):
                t0 = qt * 1024 + ns * 512
                for dc in range(8):
                    bv, frv, iv = self.bank_next(ps)
                    bg, frg, ig = self.bank_next(ps)
                    for bank, off, fr in ((bv, 0, frv), (bg, 1024, frg)):
                        for cc in range(8):
                            f = lambda e, bank=bank, off=off, cc=cc: e.matmul(bank[:], lhsT=Wg[:, cc, off + dc * 128:off + (dc + 1) * 128],
                                                                             rhs=z[:, cc, ns * 512:(ns + 1) * 512], start=(cc == 0), stop=(cc == 7))
                            if cc < 7:
                                P.op_nosig(pe, f, deps=(z_evs + [ev_w] + fr) if cc == 0 else ())
                            else:
                                ev_mm = P.op(pe, f)
                    last_mm = ev_mm
                    si = ui % 2
                    ui += 1
                    ev_s = P.op(act, lambda e: e.activation(out=sg[si][:], in_=bg[:], func=AF.Sigmoid), deps=[ev_mm] + sg_free[si])
                    ps["free"][ig] = [ev_s]
                    ev_m = P.op(dve, lambda e: e.tensor_tensor(out=gt[si][:], in0=bv[:], in1=sg[si][:], op=ALU.mult), deps=[ev_s, ev_mm])
                    ps["free"][iv] = [ev_m]
                    sx, xt, xsem, xfr = xc.next()
                    ev_ld = P.dma(sp, xsem, xt[:], xs[:, dc, t0:t0 + 512], deps=list(src_ready) + xfr)
                    ev_r = P.op(dve, lambda e: e.tensor_tensor(out=xt[:], in0=gt[si][:], in1=xt[:], op=ALU.add), deps=[ev_m, ev_ld])
                    sg_free[si] = [ev_r]
                    ev_st = P.dma(sp, st_sems[sx], xs[:, dc, t0:t0 + 512], xt[:], deps=[ev_r])
                    xc.free[sx] = [ev_st]
                    stores.append(ev_st)
            zT_free[qt % 2] = [last_mm]
        return stores[-4:]

    def odd_mixer(self, ps, src_ready):
        P = self.P
        NQ = S // 8
        reset = lambda: ps.__setitem__("free", [[] for _ in ps["banks"]])
        so = contextlib.ExitStack()
        small = {"r8p": P.sbuf([128, 32], F32, "r8p", so), "php": P.sbuf([128, 32], F32, "php", so)}
        with contextlib.ExitStack() as st:
            ev1 = self.odd_O1(st, ps, src_ready)
        P.barrier(ev1)
        reset()
        with contextlib.ExitStack() as st:
            ev2, small = self.odd_tables(st, ps, small)
        reset()
        with contextlib.ExitStack() as st0:
            Ztm = P.sbuf([128, S // 1024, 8, 1024], BF16, "Ztm", st0)
            with contextlib.ExitStack() as st:
                ev3 = self.odd_O3(st, ps, small, ev1 + ev2, Ztm)
            P.barrier(ev3)
            reset()
            with contextlib.ExitStack() as st:
                ev4 = self.odd_O4(st, ps, Ztm, ev3, src_ready)
            P.barrier(ev4)
            reset()
        so.close()
        return ev4

    def build(self):
        P = self.P
        nc = self.nc
        cfg = self.cfg
        phases = cfg["phases"]
        self.setup_consts()
        self.eps_sb = P.sbuf([128, 1], F32, "eps")
        ev = P.op(P.pool, lambda e: e.memset(self.eps_sb[:], EPS))
        self.const_ev.append(ev)
        self.pi2_sb = P.sbuf([128, 1], F32, "pi2")
        ev = P.op(P.pool, lambda e: e.memset(self.pi2_sb[:], math.pi / 2))
        self.const_ev.append(ev)
        if "even" in phases:
            self.decl_even()
        if "odd" in phases:
            self.decl_odd()
        for ph in phases:
            if ph.startswith("ffn"):
                self.convert_ffn(int(ph[3:]))
            elif ph == "even":
                self.convert_even()
            elif ph == "odd":
                self.convert_odd()
        banks = [P.psum([128, 512], F32, "bank") for _ in range(8)]
        psf = {"units": [(banks[0], banks[1]), (banks[2], banks[3]), (banks[4], banks[5])], "misc": [banks[6], banks[7]]}
        psm = {"banks": banks, "free": [[] for _ in banks], "i": 0}
        ready = []
        src = self.x_in
        for ph in phases:
            P.begin_phase()
            if ph.startswith("ffn"):
                i = int(ph[3:])
                with contextlib.ExitStack() as st:
                    b = self.alloc_ffn(st)
                    ready = self.ffn(b, i, i, src, self.xout, ready, psf)
                P.barrier(ready)
            elif ph == "copy":
                csem = P.newsem("copy")
                ready = [P.dma(P.sp, csem, self.xout, self.x_in)]
            elif ph == "even":
                ready = self.even_mixer(psm, ready)
            elif ph == "odd":
                ready = self.odd_mixer(psm, ready)
            P.barrier(ready)
            ready = []
            P.end_phase()
            src = self.xout
        P.es.close()
        return nc


POOL_WINDOWS = (2, 4, 8, 16)


def prep_shared(inp, phases):
    m = {}
    g = [inp["ffn_norm"][l, s] for l in range(2) for s in range(2)] + [inp["mix_norm"][0], inp["mix_norm"][1]]
    m["gains"] = np.ascontiguousarray(np.stack([_lay_gain(np.asarray(x, np.float32)) for x in g], axis=1))
    wg, wu, wd = inp["ffn_w_gate"], inp["ffn_w_up"], inp["ffn_w_down"]
    m["wgu_f"] = np.stack([_lay_wgu(wg[l, s], wu[l, s]) for l in range(2) for s in range(2)])
    m["wd_f"] = np.stack([_lay_wd(wd[l, s]) for l in range(2) for s in range(2)])
    if "even" in phases:
        m["win_f"] = np.ascontiguousarray(inp["ev_w_in"][0].reshape(8, 128, 2048).transpose(1, 0, 2)).reshape(1024, 2048)
        m["wout_f"] = np.ascontiguousarray(inp["ev_w_out"][0].reshape(8, 128, 1024).transpose(1, 0, 2)).reshape(512, 2048)
        m["poolw_f"] = np.ascontiguousarray(inp["ev_pool_w"][0].transpose(1, 0, 2)).reshape(128, 512)
        m["pscale"] = np.ascontiguousarray(inp["ev_pool_scale"][0].reshape(4, 128).T)
        m["qkn"] = np.ascontiguousarray(np.stack([inp["ev_q_norm"][0], inp["ev_k_norm"][0]]))
        cst = np.zeros((128, 72), np.float32)
        cst[:, 0:8] = (500000.0 ** (-np.arange(0, 16, 2, dtype=np.float32) / 16.0)).astype(np.float32)[None, :]
        for pc, w in enumerate(POOL_WINDOWS):
            cst[:, 8 + pc * 16:8 + (pc + 1) * 16] = (1.0 / np.minimum(np.arange(1, 17), w)).astype(np.float32)[None, :]
        m["cst"] = cst
    if "odd" in phases:
        f = lambda a: np.asarray(a, np.float32)
        a_re, a_im = f(inp["s5_a_re"][0]), f(inp["s5_a_im"][0])
        m["s5_a1"] = np.ascontiguousarray(np.stack([a_re.reshape(-1), a_im.reshape(-1)]))
        m["s5_a2"] = np.ascontiguousarray(np.stack([np.tile(a_re.T, (2, 1)), np.tile(a_im.T, (2, 1))], axis=1))
        m["s5_ldt"] = np.ascontiguousarray(f(inp["s5_log_dt"][0]))
        b_re, b_im = f(inp["s5_b_re"][0]), f(inp["s5_b_im"][0])
        lay1 = lambda b: np.tile(b.transpose(2, 0, 1).reshape(16, 4096), (8, 1))
        m["s5_b1"] = np.ascontiguousarray(np.stack([lay1(b_re), lay1(b_im)], axis=1))
        lay2 = lambda b: np.tile(b.transpose(1, 0, 2).reshape(64, 1024), (2, 1))
        m["s5_b2"] = np.ascontiguousarray(np.stack([lay2(b_re), lay2(b_im)], axis=1))
        c_re, c_im = f(inp["s5_c_re"][0]), f(inp["s5_c_im"][0])
        lay3 = lambda c: np.tile(c.transpose(2, 0, 1).reshape(64, 1024), (2, 1))
        m["s5_c2"] = np.ascontiguousarray(np.stack([lay3(c_re), lay3(c_im)], axis=1))
        m["s5_dv"] = np.ascontiguousarray(f(inp["s5_d"][0]))
        cst = np.zeros((128, 12), np.float32)
        ii = np.arange(128) // 16
        cst[:, 0] = 7 - ii
        cst[:, 1] = 8 - ii
        cst[:, 2] = (np.arange(128) < 64)
        cst[:, 3] = (np.arange(128) >= 64)
        for s_ in range(8):
            cst[:, 4 + s_] = (ii == s_)
        m["s5_cst"] = cst
        m["s5_iota"] = np.ascontiguousarray(np.tile(np.arange(512, dtype=np.float32)[None, :], (128, 1)))
        m["wglu_f"] = np.ascontiguousarray(f(inp["s5_w_glu"][0]).reshape(8, 128, 2048).transpose(1, 0, 2)).reshape(1024, 2048)
    return m


def prep_core(inp, b, phases):
    m = {"x_in": np.ascontiguousarray(np.asarray(inp["x"][b], np.float32).T)}
    if "even" in phases:
        m["pos"] = np.ascontiguousarray(np.asarray(inp["positions"][b], np.int32).reshape(-1, 128).T)
    return m


PHASES = ["ffn0", "even", "ffn1", "ffn2", "odd", "ffn3"]


_NC_CACHE = {}


def kernel(**inputs):
    inp = {k: np.asarray(v) for k, v in inputs.items()}
    phases = PHASES
    if "nc" not in _NC_CACHE:
        _NC_CACHE["nc"] = Builder({"phases": phases, "S": 4096}).build()
    nc = _NC_CACHE["nc"]
    shared = prep_shared(inp, phases)
    nb = inp["x"].shape[0]
    in_maps = []
    for b in range(nb):
        m = dict(shared)
        m.update(prep_core(inp, b, phases))
        in_maps.append(m)
    res = run_bass_kernel_spmd(nc, in_maps, core_ids=list(range(nb)))
    out = np.stack([np.ascontiguousarray(np.asarray(res.results[b]["xout"]).T) for b in range(nb)], axis=0)
    return out.astype(np.float32)
```

```python
import contextlib
import math
import numpy as np
import concourse.bass as bass
import concourse.mybir as mybir
from concourse.bass_utils import run_bass_kernel_spmd

F32 = mybir.dt.float32
BF16 = mybir.dt.bfloat16
I32 = mybir.dt.int32
AF = mybir.ActivationFunctionType
ALU = mybir.AluOpType

S = 4096
D = 1024
DFF = 2816
NFC = DFF // 128
EPS = 1e-6


class Sem:
    _n = 0

    def __init__(self, h):
        self.h = h
        Sem._n += 1
        self.k = Sem._n
        self.v = 0
        self.sw = False


class Eng:
    def __init__(self, prog, eng, name):
        self.e = eng
        self.sem = prog.newsem("e_" + name)
        self.waited = {}

    def wait(self, deps):
        for ev in deps:
            if ev is None:
                continue
            sem, val = ev
            if self.waited.get(sem.k, 0) >= val:
                continue
            self.e.wait_ge(sem.h, val)
            self.waited[sem.k] = val

    def done(self, ins):
        self.sem.v += 1
        ins.then_inc(self.sem.h, 1)
        return (self.sem, self.sem.v)


class Prog:
    def __init__(self):
        self.nc = bass.Bass("TRN2", target_bir_lowering=False)
        self.es = contextlib.ExitStack()
        self._nm = 0
        self.sem_pool = []
        self.sem_pool_sw = []
        self.phase_sems = None
        nc = self.nc
        self.pe = Eng(self, nc.tensor, "pe")
        self.act = Eng(self, nc.scalar, "act")
        self.dve = Eng(self, nc.vector, "dve")
        self.pool = Eng(self, nc.gpsimd, "pool")
        self.sp = Eng(self, nc.sync, "sp")

    def newsem(self, name, sw=False):
        pool_ = self.sem_pool_sw if sw else self.sem_pool
        if pool_:
            sm = pool_.pop()
        else:
            self._nm += 1
            sm = Sem(self.es.enter_context(self.nc.semaphore("%s_s%d" % (name, self._nm))))
            sm.sw = sw
        if self.phase_sems is not None:
            self.phase_sems.append(sm)
        return sm

    def begin_phase(self):
        self.phase_sems = []

    def end_phase(self):
        for sm in self.phase_sems:
            (self.sem_pool_sw if sm.sw else self.sem_pool).append(sm)
        self.phase_sems = None

    def name(self, p):
        self._nm += 1
        return "%s_%d" % (p, self._nm)

    def sbuf(self, shape, dt, name="t", stack=None):
        return (stack or self.es).enter_context(self.nc.sbuf_tensor(self.name(name), list(shape), dt))

    def psum(self, shape, dt, name="p", stack=None):
        return (stack or self.es).enter_context(self.nc.psum_tensor(self.name(name), list(shape), dt))

    def op(self, eng, fn, deps=()):
        eng.wait(deps)
        return eng.done(fn(eng.e))

    def op_nosig(self, eng, fn, deps=()):
        eng.wait(deps)
        fn(eng.e)

    def barrier(self, extra=()):
        evs = [(e.sem, e.sem.v) for e in (self.pe, self.act, self.dve, self.pool) if e.sem.v > 0] + list(extra)
        for e in (self.pe, self.act, self.dve, self.pool, self.sp):
            e.wait(evs)

    def dma(self, q, sem, out, in_, deps=(), **kw):
        q.wait(deps)
        sem.v += 16
        q.e.dma_start(out=out, in_=in_, **kw).then_inc(sem.h, 16)
        return (sem, sem.v)


class Ring:
    def __init__(self, prog, n, shape, dt, name, stack=None, sw=False):
        self.n = n
        self.bufs = [prog.sbuf(shape, dt, name, stack) for _ in range(n)]
        self.sems = [prog.newsem(name + "_ld%d" % i, sw) for i in range(n)]
        self.free = [[] for _ in range(n)]
        self.i = 0

    def next(self):
        s = self.i % self.n
        self.i += 1
        fr = self.free[s]
        self.free[s] = []
        return s, self.bufs[s], self.sems[s], fr


class Chain:
    def __init__(self, P):
        self.P = P
        self.last = None

    def _do(self, eng, fn, extra=()):
        self.last = self.P.op(eng, fn, deps=[self.last] + list(extra))
        return self.last

    def dve(self, fn, extra=()):
        return self._do(self.P.dve, fn, extra)

    def act(self, fn, extra=()):
        return self._do(self.P.act, fn, extra)

    def pool(self, fn, extra=()):
        return self._do(self.P.pool, fn, extra)

    def pe(self, fn, extra=()):
        return self._do(self.P.pe, fn, extra)


def _lay_wgu(wg, wu):
    a = wg.reshape(8, 128, NFC, 128).transpose(2, 1, 0, 3)
    b = wu.reshape(8, 128, NFC, 128).transpose(2, 1, 0, 3)
    return np.ascontiguousarray(np.stack([a, b], axis=2)).reshape(NFC, 128, 2048)


def _lay_wd(wd):
    return np.ascontiguousarray(wd.reshape(NFC, 128, 8, 128).transpose(2, 1, 0, 3)).reshape(8, 128, DFF)


def _lay_gain(g):
    return np.ascontiguousarray(g.reshape(8, 128).T)


class Builder:
    def __init__(self, cfg):
        self.cfg = cfg
        global S
        S = cfg.get("S", 4096)
        self.P = Prog()
        P = self.P
        nc = P.nc
        self.nc = nc
        dt = nc.dram_tensor
        self.x_in = dt("x_in", [D, S], F32, kind="ExternalInput").ap()
        self.xout = dt("xout", [D, S], F32, kind="ExternalOutput").ap()
        self.gains = dt("gains", [128, 6, 8], F32, kind="ExternalInput").ap()
        self.wgu_f = dt("wgu_f", [4, NFC, 128, 2048], F32, kind="ExternalInput").ap()
        self.wd_f = dt("wd_f", [4, 8, 128, DFF], F32, kind="ExternalInput").ap()
        self.wgu_b = dt("wgu_b", [4, NFC, 128, 2048], BF16, kind="Internal").ap()
        self.wd_b = dt("wd_b", [4, 8, 128, DFF], BF16, kind="Internal").ap()
        self.conv_ev = {}

    def setup_consts(self):
        P = self.P
        nc = self.nc
        self.ones_bf = P.sbuf([128, 128], BF16, "ones")
        self.gain_sb = P.sbuf([128, 6, 8], F32, "gain")
        self.sem_misc = P.newsem("misc")
        ev1 = P.op(P.pool, lambda e: e.memset(self.ones_bf[:], 1.0))
        ev2 = P.dma(P.sp, self.sem_misc, self.gain_sb[:], self.gains)
        self.const_ev = [ev1, ev2]

    def convert_ffn(self, i):
        P = self.P
        sem = P.newsem("cv%d" % i, sw=True)
        ev = None
        for fc in range(NFC):
            ev = P.dma(P.pool, sem, self.wgu_b[i, fc], self.wgu_f[i, fc])
        for dc in range(8):
            for h in range(2):
                ev = P.dma(P.pool, sem, self.wd_b[i, dc, :, h * 1408:(h + 1) * 1408],
                           self.wd_f[i, dc, :, h * 1408:(h + 1) * 1408])
        self.conv_ev[i] = ev

    def alloc_ffn(self, st):
        P = self.P
        b = {}
        b["xa"] = Ring(P, 2, [128, 8, 512], F32, "xa", st)
        b["sq"] = P.sbuf([128, 8, 512], BF16, "sq", st)
        b["rt"] = P.sbuf([128, 512], F32, "rt", st)
        b["rstd"] = [P.sbuf([128, 512], F32, "rstd", st) for _ in range(2)]
        b["hT"] = P.sbuf([128, 8, 1024], BF16, "hT", st)
        b["actT"] = P.sbuf([128, NFC, 1024], BF16, "actT", st)
        b["wgu"] = Ring(P, 4, [128, 2048], BF16, "wgu", st, sw=True)
        b["wd"] = Ring(P, 3, [128, DFF], BF16, "wd", st, sw=True)
        b["xc"] = Ring(P, 4, [128, 512], F32, "xc", st)
        b["sg"] = [P.sbuf([128, 512], F32, "sg", st) for _ in range(2)]
        b["st_sems"] = [P.newsem("ffn_st%d" % j) for j in range(4)]
        return b

    def ffn(self, b, i, gidx, x_src, x_dst, src_ready, ps):
        P = self.P
        nc = self.nc
        pe, act, dve, pool, sp = P.pe, P.act, P.dve, P.pool, P.sp
        NT = 1024
        ntile = S // NT
        xs = x_src.rearrange("(dc p) t -> p dc t", p=128)
        xd = x_dst.rearrange("(dc p) t -> p dc t", p=128)
        units = ps["units"]
        misc = ps["misc"]
        st = b.setdefault("state", {"unit_i": 0, "misc_i": 0, "unit_free": [[] for _ in units],
                                    "misc_free": [[] for _ in misc], "hT_free": [], "act_free": [],
                                    "sg_free": [[], []], "sg_i": 0, "rstd_free": [[], []], "sq_free": [], "rt_free": []})
        store_evs = []
        A_state = {}

        def stage_A1(k):
            t0 = k * NT
            res = []
            for ns in range(2):
                s, xa, sem, fr = b["xa"].next()
                ev_ld = P.dma(sp, sem, xa[:], xs[:, :, t0 + ns * 512:t0 + (ns + 1) * 512], deps=list(src_ready) + fr)
                ev_sq = P.op(act, lambda e: e.activation(out=b["sq"][:], in_=xa[:], func=AF.Square),
                             deps=[ev_ld] + st["sq_free"])
                mi = st["misc_i"] % 2
                st["misc_i"] += 1
                bank = misc[mi]
                deps = [ev_sq] + st["misc_free"][mi] + self.const_ev
                for dc in range(8):
                    f = lambda e, dc=dc: e.matmul(bank[:], lhsT=self.ones_bf[:], rhs=b["sq"][:, dc, :],
                                                  start=(dc == 0), stop=(dc == 7))
                    if dc < 7:
                        P.op_nosig(pe, f, deps=deps if dc == 0 else ())
                    else:
                        ev_mm = P.op(pe, f)
                st["sq_free"] = [ev_mm]
                ev_rt = P.op(act, lambda e: e.activation(out=b["rt"][:], in_=bank[:], func=AF.Sqrt,
                                                         scale=1.0 / D, bias=self.eps_sb[:]),
                             deps=[ev_mm] + st["rt_free"])
                st["misc_free"][mi] = [ev_rt]
                rstd = b["rstd"][ns]
                ev_rs = P.op(dve, lambda e: e.reciprocal(out=rstd[:], in_=b["rt"][:]),
                             deps=[ev_rt] + st["rstd_free"][ns])
                st["rt_free"] = [ev_rs]
                res.append((s, xa, ev_ld, ev_rs))
            A_state[k] = res

        def stage_A2(k):
            evs = []
            for ns in range(2):
                s, xa, ev_ld, ev_rs = A_state[k][ns]
                rstd = b["rstd"][ns]
                for dc in range(8):
                    ev = P.op(dve, lambda e, dc=dc: e.scalar_tensor_tensor(
                        out=b["hT"][:, dc, ns * 512:(ns + 1) * 512], in0=xa[:, dc, :],
                        scalar=self.gain_sb[:, gidx, dc:dc + 1], in1=rstd[:],
                        op0=ALU.mult, op1=ALU.mult),
                        deps=[ev_ld, ev_rs] + st["hT_free"] + self.const_ev)
                evs.append(ev)
                b["xa"].free[s] = [ev]
                st["rstd_free"][ns] = [ev]
            A_state[k] = evs

        def stage_B(k, hook=None):
            hT_ready = A_state[k]
            last_mm = None
            act_evs = []
            for fc in range(NFC):
                if hook is not None and fc == 14:
                    hook()
                s, w, sem, fr = b["wgu"].next()
                ev_w = P.dma(pool, sem, w[:], self.wgu_b[i, fc], deps=fr + [self.conv_ev[i]])
                for ns in range(2):
                    ui = st["unit_i"] % len(units)
                    st["unit_i"] += 1
                    pg, pu = units[ui]
                    deps = [ev_w] + hT_ready + st["unit_free"][ui]
                    first = True
                    for half, bank in ((0, pg), (1, pu)):
                        for dc in range(8):
                            f = lambda e, half=half, bank=bank, dc=dc: e.matmul(
                                bank[:], lhsT=w[:, (half * 8 + dc) * 128:(half * 8 + dc + 1) * 128],
                                rhs=b["hT"][:, dc, ns * 512:(ns + 1) * 512], start=(dc == 0), stop=(dc == 7))
                            if dc < 7:
                                P.op_nosig(pe, f, deps=deps if first else ())
                            else:
                                ev = P.op(pe, f)
                            first = False
                        if half == 0:
                            ev_g = ev
                        else:
                            ev_u = ev
                    last_mm = ev_u
                    si = st["sg_i"] % 2
                    st["sg_i"] += 1
                    sg = b["sg"][si]
                    ev_s = P.op(act, lambda e: e.activation(out=sg[:], in_=pg[:], func=AF.Silu),
                                deps=[ev_g] + st["sg_free"][si])
                    ev_a = P.op(dve, lambda e: e.tensor_tensor(out=b["actT"][:, fc, ns * 512:(ns + 1) * 512],
                                                               in0=sg[:], in1=pu[:], op=ALU.mult),
                                deps=[ev_s, ev_u] + st["act_free"])
                    st["sg_free"][si] = [ev_a]
                    st["unit_free"][ui] = [ev_a]
                    act_evs.append(ev_a)
                b["wgu"].free[s] = [last_mm]
            st["hT_free"] = [last_mm]
            return act_evs

        def stage_C(k, act_evs):
            t0 = k * NT
            last_mm = None
            for dc in range(8):
                s, w, sem, fr = b["wd"].next()
                ev_w = None
                for h in range(2):
                    ev_w = P.dma(pool, sem, w[:, h * 1408:(h + 1) * 1408], self.wd_b[i, dc, :, h * 1408:(h + 1) * 1408],
                                 deps=fr + [self.conv_ev[i]])
                for ns in range(2):
                    mi = st["misc_i"] % 2
                    st["misc_i"] += 1
                    bank = misc[mi]
                    deps = [ev_w] + act_evs[-2:] + st["misc_free"][mi]
                    for fc in range(NFC):
                        f = lambda e, fc=fc: e.matmul(bank[:], lhsT=w[:, fc * 128:(fc + 1) * 128],
                                                      rhs=b["actT"][:, fc, ns * 512:(ns + 1) * 512],
                                                      start=(fc == 0), stop=(fc == NFC - 1))
                        if fc < NFC - 1:
                            P.op_nosig(pe, f, deps=deps if fc == 0 else ())
                        else:
                            ev_mm = P.op(pe, f)
                    last_mm = ev_mm
                    sx, xc, semx, frx = b["xc"].next()
                    ev_ld = P.dma(sp, semx, xc[:], xs[:, dc, t0 + ns * 512:t0 + (ns + 1) * 512],
                                  deps=list(src_ready) + frx)
                    ev_r = P.op(dve, lambda e: e.scalar_tensor_tensor(out=xc[:], in0=bank[:], scalar=0.5, in1=xc[:],
                                                                      op0=ALU.mult, op1=ALU.add),
                                deps=[ev_mm, ev_ld])
                    st["misc_free"][mi] = [ev_r]
                    ev_st = P.dma(sp, b["st_sems"][sx], xd[:, dc, t0 + ns * 512:t0 + (ns + 1) * 512], xc[:], deps=[ev_r])
                    b["xc"].free[sx] = [ev_st]
                    store_evs.append(ev_st)
                b["wd"].free[s] = [last_mm]
            st["act_free"] = [last_mm]

        stage_A1(0)
        stage_A2(0)
        for k in range(ntile):
            def nxt(k=k):
                if k == 0:
                    self.bg()
                if k + 1 < ntile:
                    stage_A1(k + 1)
            act_evs = stage_B(k, nxt)
            if k + 1 < ntile:
                stage_A2(k + 1)
            stage_C(k, act_evs)
        return store_evs[-4:]

    def decl_even(self):
        dt = self.nc.dram_tensor
        self.pos = dt("pos", [128, S // 128], I32, kind="ExternalInput").ap()
        self.win_f = dt("win_f", [1024, 2048], F32, kind="ExternalInput").ap()
        self.wout_f = dt("wout_f", [512, 2048], F32, kind="ExternalInput").ap()
        self.poolw_f = dt("poolw_f", [128, 512], F32, kind="ExternalInput").ap()
        self.pscale = dt("pscale", [128, 4], F32, kind="ExternalInput").ap()
        self.qkn = dt("qkn", [2, 64], F32, kind="ExternalInput").ap()
        self.cst = dt("cst", [128, 8 + 64], F32, kind="ExternalInput").ap()
        self.win_b = dt("win_b", [1024, 2048], BF16, kind="Internal").ap()
        self.wout_b = dt("wout_b", [512, 2048], BF16, kind="Internal").ap()
        self.poolw_b = dt("poolw_b", [128, 512], BF16, kind="Internal").ap()
        self.catT = dt("catT", [1024, S], BF16, kind="Internal").ap()
        self.vd = dt("vd", [S, 512], BF16, kind="Internal").ap()

    def convert_even(self):
        P = self.P
        sem = P.newsem("cv_even", sw=True)
        ev = None
        for j in range(8):
            ev = P.dma(P.pool, sem, self.win_b[j * 128:(j + 1) * 128], self.win_f[j * 128:(j + 1) * 128])
        for j in range(4):
            ev = P.dma(P.pool, sem, self.wout_b[j * 128:(j + 1) * 128], self.wout_f[j * 128:(j + 1) * 128])
        ev = P.dma(P.pool, sem, self.poolw_b, self.poolw_f)
        self.conv_ev["even"] = ev

    def norm_tile(self, nb, xs, t0, gidx, hT, src_ready, ps, hT_free, octet=None):
        P = self.P
        pe, act, dve, sp = P.pe, P.act, P.dve, P.sp
        s, xa, sem, fr = nb["xa"].next()
        ev_ld = P.dma(sp, sem, xa[:], xs[:, :, t0:t0 + 512], deps=list(src_ready) + fr)
        ev_sq = P.op(act, lambda e: e.activation(out=nb["sq"][:], in_=xa[:], func=AF.Square),
                     deps=[ev_ld] + nb["sq_free"])
        bank, bfree, bi = self.bank_next(ps)
        deps = [ev_sq] + bfree + self.const_ev
        for dc in range(8):
            f = lambda e, dc=dc: e.matmul(bank[:], lhsT=self.ones_bf[:], rhs=nb["sq"][:, dc, :],
                                          start=(dc == 0), stop=(dc == 7))
            if dc < 7:
                P.op_nosig(pe, f, deps=deps if dc == 0 else ())
            else:
                ev_mm = P.op(pe, f)
        nb["sq_free"] = [ev_mm]
        ev_rt = P.op(act, lambda e: e.activation(out=nb["rt"][:], in_=bank[:], func=AF.Sqrt,
                                                 scale=1.0 / D, bias=self.eps_sb[:]),
                     deps=[ev_mm] + nb["rt_free"])
        ps["free"][bi] = [ev_rt]
        ev_rs = P.op(dve, lambda e: e.reciprocal(out=nb["rstd"][:], in_=nb["rt"][:]),
                     deps=[ev_rt] + nb["rstd_free"])
        nb["rt_free"] = [ev_rs]
        for dc in range(8):
            if octet is None:
                o_, i0_, i1_ = hT[:, dc, :], xa[:, dc, :], nb["rstd"][:]
            else:
                o_ = hT[:, dc, :, octet * 64:(octet + 1) * 64].rearrange("p i q -> p q i")
                i0_ = xa[:, dc, :].rearrange("p (q i) -> p q i", i=8)
                i1_ = nb["rstd"][:].rearrange("p (q i) -> p q i", i=8)
            ev = P.op(dve, lambda e, dc=dc: e.scalar_tensor_tensor(
                out=o_, in0=i0_, scalar=self.gain_sb[:, gidx, dc:dc + 1], in1=i1_,
                op0=ALU.mult, op1=ALU.mult), deps=[ev_ld, ev_rs] + hT_free + self.const_ev)
        nb["xa"].free[s] = [ev]
        nb["rstd_free"] = [ev]
        return ev

    def alloc_norm(self, st):
        P = self.P
        return {"xa": Ring(P, 2, [128, 8, 512], F32, "nxa", st), "sq": P.sbuf([128, 8, 512], BF16, "nsq", st),
                "rt": P.sbuf([128, 512], F32, "nrt", st), "rstd": P.sbuf([128, 512], F32, "nrstd", st),
                "sq_free": [], "rt_free": [], "rstd_free": []}

    def bank_next(self, ps):
        bi = ps["i"] % len(ps["banks"])
        ps["i"] += 1
        fr = ps["free"][bi]
        ps["free"][bi] = []
        return ps["banks"][bi], fr, bi

    def even_consts(self, st):
        P = self.P
        pool, dve, act, sp = P.pool, P.dve, P.act, P.sp
        c = {}
        nblk = S // 128
        c["ident"] = P.sbuf([128, 128], BF16, "ident", st)
        c["mask"] = P.sbuf([128, 256], BF16, "mask", st)
        c["ones64"] = P.sbuf([128, 64], BF16, "ones64", st)
        c["gqk"] = P.sbuf([128, 2, 64], F32, "gqk", st)
        c["pscale"] = P.sbuf([128, 4], F32, "pscale", st)
        c["cst"] = P.sbuf([128, 72], F32, "cst", st)
        c["cos"] = P.sbuf([128, nblk, 8], F32, "cos", st)
        c["sin"] = P.sbuf([128, nblk, 8], F32, "sin", st)
        posi = P.sbuf([128, nblk], I32, "posi", st)
        posf = P.sbuf([128, nblk], F32, "posf", st)
        ang = P.sbuf([128, nblk, 8], F32, "ang", st)
        kf = P.sbuf([128, nblk, 8], F32, "kf", st)
        ki = P.sbuf([128, nblk, 8], I32, "ki", st)
        red = P.sbuf([128, nblk, 8], F32, "red", st)
        sem = P.newsem("ec")
        evs = []
        e1 = P.op(pool, lambda e: e.memset(c["ident"][:], 1.0))
        e1 = P.op(pool, lambda e: e.affine_select(out=c["ident"][:], in_=c["ident"][:], pattern=[[-1, 128]],
                                                  compare_op=ALU.is_equal, fill=0.0, base=0, channel_multiplier=1), deps=[e1])
        evs.append(e1)
        e2 = P.op(pool, lambda e: e.memset(c["mask"][:], 1.0))
        e3 = P.op(pool, lambda e: e.affine_select(out=c["mask"][:, 0:128], in_=c["mask"][:, 0:128], pattern=[[-1, 128]],
                                                  compare_op=ALU.is_ge, fill=0.0, base=0, channel_multiplier=1), deps=[e2])
        e4 = P.op(pool, lambda e: e.affine_select(out=c["mask"][:, 128:256], in_=c["mask"][:, 128:256], pattern=[[1, 128]],
                                                  compare_op=ALU.is_ge, fill=0.0, base=0, channel_multiplier=-1), deps=[e2])
        evs += [e3, e4]
        evs.append(P.op(pool, lambda e: e.memset(c["ones64"][:], 1.0)))
        d1 = P.dma(sp, sem, c["gqk"][:, 0, :], self.qkn[0].partition_broadcast(128))
        d1 = P.dma(sp, sem, c["gqk"][:, 1, :], self.qkn[1].partition_broadcast(128))
        d1 = P.dma(sp, sem, c["pscale"][:], self.pscale)
        d1 = P.dma(sp, sem, c["cst"][:], self.cst)
        d1 = P.dma(sp, sem, posi[:], self.pos)
        evs.append(d1)
        ev = P.op(dve, lambda e: e.tensor_copy(out=posf[:], in_=posi[:]), deps=[d1])
        ev = P.op(dve, lambda e: e.tensor_tensor(out=ang[:], in0=posf[:].unsqueeze(2).to_broadcast([128, nblk, 8]),
                                                 in1=c["cst"][:, 0:8].unsqueeze(1).to_broadcast([128, nblk, 8]), op=ALU.mult), deps=[ev])
        C1 = 6.28125
        C2 = 2.0 * math.pi - C1
        for which, shift in (("sin", 0.0), ("cos", math.pi / 2)):
            ev = P.op(dve, lambda e: e.tensor_scalar(out=kf[:], in0=ang[:], scalar1=shift, scalar2=1.0 / (2 * math.pi),
                                                     op0=ALU.add, op1=ALU.mult), deps=[ev])
            ev = P.op(dve, lambda e: e.tensor_copy(out=ki[:], in_=kf[:]), deps=[ev])
            ev = P.op(dve, lambda e: e.tensor_copy(out=kf[:], in_=ki[:]), deps=[ev])
            ev = P.op(dve, lambda e: e.scalar_tensor_tensor(out=red[:], in0=kf[:], scalar=-C1, in1=ang[:],
                                                            op0=ALU.mult, op1=ALU.add), deps=[ev])
            ev = P.op(dve, lambda e: e.scalar_tensor_tensor(out=red[:], in0=kf[:], scalar=-C2, in1=red[:],
                                                            op0=ALU.mult, op1=ALU.add), deps=[ev])
            ev = P.op(dve, lambda e: e.tensor_scalar(out=red[:], in0=red[:], scalar1=shift, scalar2=None, op0=ALU.add), deps=[ev])
            ev = P.op(dve, lambda e: e.tensor_scalar(out=kf[:], in0=red[:], scalar1=math.pi, scalar2=-2 * math.pi, op0=ALU.is_gt, op1=ALU.mult), deps=[ev])
            ev = P.op(dve, lambda e: e.tensor_tensor(out=red[:], in0=red[:], in1=kf[:], op=ALU.add), deps=[ev])
            ev = P.op(dve, lambda e: e.tensor_scalar(out=kf[:], in0=red[:], scalar1=-math.pi, scalar2=2 * math.pi, op0=ALU.is_lt, op1=ALU.mult), deps=[ev])
            ev = P.op(dve, lambda e: e.tensor_tensor(out=red[:], in0=red[:], in1=kf[:], op=ALU.add), deps=[ev])
            ev = P.op(act, lambda e, which=which: e.activation(out=c[which][:], in_=red[:], func=AF.Sin), deps=[ev])
        evs.append(ev)
        c["evs"] = evs
        return c

    def even_E1(self, st, c, qT, kT, ps, src_ready):
        P = self.P
        pe, act, dve, pool, sp = P.pe, P.act, P.dve, P.pool, P.sp
        xs = self.xout.rearrange("(dc p) t -> p dc t", p=128)
        nb = self.alloc_norm(st)
        Win = P.sbuf([128, 8, 2048], BF16, "Win", st)
        poolw = P.sbuf([128, 4, 128], BF16, "poolw", st)
        hT = P.sbuf([128, 8, 512], BF16, "ehT", st)
        lanes = []
        for _l in range(4):
            lanes.append({"sqt": P.sbuf([128, 512], F32, "sqt", st), "ss": P.sbuf([128, 8], F32, "ss", st),
                          "rt8": P.sbuf([128, 8], F32, "rt8", st), "rs8": P.sbuf([128, 8], F32, "rs8", st),
                          "qn": P.sbuf([128, 8, 64], F32, "qn", st), "tA": P.sbuf([128, 8, 8], F32, "tA", st),
                          "tB": P.sbuf([128, 8, 8], F32, "tB", st),
                          "qb": [P.sbuf([128, 512], BF16, "qb", st) for _ in range(2)],
                          "qb_free": [[], []], "qbi": 0, "free": []})
        vb = Ring(P, 2, [128, 512], BF16, "vb", st)
        pbuf = P.sbuf([128, 4, 528], F32, "pbuf", st)
        sA = P.sbuf([128, 528], F32, "sA", st)
        sB = P.sbuf([128, 528], F32, "sB", st)
        pooled = P.sbuf([128, 512], BF16, "pooled", st)
        tmp16 = P.sbuf([128, 16], F32, "tmp16", st)
        pooled_free = []
        catp = Ring(P, 2, [128, 512], BF16, "catp", st)
        sem = P.newsem("e1w", sw=True)
        ev_w = P.dma(pool, sem, Win[:], self.win_b.rearrange("(p dc) f -> p dc f", dc=8), deps=[self.conv_ev["even"]])
        ev_w = P.dma(pool, sem, poolw[:], self.poolw_b.rearrange("c (g e) -> c g e", g=4), deps=[self.conv_ev["even"]])
        ev0 = P.op(dve, lambda e: e.memset(sA[:], 0.0))
        ev0 = P.op(dve, lambda e: e.memset(sB[:], 0.0))
        ev0 = P.op(dve, lambda e: e.memset(pbuf[:, :, 0:16], 0.0))
        cev = c["evs"]
        hT_free = []
        qb_free = [[], []]
        qbi = 0
        halo_ev = [ev0] * 4
        small_free = []
        stores = []
        ntile = S // 512
        windows = (2, 4, 8, 16)
        for n in range(ntile):
            t0 = n * 512
            ev_h = self.norm_tile(nb, xs, t0, 4, hT, src_ready, ps, hT_free)
            last_pe = None
            for tbp in range(0, 4, 2):
              gens = []
              for tb in (tbp, tbp + 1):
                blk = n * 4 + tb
                bk = {}
                for sec in range(3):
                    bank, bfree, bi = self.bank_next(ps)
                    deps = [ev_h, ev_w] + bfree
                    for dc in range(8):
                        f = lambda e, dc=dc: e.matmul(bank[:], lhsT=hT[:, dc, tb * 128:(tb + 1) * 128],
                                                      rhs=Win[:, dc, sec * 512:(sec + 1) * 512], start=(dc == 0), stop=(dc == 7))
                        if dc < 7:
                            P.op_nosig(pe, f, deps=deps if dc == 0 else ())
                        else:
                            ev_mm = P.op(pe, f)
                    last_pe = ev_mm
                    bk[sec] = (bank, bi, ev_mm)
                bank, bi, ev_mm = bk[2]
                s, vbt, vsem, vfr = vb.next()
                ev_v = P.op(act, lambda e: e.activation(out=vbt[:], in_=bank[:], func=AF.Copy), deps=[ev_mm] + vfr)
                ps["free"][bi] = [ev_v]
                ev_st = P.dma(sp, vsem, self.vd[blk * 128:(blk + 1) * 128, :], vbt[:], deps=[ev_v])
                vb.free[s] = [ev_st]
                stores.append(ev_st)

                def qk_chain(sec, bk=bk, blk=blk, tb=tb):
                    bank, bi, ev_mm = bk[sec]
                    L = lanes[(tb % 2) * 2 + sec]
                    sqt, ss, rt8, rs8, qn, tA, tB = L["sqt"], L["ss"], L["rt8"], L["rs8"], L["qn"], L["tA"], L["tB"]
                    bank3 = bank[:].rearrange("p (h e) -> p h e", h=8)
                    ev = P.op(act, lambda e: e.activation(out=sqt[:], in_=bank[:], func=AF.Square), deps=[ev_mm] + L["free"])
                    yield
                    ev = P.op(dve, lambda e: e.tensor_reduce(out=ss[:], in_=sqt[:].rearrange("p (h e) -> p h e", h=8),
                                                             axis=mybir.AxisListType.X, op=ALU.add), deps=[ev] + L["free"])
                    yield
                    ev = P.op(act, lambda e: e.activation(out=rt8[:], in_=ss[:], func=AF.Sqrt, scale=1.0 / 64, bias=self.eps_sb[:]),
                              deps=[ev])
                    yield
                    ev = P.op(dve, lambda e: e.reciprocal(out=rs8[:], in_=rt8[:]), deps=[ev])
                    yield
                    ev = P.op(dve, lambda e: e.tensor_tensor(out=qn[:], in0=bank3, in1=rs8[:].unsqueeze(2).to_broadcast([128, 8, 64]),
                                                             op=ALU.mult), deps=[ev])
                    ps["free"][bi] = [ev]
                    yield
                    ev = P.op(dve, lambda e: e.tensor_tensor(out=qn[:], in0=qn[:],
                                                             in1=c["gqk"][:, sec, :].unsqueeze(1).to_broadcast([128, 8, 64]),
                                                             op=ALU.mult), deps=[ev] + cev)
                    yield
                    q_b = L["qb"][L["qbi"] % 2]
                    qfr = L["qb_free"][L["qbi"] % 2]
                    ev_c = P.op(act, lambda e: e.activation(out=q_b[:], in_=qn[:].rearrange("p h e -> p (h e)"), func=AF.Copy),
                                deps=[ev] + qfr)
                    yield
                    q3 = q_b[:].rearrange("p (h e) -> p h e", h=8)
                    cosb = c["cos"][:, blk, :].unsqueeze(1).to_broadcast([128, 8, 8])
                    sinb = c["sin"][:, blk, :].unsqueeze(1).to_broadcast([128, 8, 8])
                    t1 = qn[:, :, 0:8]
                    t2 = qn[:, :, 8:16]
                    e_a = P.op(dve, lambda e: e.tensor_tensor(out=tA[:], in0=t1, in1=cosb, op=ALU.mult), deps=[ev])
                    e_b = P.op(dve, lambda e: e.tensor_tensor(out=tB[:], in0=t2, in1=sinb, op=ALU.mult), deps=[ev])
                    yield
                    e_r1 = P.op(dve, lambda e: e.tensor_tensor(out=q3[:, :, 0:8], in0=tA[:], in1=tB[:], op=ALU.subtract),
                                deps=[e_a, e_b, ev_c])
                    yield
                    e_a = P.op(dve, lambda e: e.tensor_tensor(out=tA[:], in0=t2, in1=cosb, op=ALU.mult), deps=[e_r1])
                    e_b = P.op(dve, lambda e: e.tensor_tensor(out=tB[:], in0=t1, in1=sinb, op=ALU.mult), deps=[e_r1])
                    yield
                    e_r2 = P.op(dve, lambda e: e.tensor_tensor(out=q3[:, :, 8:16], in0=tA[:], in1=tB[:], op=ALU.add),
                                deps=[e_a, e_b])
                    L["free"] = [e_r2]
                    yield
                    tbank, tfree, ti = self.bank_next(ps)
                    tb16 = tbank[:].bitcast(BF16)
                    for cc in range(4):
                        f = lambda e, cc=cc: e.transpose(tb16[:, cc * 128:(cc + 1) * 128], q_b[:, cc * 128:(cc + 1) * 128], c["ident"][:])
                        if cc < 3:
                            P.op_nosig(pe, f, deps=([e_r2] + tfree + cev) if cc == 0 else ())
                        else:
                            ev_t = P.op(pe, f)
                    yield
                    dst = (qT if sec == 0 else kT)[:, :, blk * 128:(blk + 1) * 128]
                    ev_e = P.op(act, lambda e: e.activation(out=dst, in_=tb16[:, 0:512].rearrange("p (c t) -> p c t", c=4), func=AF.Copy),
                                deps=[ev_t])
                    ps["free"][ti] = [ev_e]
                    L["qb_free"][L["qbi"] % 2] = [ev_t]
                    L["qbi"] += 1

                gens += [qk_chain(0), qk_chain(1)]
              while gens:
                  for g_ in list(gens):
                      try:
                          next(g_)
                      except StopIteration:
                          gens.remove(g_)
            for pc in range(4):
                w = windows[pc]
                bank, bfree, bi = self.bank_next(ps)
                deps = [ev_h, ev_w] + bfree
                for dc in range(8):
                    f = lambda e, dc=dc: e.matmul(bank[:], lhsT=Win[:, dc, 1536 + pc * 128:1536 + (pc + 1) * 128],
                                                  rhs=hT[:, dc, :], start=(dc == 0), stop=(dc == 7))
                    if dc < 7:
                        P.op_nosig(pe, f, deps=deps if dc == 0 else ())
                    else:
                        ev_mm = P.op(pe, f)
                last_pe = ev_mm
                ev_p = P.op(act, lambda e: e.activation(out=pbuf[:, pc, 16:528], in_=bank[:], func=AF.Copy),
                            deps=[ev_mm, halo_ev[pc]])
                ps["free"][bi] = [ev_p]
                cur = pbuf[:, pc, :]
                ev = ev_p
                lvl = 1
                bufs2 = [sA, sB]
                k = 0
                while lvl < w:
                    dstb = bufs2[k % 2]
                    src = cur
                    ev = P.op(dve, lambda e, src=src, dstb=dstb, lvl=lvl: e.tensor_tensor(
                        out=dstb[:, lvl:528], in0=src[:, lvl:528], in1=src[:, 0:528 - lvl], op=ALU.add), deps=[ev])
                    cur = dstb
                    lvl *= 2
                    k += 1
                ev_pl = P.op(dve, lambda e, cur=cur: e.scalar_tensor_tensor(out=pooled[:], in0=cur[:, 16:528], scalar=1.0 / w,
                                                                           in1=pbuf[:, pc, 16:528], op0=ALU.mult, op1=ALU.subtract),
                             deps=[ev] + pooled_free)
                if n == 0:
                    ev = P.op(dve, lambda e, cur=cur: e.tensor_tensor(out=tmp16[:], in0=cur[:, 16:32],
                                                                     in1=c["cst"][:, 8 + pc * 16:8 + (pc + 1) * 16], op=ALU.mult), deps=[ev_pl] + cev)
                    ev_pl = P.op(dve, lambda e: e.tensor_tensor(out=pooled[:, 0:16], in0=tmp16[:], in1=pbuf[:, pc, 16:32], op=ALU.subtract), deps=[ev])
                halo_ev[pc] = P.op(dve, lambda e: e.tensor_copy(out=pbuf[:, pc, 0:16], in_=pbuf[:, pc, 512:528]), deps=[ev_pl])
                bank2, bfree2, bi2 = self.bank_next(ps)
                ev_m2 = P.op(pe, lambda e: e.matmul(bank2[:], lhsT=poolw[:, pc, :], rhs=pooled[:], start=True, stop=True),
                             deps=[ev_pl, ev_w] + bfree2)
                last_pe = ev_m2
                s, cp, csem, cfr = catp.next()
                ev_cp = P.op(act, lambda e: e.activation(out=cp[:], in_=bank2[:], func=AF.Identity, scale=c["pscale"][:, pc:pc + 1]),
                             deps=[ev_m2] + cfr + cev)
                ps["free"][bi2] = [ev_cp]
                ev_st = P.dma(sp, csem, self.catT[512 + pc * 128:512 + (pc + 1) * 128, t0:t0 + 512], cp[:], deps=[ev_cp])
                catp.free[s] = [ev_st]
                stores.append(ev_st)
                pooled_free = [ev_m2]
            hT_free = [last_pe]
        return stores

    def even_E2(self, st, c, qT, kT, ps, vd_ready):
        P = self.P
        pe, act, dve, pool, sp = P.pe, P.act, P.dve, P.pool, P.sp
        nblk = S // 128
        acc = P.sbuf([128, 2, S], F32, "acc", st)
        rcp = P.sbuf([128, S], F32, "rcp", st)
        att = Ring(P, 2, [128, S], BF16, "att", st)
        vring = Ring(P, 2, [128, nblk, 128], BF16, "vring", st, sw=True)
        NPT = 3
        pt = [[P.sbuf([128, 256], BF16, "pt", st) for _ in range(2)] for _ in range(NPT)]
        pt_free = [[[], []] for _ in range(NPT)]
        cev = c["evs"]
        stores = []
        patterns = (1, 4, 16)
        units = []
        for hp in range(4):
            for di, d in enumerate(patterns):
                nb = (S // d) // 128
                for r in range(d):
                    for b in range(nb):
                        units.append((hp, di, d, nb, r, b))
        vstate = {}
        state = {"acc_ev": None, "att_last": [], "last_uz": {}}

        def stage1(u):
            hp, di, d, nb, r, b = units[u]
            if (hp, di) not in vstate:
                s, vt, vsem, vfr = vring.next()
                src = self.vd[:, hp * 128:(hp + 1) * 128].rearrange("(b i r) f -> i r b f", i=128, r=d)
                ev_v = None
                for rr in range(d):
                    ev_v = P.dma(pool, vsem, vt[:, rr * nb:(rr + 1) * nb, :], src[:, rr], deps=list(vd_ready) + vfr)
                vstate[(hp, di)] = (s, vt, ev_v)
            slot = u % NPT
            q0 = r + d * 128 * b
            qsl = slice(q0, q0 + d * 127 + 1, d)
            ksl_prev = slice(q0 - d * 128, q0 - d * 128 + d * 127 + 1, d)
            bankA, frA, iA = self.bank_next(ps)
            bankB, frB, iB = self.bank_next(ps)
            evS = []
            for h, bank, fr in ((0, bankA, frA), (1, bankB, frB)):
                rows = slice(h * 64, (h + 1) * 64)
                if b > 0:
                    P.op_nosig(pe, lambda e: e.matmul(bank[:, 0:128], lhsT=kT[rows, hp, ksl_prev], rhs=qT[rows, hp, qsl], start=True, stop=True), deps=fr)
                evS.append(P.op(pe, lambda e: e.matmul(bank[:, 128:256], lhsT=kT[rows, hp, qsl], rhs=qT[rows, hp, qsl], start=True, stop=True), deps=fr))
            lo = 0 if b > 0 else 128
            evP = []
            for h, bank, bi in ((0, bankA, iA), (1, bankB, iB)):
                p_t = pt[slot][h]
                ev = P.op(act, lambda e: e.activation(out=p_t[:, lo:256], in_=bank[:, lo:256], func=AF.Exp, scale=0.125),
                          deps=[evS[h]] + pt_free[slot][h])
                ps["free"][bi] = [ev]
                eng = dve if h == 0 else pool
                ev = P.op(eng, lambda e: e.tensor_tensor(out=p_t[:, lo:256], in0=p_t[:, lo:256], in1=c["mask"][:, lo:256], op=ALU.mult),
                          deps=[ev] + cev)
                evP.append(ev)
            return evP

        def stage2(u, evP):
            hp, di, d, nb, r, b = units[u]
            s, vt, ev_v = vstate[(hp, di)]
            slot = u % NPT
            q0 = r + d * 128 * b
            qsl = slice(q0, q0 + d * 127 + 1, d)
            bankU, frU, iU = self.bank_next(ps)
            blk = r * nb + b
            first = True
            ev_uz = None
            for h in range(2):
                p_t = pt[slot][h]
                osl = slice(h * 64, (h + 1) * 64)
                vcols = slice(h * 64, (h + 1) * 64)
                for kind in range(2):
                    col = slice(kind * 128, (kind + 1) * 128)
                    dps = (frU + [ev_v] + evP + cev) if first else ()
                    if b > 0:
                        lt = vt[:, blk - 1, vcols] if kind == 0 else c["ones64"][:]
                        P.op_nosig(pe, lambda e: e.matmul(bankU[osl, col], lhsT=lt, rhs=p_t[:, 0:128], start=True, stop=False), deps=dps)
                        dps = ()
                    lt = vt[:, blk, vcols] if kind == 0 else c["ones64"][:]
                    ev_uz = P.op(pe, lambda e: e.matmul(bankU[osl, col], lhsT=lt, rhs=p_t[:, 128:256], start=(b == 0), stop=True), deps=dps)
                    first = False
                pt_free[slot][h] = [ev_uz]
            a_view = acc[:, :, qsl]
            u_view = bankU[:, 0:256].rearrange("p (k q) -> p k q", k=2)
            if di == 0:
                acc_ev = P.op(dve, lambda e: e.tensor_copy(out=a_view, in_=u_view), deps=[ev_uz] + state["att_last"])
            else:
                acc_ev = P.op(dve, lambda e: e.tensor_tensor(out=a_view, in0=a_view, in1=u_view, op=ALU.add), deps=[ev_uz])
            ps["free"][iU] = [acc_ev]
            last_in_group = (u + 1 == len(units)) or (units[u + 1][0:2] != (hp, di))
            if last_in_group:
                vring.free[s] = [ev_uz]
            last_in_hp = (u + 1 == len(units)) or (units[u + 1][0] != hp)
            if last_in_hp:
                ev = P.op(dve, lambda e: e.reciprocal(out=rcp[:], in_=acc[:, 1, :]), deps=[acc_ev])
                sa, at, asem, afr = att.next()
                ev = P.op(dve, lambda e: e.tensor_tensor(out=at[:], in0=acc[:, 0, :], in1=rcp[:], op=ALU.mult), deps=[ev] + afr)
                state["att_last"] = [ev]
                ev_st = P.dma(sp, asem, self.catT[hp * 128:(hp + 1) * 128, :], at[:], deps=[ev])
                att.free[sa] = [ev_st]
                stores.append(ev_st)

        pend = stage1(0)
        for u in range(len(units)):
            nxt = stage1(u + 1) if u + 1 < len(units) else None
            stage2(u, pend)
            pend = nxt
        return stores

    def even_E3(self, st, ps, cat_ready, src_ready):
        P = self.P
        pe, act, dve, pool, sp = P.pe, P.act, P.dve, P.pool, P.sp
        xs = self.xout.rearrange("(dc p) t -> p dc t", p=128)
        Wout = P.sbuf([128, 8, 1024], BF16, "Wout", st)
        cat = Ring(P, 2, [128, 8, 512], BF16, "cat", st)
        xc = Ring(P, 8, [128, 512], F32, "e3xc", st)
        st_sems = [P.newsem("e3st%d" % j) for j in range(8)]
        sem = P.newsem("e3w", sw=True)
        ev_w = P.dma(pool, sem, Wout[:], self.wout_b.rearrange("(p a) (b f) -> p (a b) f", p=128, b=2), deps=[self.conv_ev["even"]])
        catv = self.catT.rearrange("(c p) t -> p c t", p=128)
        stores = []
        for n in range(S // 512):
            t0 = n * 512
            s, ct, csem, cfr = cat.next()
            ev_c = P.dma(sp, csem, ct[:], catv[:, :, t0:t0 + 512], deps=list(cat_ready) + cfr)
            last = None
            pre = []
            for dc in range(8):
                sx, xt, xsem, xfr = xc.next()
                pre.append((sx, xt, P.dma(sp, xsem, xt[:], xs[:, dc, t0:t0 + 512], deps=list(src_ready) + xfr)))
            for dc in range(8):
                bank, bfree, bi = self.bank_next(ps)
                deps = [ev_c, ev_w] + bfree
                for cc in range(8):
                    f = lambda e, cc=cc: e.matmul(bank[:], lhsT=Wout[:, cc, dc * 128:(dc + 1) * 128], rhs=ct[:, cc, :],
                                                  start=(cc == 0), stop=(cc == 7))
                    if cc < 7:
                        P.op_nosig(pe, f, deps=deps if cc == 0 else ())
                    else:
                        ev_mm = P.op(pe, f)
                last = ev_mm
                sx, xt, ev_ld = pre[dc]
                ev_r = P.op(dve, lambda e: e.tensor_tensor(out=xt[:], in0=bank[:], in1=xt[:], op=ALU.add), deps=[ev_mm, ev_ld])
                ps["free"][bi] = [ev_r]
                ev_st = P.dma(sp, st_sems[sx], xs[:, dc, t0:t0 + 512], xt[:], deps=[ev_r])
                xc.free[sx] = [ev_st]
                stores.append(ev_st)
            cat.free[s] = [last]
        return stores[-8:]

    def even_mixer(self, ps, src_ready):
        P = self.P
        with contextlib.ExitStack() as st0:
            c = self.even_consts(st0)
            qT = P.sbuf([128, 4, S], BF16, "qT", st0)
            kT = P.sbuf([128, 4, S], BF16, "kT", st0)
            with contextlib.ExitStack() as st1:
                ev1 = self.even_E1(st1, c, qT, kT, ps, src_ready)
            P.barrier(ev1)
            ps["free"] = [[] for _ in ps["banks"]]
            self.bg()
            with contextlib.ExitStack() as st2:
                ev2 = self.even_E2(st2, c, qT, kT, ps, ev1)
            P.barrier(ev2)
            ps["free"] = [[] for _ in ps["banks"]]
        with contextlib.ExitStack() as st3:
            ev3 = self.even_E3(st3, ps, ev1 + ev2, src_ready)
        P.barrier(ev3)
        ps["free"] = [[] for _ in ps["banks"]]
        return ev3

    def decl_odd(self):
        dt = self.nc.dram_tensor
        NQ = S // 8
        self.s5_a1 = dt("s5_a1", [2, 4096], F32, kind="ExternalInput").ap()
        self.s5_a2 = dt("s5_a2", [128, 2, 64], F32, kind="ExternalInput").ap()
        self.s5_ldt = dt("s5_ldt", [64], F32, kind="ExternalInput").ap()
        self.s5_b1 = dt("s5_b1", [128, 2, 4096], F32, kind="ExternalInput").ap()
        self.s5_b2 = dt("s5_b2", [128, 2, 1024], F32, kind="ExternalInput").ap()
        self.s5_c2 = dt("s5_c2", [128, 2, 1024], F32, kind="ExternalInput").ap()
        self.s5_dv = dt("s5_dv", [1024], F32, kind="ExternalInput").ap()
        self.s5_cst = dt("s5_cst", [128, 12], F32, kind="ExternalInput").ap()
        self.s5_iota = dt("s5_iota", [128, 512], F32, kind="ExternalInput").ap()
        self.wglu_f = dt("wglu_f", [1024, 2048], F32, kind="ExternalInput").ap()
        self.wglu_b = dt("wglu_b", [1024, 2048], BF16, kind="Internal").ap()
        self.Ud = dt("Ud", [64, 128, NQ], BF16, kind="Internal").ap()
        self.Tabd = dt("Tabd", [32, 128, 5, 256], BF16, kind="Internal").ap()

    def convert_odd(self):
        P = self.P
        sem = P.newsem("cv_odd", sw=True)
        ev = None
        for j in range(8):
            ev = P.dma(P.pool, sem, self.wglu_b[j * 128:(j + 1) * 128], self.wglu_f[j * 128:(j + 1) * 128])
        self.conv_ev["odd"] = ev

    def trig(self, ch, ang, kf, ki, red, out_sin, out_cos):
        C1 = 6.28125
        C2 = 2.0 * math.pi - C1
        ch.dve(lambda e: e.tensor_scalar(out=red, in0=ang, scalar1=1.0 / (2 * math.pi), scalar2=None, op0=ALU.mult))
        ch.dve(lambda e: e.tensor_copy(out=ki, in_=red))
        ch.dve(lambda e: e.scalar_tensor_tensor(out=red, in0=ki, scalar=-C1, in1=ang, op0=ALU.mult, op1=ALU.add))
        ch.dve(lambda e: e.scalar_tensor_tensor(out=red, in0=ki, scalar=-C2, in1=red, op0=ALU.mult, op1=ALU.add))
        ev_r = ch.dve(lambda e: e.tensor_scalar(out=red, in0=red, scalar1=-math.pi, scalar2=math.pi, op0=ALU.max, op1=ALU.min))
        if out_sin is not None:
            ch.act(lambda e: e.activation(out=out_sin, in_=red, func=AF.Sin))
        if out_cos is not None:
            ch.dve(lambda e: e.scalar_tensor_tensor(out=kf, in0=red, scalar=-1.0, in1=red, op0=ALU.mult, op1=ALU.max), extra=[ev_r])
            ch.act(lambda e: e.activation(out=out_cos, in_=kf, func=AF.Sin, scale=-1.0, bias=self.pi2_sb[:]))

    def odd_tables(self, st0, ps, small):
        P = self.P
        stA = contextlib.ExitStack()
        st = stA
        pe, act, dve, pool, sp = P.pe, P.act, P.dve, P.pool, P.sp
        ch = Chain(P)
        sem = P.newsem("o2ld")
        f32 = lambda shape, nm: P.sbuf(shape, F32, nm, st)
        cst = P.sbuf([128, 12], F32, "ocst", st0)
        dt2 = P.sbuf([128, 64], F32, "dt2", st0)
        a2 = f32([128, 2, 64], "a2")
        ldt = f32([128, 64], "ldt")
        b2 = f32([128, 2, 1024], "b2")
        c2 = f32([128, 2, 1024], "c2")
        dv = f32([128, 1024], "dv")
        evs = [P.dma(sp, sem, cst[:], self.s5_cst), P.dma(sp, sem, a2[:], self.s5_a2),
               P.dma(sp, sem, ldt[:], self.s5_ldt.partition_broadcast(128)), P.dma(sp, sem, b2[:], self.s5_b2),
               P.dma(sp, sem, c2[:], self.s5_c2), P.dma(sp, sem, dv[:], self.s5_dv.partition_broadcast(128))]
        ch.last = evs[-1]
        ncol, n1col, mlo, mhi = cst[:, 0:1], cst[:, 1:2], cst[:, 2:3], cst[:, 3:4]
        T = [None, None] + [P.sbuf([128, 64, 128], BF16, "tab%d" % t, st) for t in range(2, 5)]
        lr2 = f32([128, 64], "lr2"); li2 = a2[:, 1, :]
        lrdt2 = f32([128, 64], "lrdt2"); th2 = f32([128, 64], "th2")
        Lre = f32([128, 64, 9], "Lre"); Lim = f32([128, 64, 9], "Lim")
        mag = f32([128, 64], "mag"); ang = f32([128, 64], "ang2"); kf = f32([128, 64], "kf2")
        ki = P.sbuf([128, 64], I32, "ki2", st); red = f32([128, 64], "red2")
        cs = f32([128, 64], "cs2"); sn = f32([128, 64], "sn2")
        ch.act(lambda e: e.activation(out=dt2[:], in_=ldt[:], func=AF.Exp))
        ch.dve(lambda e: e.tensor_scalar(out=lr2[:], in0=a2[:, 0, :], scalar1=-1e-4, scalar2=None, op0=ALU.min))
        ch.dve(lambda e: e.tensor_tensor(out=lrdt2[:], in0=lr2[:], in1=dt2[:], op=ALU.mult))
        ch.dve(lambda e: e.tensor_tensor(out=th2[:], in0=li2, in1=dt2[:], op=ALU.mult))
        for ex in range(9):
            ch.act(lambda e, ex=ex: e.activation(out=mag[:], in_=lrdt2[:], func=AF.Exp, scale=float(ex)))
            ch.dve(lambda e, ex=ex: e.tensor_scalar(out=ang[:], in0=th2[:], scalar1=float(ex), scalar2=None, op0=ALU.mult))
            self.trig(ch, ang[:], kf[:], ki[:], red[:], sn[:], cs[:])
            ch.dve(lambda e, ex=ex: e.tensor_tensor(out=Lre[:, :, ex], in0=mag[:], in1=cs[:], op=ALU.mult))
            ch.dve(lambda e, ex=ex: e.tensor_tensor(out=Lim[:, :, ex], in0=mag[:], in1=sn[:], op=ALU.mult))
            if ex == 8:
                ch.dve(lambda e: e.tensor_scalar(out=kf[:, 0:32], in0=mag[:, 0::2], scalar1=mlo, scalar2=None, op0=ALU.mult))
                ch.dve(lambda e: e.scalar_tensor_tensor(out=small["r8p"][:], in0=mag[:, 1::2], scalar=mhi, in1=kf[:, 0:32], op0=ALU.mult, op1=ALU.add))
                ch.dve(lambda e: e.tensor_scalar(out=kf[:, 0:32], in0=ang[:, 0::2], scalar1=mlo, scalar2=None, op0=ALU.mult))
                ch.dve(lambda e: e.scalar_tensor_tensor(out=small["php"][:], in0=ang[:, 1::2], scalar=mhi, in1=kf[:, 0:32], op0=ALU.mult, op1=ALU.add))
        den = f32([128, 64], "den2"); cre = f32([128, 64], "cre2"); cim = f32([128, 64], "cim2"); nre = f32([128, 64], "nre2")
        t1 = f32([128, 64], "t12")
        ch.dve(lambda e: e.tensor_tensor(out=den[:], in0=lr2[:], in1=lr2[:], op=ALU.mult))
        ch.dve(lambda e: e.tensor_tensor(out=t1[:], in0=li2, in1=li2, op=ALU.mult))
        ch.dve(lambda e: e.tensor_tensor(out=den[:], in0=den[:], in1=t1[:], op=ALU.add))
        ch.dve(lambda e: e.reciprocal(out=den[:], in_=den[:]))
        ch.dve(lambda e: e.tensor_scalar(out=nre[:], in0=Lre[:, :, 1], scalar1=-1.0, scalar2=None, op0=ALU.add))
        ch.dve(lambda e: e.tensor_tensor(out=cre[:], in0=nre[:], in1=lr2[:], op=ALU.mult))
        ch.dve(lambda e: e.tensor_tensor(out=t1[:], in0=Lim[:, :, 1], in1=li2, op=ALU.mult))
        ch.dve(lambda e: e.tensor_tensor(out=cre[:], in0=cre[:], in1=t1[:], op=ALU.add))
        ch.dve(lambda e: e.tensor_tensor(out=cre[:], in0=cre[:], in1=den[:], op=ALU.mult))
        ch.dve(lambda e: e.tensor_tensor(out=cim[:], in0=Lim[:, :, 1], in1=lr2[:], op=ALU.mult))
        ch.dve(lambda e: e.tensor_tensor(out=t1[:], in0=nre[:], in1=li2, op=ALU.mult))
        ch.dve(lambda e: e.tensor_tensor(out=cim[:], in0=cim[:], in1=t1[:], op=ALU.subtract))
        ch.dve(lambda e: e.tensor_tensor(out=cim[:], in0=cim[:], in1=den[:], op=ALU.mult))
        w1 = f32([128, 64, 16], "w1"); w2 = f32([128, 64, 16], "w2"); w3 = f32([128, 64, 16], "w3")
        Brep = P.sbuf([128, 64, 8, 16], BF16, "Brep", st)
        bre = b2[:, 0, :].rearrange("r (g c) -> r g c", g=64); bim = b2[:, 1, :].rearrange("r (g c) -> r g c", g=64)
        creb = cre[:].unsqueeze(2).to_broadcast([128, 64, 16]); cimb = cim[:].unsqueeze(2).to_broadcast([128, 64, 16])
        ch.dve(lambda e: e.tensor_tensor(out=w1[:], in0=bre, in1=creb, op=ALU.mult))
        ch.dve(lambda e: e.tensor_tensor(out=w2[:], in0=bim, in1=cimb, op=ALU.mult))
        ch.dve(lambda e: e.tensor_tensor(out=w1[:], in0=w1[:], in1=w2[:], op=ALU.subtract))
        ch.dve(lambda e: e.tensor_tensor(out=w2[:], in0=bim, in1=creb, op=ALU.mult))
        ch.dve(lambda e: e.tensor_tensor(out=w3[:], in0=bre, in1=cimb, op=ALU.mult))
        ch.dve(lambda e: e.tensor_tensor(out=w2[:], in0=w2[:], in1=w3[:], op=ALU.add))
        ch.dve(lambda e: e.tensor_scalar(out=w1[:], in0=w1[:], scalar1=mlo, scalar2=None, op0=ALU.mult))
        ch.dve(lambda e: e.tensor_scalar(out=w2[:], in0=w2[:], scalar1=mhi, scalar2=None, op0=ALU.mult))
        ch.dve(lambda e: e.tensor_tensor(out=w1[:], in0=w1[:], in1=w2[:], op=ALU.subtract))
        ch.dve(lambda e: e.tensor_copy(out=Brep[:], in_=w1[:].unsqueeze(2).to_broadcast([128, 64, 8, 16])))
        pmask = f32([128, 64], "pmask")
        ch.dve(lambda e: e.tensor_copy(out=pmask[:, 0::2], in_=mlo.to_broadcast([128, 32])))
        ch.dve(lambda e: e.tensor_copy(out=pmask[:, 1::2], in_=mhi.to_broadcast([128, 32])))
        pmb = pmask[:].unsqueeze(2).to_broadcast([128, 64, 16])
        X = P.sbuf([128, 64, 8, 16], BF16, "Xcl", st)
        cre3 = c2[:, 0, :].rearrange("r (g c) -> r g c", g=64); cim3 = c2[:, 1, :].rearrange("r (g c) -> r g c", g=64)
        MI_re = T[3][:].rearrange("r g (j c) -> r g j c", j=8); MI_im = T[4][:].rearrange("r g (j c) -> r g j c", j=8)
        for ex in range(9):
            lre = Lre[:, :, ex].unsqueeze(2).to_broadcast([128, 64, 16]); lim = Lim[:, :, ex].unsqueeze(2).to_broadcast([128, 64, 16])
            ch.dve(lambda e, lre=lre: e.tensor_tensor(out=w1[:], in0=cre3, in1=lre, op=ALU.mult))
            ch.dve(lambda e, lim=lim: e.tensor_tensor(out=w2[:], in0=cim3, in1=lim, op=ALU.mult))
            ch.dve(lambda e: e.tensor_tensor(out=w1[:], in0=w1[:], in1=w2[:], op=ALU.subtract))
            ch.dve(lambda e, lim=lim: e.tensor_tensor(out=w2[:], in0=cre3, in1=lim, op=ALU.mult))
            ch.dve(lambda e, lre=lre: e.tensor_tensor(out=w3[:], in0=cim3, in1=lre, op=ALU.mult))
            ch.dve(lambda e: e.tensor_tensor(out=w2[:], in0=w2[:], in1=w3[:], op=ALU.add))
            if ex >= 1:
                ch.dve(lambda e, ex=ex: e.tensor_tensor(out=MI_re[:, :, ex - 1, :], in0=w1[:], in1=pmb, op=ALU.mult))
                ch.dve(lambda e, ex=ex: e.scalar_tensor_tensor(out=MI_im[:, :, ex - 1, :], in0=w2[:], scalar=-1.0, in1=pmb, op0=ALU.mult, op1=ALU.mult))
            if ex <= 7:
                ch.dve(lambda e: e.tensor_scalar(out=w3[:], in0=w1[:], scalar1=mlo, scalar2=None, op0=ALU.mult))
                ch.dve(lambda e, ex=ex: e.scalar_tensor_tensor(out=X[:, :, ex, :], in0=w2[:], scalar=mhi, in1=w3[:], op0=ALU.mult, op1=ALU.add))
        identf = f32([128, 128], "identf")
        ch.pool(lambda e: e.memset(identf[:], 1.0))
        ch.pool(lambda e: e.affine_select(out=identf[:], in_=identf[:], pattern=[[-1, 128]], compare_op=ALU.is_equal, fill=0.0, base=0, channel_multiplier=1))
        macc = f32([128, 4, 128], "macc"); dtmp = f32([128, 4, 8, 16], "dtmp")
        for gb in range(16):
            bank, bfree, bi = self.bank_next(ps)
            for gg in range(4):
                g = gb * 4 + gg
                ch.pe(lambda e, g=g, gg=gg: e.matmul(bank[:, gg * 128:(gg + 1) * 128], lhsT=Brep[:, g, :, :].rearrange("r i c -> r (i c)"),
                                                    rhs=X[:, g, :, :].rearrange("r j c -> r (j c)"), start=True, stop=True), extra=bfree)
            b3 = bank[:].rearrange("r (g x) -> r g x", g=4)
            ch.dve(lambda e: e.tensor_scalar(out=macc[:], in0=b3, scalar1=cst[:, 4:5], scalar2=None, op0=ALU.mult))
            for s in range(1, 8):
                ch.dve(lambda e, s=s: e.scalar_tensor_tensor(out=macc[:, :, 16 * s:128], in0=b3[:, :, 0:128 - 16 * s], scalar=cst[:, 4 + s:5 + s],
                                                             in1=macc[:, :, 16 * s:128], op0=ALU.mult, op1=ALU.add))
            ps["free"][bi] = [ch.last]
            dsl = dv[:, gb * 64:(gb + 1) * 64].rearrange("r (g c) -> r g c", g=4).unsqueeze(2).to_broadcast([128, 4, 8, 16])
            idb = identf[:].rearrange("r (j c) -> r j c", j=8).unsqueeze(1).to_broadcast([128, 4, 8, 16])
            ch.dve(lambda e: e.tensor_tensor(out=dtmp[:], in0=idb, in1=dsl, op=ALU.mult))
            ch.dve(lambda e, gb=gb: e.tensor_tensor(out=T[2][:, gb * 4:(gb + 1) * 4, :], in0=macc[:], in1=dtmp[:].rearrange("r g j c -> r g (j c)"), op=ALU.add))
        outs = []
        for t in range(2, 5):
            dst = self.Tabd[:, :, t, :].rearrange("k r (g x) -> r k g x", g=2)
            outs.append(P.dma(sp, sem, dst, T[t][:].rearrange("r (k g) x -> r k g x", g=2), deps=[ch.last]))
        P.barrier(outs[-1:])
        ps["free"] = [[] for _ in ps["banks"]]
        stA.close()
        stB = contextlib.ExitStack()
        st = stB
        T[0] = P.sbuf([128, 64, 128], BF16, "tab0", st)
        T[1] = P.sbuf([128, 64, 128], BF16, "tab1", st)
        ch.last = outs[-1]
        ch.pool(lambda e: e.memset(T[0][:], 0.0))
        ch.pool(lambda e: e.memset(T[1][:], 0.0))
        QG = 16
        NE = QG * 64
        a1 = f32([128, 2, NE], "a1"); b1 = f32([128, 2, NE], "b1")
        lr1 = f32([128, NE], "lr1"); lrdt1 = f32([128, NE], "lrdt1"); th1 = f32([128, NE], "th1")
        mg1 = f32([128, NE], "mg1"); an1 = f32([128, NE], "an1"); kf1 = f32([128, NE], "kf1"); ki1 = P.sbuf([128, NE], I32, "ki1", st)
        rd1 = f32([128, NE], "rd1"); cs1 = f32([128, NE], "cs1"); sn1 = f32([128, NE], "sn1")
        pa_re = f32([128, NE], "pa_re"); pa_im = f32([128, NE], "pa_im"); dn1 = f32([128, NE], "dn1")
        for qc in range(64 // QG):
            g0 = qc * QG
            sl = slice(g0 * 64, (g0 + QG) * 64)
            d1 = P.dma(sp, sem, a1[:, 0, :], self.s5_a1[0, sl].partition_broadcast(128), deps=[ch.last])
            d1 = P.dma(sp, sem, a1[:, 1, :], self.s5_a1[1, sl].partition_broadcast(128), deps=[ch.last])
            d1 = P.dma(sp, sem, b1[:], self.s5_b1[:, :, sl], deps=[ch.last])
            ch.last = d1
            dtb = dt2[:, g0:g0 + QG].unsqueeze(2).to_broadcast([128, QG, 64])
            v3 = lambda t: t[:].rearrange("r (g p) -> r g p", g=QG)
            ch.dve(lambda e: e.tensor_scalar(out=lr1[:], in0=a1[:, 0, :], scalar1=-1e-4, scalar2=None, op0=ALU.min))
            ch.dve(lambda e: e.tensor_tensor(out=v3(lrdt1), in0=v3(lr1), in1=dtb, op=ALU.mult))
            ch.dve(lambda e: e.tensor_tensor(out=v3(th1), in0=a1[:, 1, :].rearrange("r (g p) -> r g p", g=QG), in1=dtb, op=ALU.mult))
            for k_, col in enumerate((ncol, n1col)):
                ch.act(lambda e, col=col: e.activation(out=mg1[:], in_=lrdt1[:], func=AF.Exp, scale=col))
                ch.dve(lambda e, col=col: e.tensor_scalar(out=an1[:], in0=th1[:], scalar1=col, scalar2=None, op0=ALU.mult))
                self.trig(ch, an1[:], kf1[:], ki1[:], rd1[:], sn1[:], cs1[:])
                if k_ == 0:
                    ch.dve(lambda e: e.tensor_tensor(out=pa_re[:], in0=mg1[:], in1=cs1[:], op=ALU.mult))
                    ch.dve(lambda e: e.tensor_tensor(out=pa_im[:], in0=mg1[:], in1=sn1[:], op=ALU.mult))
                else:
                    ch.dve(lambda e: e.tensor_tensor(out=cs1[:], in0=mg1[:], in1=cs1[:], op=ALU.mult))
                    ch.dve(lambda e: e.tensor_tensor(out=sn1[:], in0=mg1[:], in1=sn1[:], op=ALU.mult))
                    ch.dve(lambda e: e.tensor_tensor(out=pa_re[:], in0=cs1[:], in1=pa_re[:], op=ALU.subtract))
                    ch.dve(lambda e: e.tensor_tensor(out=pa_im[:], in0=sn1[:], in1=pa_im[:], op=ALU.subtract))
            li1 = a1[:, 1, :]
            ch.dve(lambda e: e.tensor_tensor(out=dn1[:], in0=lr1[:], in1=lr1[:], op=ALU.mult))
            ch.dve(lambda e: e.tensor_tensor(out=kf1[:], in0=li1, in1=li1, op=ALU.mult))
            ch.dve(lambda e: e.tensor_tensor(out=dn1[:], in0=dn1[:], in1=kf1[:], op=ALU.add))
            ch.dve(lambda e: e.reciprocal(out=dn1[:], in_=dn1[:]))
            ch.dve(lambda e: e.tensor_tensor(out=cs1[:], in0=pa_re[:], in1=lr1[:], op=ALU.mult))
            ch.dve(lambda e: e.tensor_tensor(out=kf1[:], in0=pa_im[:], in1=li1, op=ALU.mult))
            ch.dve(lambda e: e.tensor_tensor(out=cs1[:], in0=cs1[:], in1=kf1[:], op=ALU.add))
            ch.dve(lambda e: e.tensor_tensor(out=cs1[:], in0=cs1[:], in1=dn1[:], op=ALU.mult))
            ch.dve(lambda e: e.tensor_tensor(out=sn1[:], in0=pa_im[:], in1=lr1[:], op=ALU.mult))
            ch.dve(lambda e: e.tensor_tensor(out=kf1[:], in0=pa_re[:], in1=li1, op=ALU.mult))
            ch.dve(lambda e: e.tensor_tensor(out=sn1[:], in0=sn1[:], in1=kf1[:], op=ALU.subtract))
            ch.dve(lambda e: e.tensor_tensor(out=sn1[:], in0=sn1[:], in1=dn1[:], op=ALU.mult))
            ch.dve(lambda e: e.tensor_tensor(out=mg1[:], in0=cs1[:], in1=b1[:, 0, :], op=ALU.mult))
            ch.dve(lambda e: e.tensor_tensor(out=kf1[:], in0=sn1[:], in1=b1[:, 1, :], op=ALU.mult))
            ch.dve(lambda e: e.tensor_tensor(out=mg1[:], in0=mg1[:], in1=kf1[:], op=ALU.subtract))
            ch.dve(lambda e: e.tensor_tensor(out=an1[:], in0=cs1[:], in1=b1[:, 1, :], op=ALU.mult))
            ch.dve(lambda e: e.tensor_tensor(out=kf1[:], in0=sn1[:], in1=b1[:, 0, :], op=ALU.mult))
            ch.dve(lambda e: e.tensor_tensor(out=an1[:], in0=an1[:], in1=kf1[:], op=ALU.add))
            for t, srcb in ((0, mg1), (1, an1)):
                s3 = srcb[:].rearrange("r (g p) -> r g p", g=QG)
                ch.dve(lambda e, t=t, s3=s3: e.tensor_copy(out=T[t][:, g0:g0 + QG:2, 0:64], in_=s3[:, 0::2, :]))
                ch.dve(lambda e, t=t, s3=s3: e.tensor_copy(out=T[t][:, g0 + 1:g0 + QG:2, 64:128], in_=s3[:, 1::2, :]))
        outs = []
        for t in range(2):
            dst = self.Tabd[:, :, t, :].rearrange("k r (g x) -> r k g x", g=2)
            outs.append(P.dma(sp, sem, dst, T[t][:].rearrange("r (k g) x -> r k g x", g=2), deps=[ch.last]))
        P.barrier(outs[-1:])
        stB.close()
        return outs[-1:], small

    def odd_O1(self, st, ps, src_ready):
        P = self.P
        pe, act, dve, pool, sp = P.pe, P.act, P.dve, P.pool, P.sp
        NQ = S // 8
        xs = self.xout.rearrange("(dc p) t -> p dc t", p=128)
        nb = self.alloc_norm(st)
        hTf = P.sbuf([128, 8, 8, NQ], BF16, "ohTf", st)
        Sel = P.sbuf([128, 8, 8, 128], BF16, "Sel", st)
        uring = Ring(P, 4, [128, NQ], BF16, "uring", st)
        ev = P.op(pool, lambda e: e.memset(Sel[:], 0.0))
        evs = []
        for m in range(8):
            for i in range(8):
                evs.append(P.op(pool, lambda e, m=m, i=i: e.affine_select(
                    out=Sel[:, m, i, 16 * i:16 * i + 16], in_=self.ones_bf[:, 0:16], pattern=[[-1, 16]],
                    compare_op=ALU.is_equal, fill=0.0, base=-16 * m, channel_multiplier=1), deps=[ev] + self.const_ev))
        sel_ev = [evs[-1]]
        ev_h = None
        for n in range(S // 512):
            ev_h = self.norm_tile(nb, xs, n * 512, 5, hTf, src_ready, ps, [], octet=n)
        stores = []
        k = 0
        for gc in range(8):
            for m in range(8):
                g = gc * 8 + m
                bank, bfree, bi = self.bank_next(ps)
                for i in range(8):
                    f = lambda e, i=i: e.matmul(bank[:, 0:NQ], lhsT=Sel[:, m, i, :], rhs=hTf[:, gc, i, :], start=(i == 0), stop=(i == 7))
                    if i < 7:
                        P.op_nosig(pe, f, deps=([ev_h] + bfree + sel_ev) if i == 0 else ())
                    else:
                        ev_mm = P.op(pe, f)
                s_, ub, usem, ufr = uring.next()
                if k % 2 == 0:
                    ev_c = P.op(act, lambda e: e.activation(out=ub[:], in_=bank[:, 0:NQ], func=AF.Copy), deps=[ev_mm] + ufr)
                else:
                    ev_c = P.op(dve, lambda e: e.tensor_copy(out=ub[:], in_=bank[:, 0:NQ]), deps=[ev_mm] + ufr)
                k += 1
                ps["free"][bi] = [ev_c]
                ev_st = P.dma(sp, usem, self.Ud[g], ub[:], deps=[ev_c])
                uring.free[s_] = [ev_st]
                stores.append(ev_st)
        return stores[-4:]

    def odd_O3(self, st, ps, small, ready, Ztm):
        P = self.P
        pe, act, dve, pool, sp = P.pe, P.act, P.dve, P.pool, P.sp
        NQ = S // 8
        NQT = NQ // 128
        f32 = lambda shape, nm: P.sbuf(shape, F32, nm, st)
        up = Ring(P, 3, [128, 2, NQ], BF16, "up", st)
        tabr = Ring(P, 3, [128, 5, 256], BF16, "tabr", st)
        iota = f32([128, 512], "iota")
        sem = P.newsem("o3c")
        ev_io = P.dma(sp, sem, iota[:], self.s5_iota)
        ang = f32([128, NQ], "o3ang"); kf = f32([128, NQ], "o3kf"); ki = P.sbuf([128, NQ], I32, "o3ki", st); red = f32([128, NQ], "o3red")
        cs = [f32([128, NQ], "o3cs") for _ in range(2)]; sn = [f32([128, NQ], "o3sn") for _ in range(2)]
        ta = f32([128, NQ], "o3ta"); tb = f32([128, NQ], "o3tb")
        pa = f32([128, NQ], "o3pa"); pb = f32([128, NQ], "o3pb")
        wre = f32([128, NQ], "o3wre"); wim = f32([128, NQ], "o3wim")
        zre = [f32([128, NQ], "o3zre") for _ in range(2)]; zim = [f32([128, NQ], "o3zim") for _ in range(2)]
        Hs = [[P.sbuf([128, NQ + 1], BF16, "Hs", st) for _ in range(2)] for _ in range(2)]
        hs_free = [[], []]
        rot_free = [[], []]
        ch = Chain(P)
        ev0 = None
        for par in range(2):
            for c_ in range(2):
                ev0 = P.op(pool, lambda e, par=par, c_=c_: e.memset(Hs[par][c_][:, 0:1], 0.0))
        ch.last = ev_io
        A = {}
        last = [None]

        def stageA(k):
            par = k % 2
            s, u, usem, ufr = up.next()
            ev_u = P.dma(sp, usem, u[:], self.Ud[2 * k:2 * k + 2].rearrange("g r q -> r g q"), deps=list(ready) + ufr)
            s2, tab, tsem, tfr = tabr.next()
            ev_t = P.dma(sp, tsem, tab[:], self.Tabd[k], deps=list(ready) + tfr)
            bEr, frr, ir = self.bank_next(ps)
            bEi, fri, ii = self.bank_next(ps)
            evE = []
            for t, bank, fr in ((0, bEr, frr), (1, bEi, fri)):
                P.op_nosig(pe, lambda e: e.matmul(bank[:, 0:NQ], lhsT=tab[:, t, 0:128], rhs=u[:, 0, :], start=True, stop=False),
                           deps=[ev_u, ev_t] + fr)
                evE.append(P.op(pe, lambda e: e.matmul(bank[:, 0:NQ], lhsT=tab[:, t, 128:256], rhs=u[:, 1, :], start=False, stop=True)))
            ch.dve(lambda e: e.tensor_scalar(out=ang[:], in0=iota[:, 0:NQ], scalar1=small["php"][:, k:k + 1], scalar2=None, op0=ALU.mult),
                   extra=rot_free[par])
            self.trig(ch, ang[:], kf[:], ki[:], red[:], sn[par][:], cs[par][:])
            c_, s_ = cs[par], sn[par]
            ch.dve(lambda e: e.tensor_tensor(out=ta[:], in0=c_[:], in1=bEr[:, 0:NQ], op=ALU.mult), extra=evE)
            ch.dve(lambda e: e.tensor_tensor(out=tb[:], in0=s_[:], in1=bEi[:, 0:NQ], op=ALU.mult))
            ch.dve(lambda e: e.tensor_tensor(out=wre[:], in0=ta[:], in1=tb[:], op=ALU.add))
            ch.dve(lambda e: e.tensor_tensor(out=ta[:], in0=c_[:], in1=bEi[:, 0:NQ], op=ALU.mult))
            ch.dve(lambda e: e.tensor_tensor(out=tb[:], in0=s_[:], in1=bEr[:, 0:NQ], op=ALU.mult))
            ps["free"][ir] = [ch.last]
            ps["free"][ii] = [ch.last]
            ch.dve(lambda e: e.tensor_tensor(out=wim[:], in0=ta[:], in1=tb[:], op=ALU.subtract))
            r8 = small["r8p"][:, k:k + 1].to_broadcast([128, NQ])
            ch.dve(lambda e: e.tensor_tensor_scan(out=zre[par][:], data0=r8, data1=wre[:], initial=0.0, op0=ALU.mult, op1=ALU.add))
            ev_z = ch.dve(lambda e: e.tensor_tensor_scan(out=zim[par][:], data0=r8, data1=wim[:], initial=0.0, op0=ALU.mult, op1=ALU.add))
            A[k] = (s, u, s2, tab, ev_z)

        def stageB(k):
            par = k % 2
            s, u, s2, tab, ev_z = A.pop(k)
            c_, s_, zr, zi = cs[par], sn[par], zre[par], zim[par]
            hre, him = Hs[par]
            e1 = P.op(pool, lambda e: e.tensor_tensor(out=pa[:], in0=c_[:], in1=zr[:], op=ALU.mult), deps=[ev_z, ev0] + rot_free[1 - par])
            e2 = P.op(pool, lambda e: e.tensor_tensor(out=pb[:], in0=s_[:], in1=zi[:], op=ALU.mult), deps=[ev_z] + rot_free[1 - par])
            e3 = P.op(pool, lambda e: e.tensor_tensor(out=hre[:, 1:NQ + 1], in0=pa[:], in1=pb[:], op=ALU.subtract), deps=[e1, e2] + hs_free[par])
            e4 = P.op(pool, lambda e: e.tensor_tensor(out=pa[:], in0=c_[:], in1=zi[:], op=ALU.mult), deps=[e3])
            e5 = P.op(pool, lambda e: e.tensor_tensor(out=pb[:], in0=s_[:], in1=zr[:], op=ALU.mult), deps=[e3])
            ev_H = P.op(pool, lambda e: e.tensor_tensor(out=him[:, 1:NQ + 1], in0=pa[:], in1=pb[:], op=ALU.add), deps=[e4, e5])
            rot_free[par] = [ev_H]
            ev_y = None
            for qt in range(NQT):
                by, fry, iy = self.bank_next(ps)
                qs = slice(qt * 128, (qt + 1) * 128)
                P.op_nosig(pe, lambda e: e.matmul(by[:, 0:256], lhsT=hre[:, qs], rhs=tab[:, 3, :], start=True, stop=False), deps=[ev_H] + fry)
                P.op_nosig(pe, lambda e: e.matmul(by[:, 0:256], lhsT=him[:, qs], rhs=tab[:, 4, :], start=False, stop=False))
                P.op_nosig(pe, lambda e: e.matmul(by[:, 0:128], lhsT=u[:, 0, qs], rhs=tab[:, 2, 0:128], start=False, stop=False))
                ev_y = P.op(pe, lambda e: e.matmul(by[:, 128:256], lhsT=u[:, 1, qs], rhs=tab[:, 2, 128:256], start=False, stop=True))
                dst = Ztm[:, qt, :, 32 * k:32 * k + 32].rearrange("q j (g c) -> q g j c", g=2)
                src = by[:, 0:256].rearrange("q (g j c) -> q g j c", g=2, j=8)
                last[0] = P.op(act, lambda e: e.activation(out=dst, in_=src, func=AF.Gelu_apprx_tanh), deps=[ev_y])
                ps["free"][iy] = [last[0]]
            hs_free[par] = [ev_y]
            up.free[s] = [ev_y]
            tabr.free[s2] = [ev_y]

        stageA(0)
        for k in range(32):
            if k + 1 < 32:
                stageA(k + 1)
            stageB(k)
        return [last[0]]

    def odd_O4(self, st, ps, Ztm, z_ready, src_ready):
        P = self.P
        pe, act, dve, pool, sp = P.pe, P.act, P.dve, P.pool, P.sp
        NQT = S // 1024
        xs = self.xout.rearrange("(dc p) t -> p dc t", p=128)
        Wg = P.sbuf([128, 8, 2048], BF16, "Wglu", st)
        ident = P.sbuf([128, 128], BF16, "oident", st)
        zT = [P.sbuf([128, 8, 1024], BF16, "zT", st) for _ in range(2)]
        zT_free = [[], []]
        sg = [P.sbuf([128, 512], F32, "osg", st) for _ in range(2)]
        sg_free = [[], []]
        gt = [P.sbuf([128, 512], F32, "ogt", st) for _ in range(2)]
        xc = Ring(P, 4, [128, 512], F32, "o4xc", st)
        st_sems = [P.newsem("o4st%d" % j) for j in range(4)]
        sem = P.newsem("o4w", sw=True)
        ev_w = P.dma(pool, sem, Wg[:], self.wglu_b.rearrange("(p dc) f -> p dc f", dc=8), deps=[self.conv_ev["odd"]])
        e1 = P.op(pool, lambda e: e.memset(ident[:], 1.0))
        e1 = P.op(pool, lambda e: e.affine_select(out=ident[:], in_=ident[:], pattern=[[-1, 128]], compare_op=ALU.is_equal, fill=0.0,
                                                  base=0, channel_multiplier=1), deps=[e1])
        stores = []
        ui = 0
        for qt in range(NQT):
            z = zT[qt % 2]
            evz = None
            for j in range(8):
                for half in range(2):
                    tbank, tfree, ti = self.bank_next(ps)
                    t16 = tbank[:].bitcast(BF16)
                    for cc in range(4):
                        chn = half * 4 + cc
                        f = lambda e, cc=cc, chn=chn: e.transpose(t16[:, cc * 128:(cc + 1) * 128], Ztm[:, qt, j, chn * 128:(chn + 1) * 128], ident[:])
                        if cc < 3:
                            P.op_nosig(pe, f, deps=(list(z_ready) + tfree + [e1]) if cc == 0 else ())
                        else:
                            ev_t = P.op(pe, f)
                    evz = P.op(act if half == 0 else dve,
                               (lambda e: e.activation(out=z[:, half * 4:(half + 1) * 4, j::8], in_=t16[:, 0:512].rearrange("p (c t) -> p c t", c=4), func=AF.Copy))
                               if half == 0 else
                               (lambda e: e.tensor_copy(out=z[:, half * 4:(half + 1) * 4, j::8], in_=t16[:, 0:512].rearrange("p (c t) -> p c t", c=4))),
                               deps=[ev_t] + zT_free[qt % 2])
                    ps["free"][ti] = [evz]
            z_evs = [(P.act.sem, P.act.sem.v), (P.dve.sem, P.dve.sem.v)]
            last_mm = None
            for ns in range(2):
                t0 = qt * 1024 + ns * 512
                for dc in range(8):
                    bv, frv, iv = self.bank_next(ps)
                    bg, frg, ig = self.bank_next(ps)
                    for bank, off, fr in ((bv, 0, frv), (bg, 1024, frg)):
                        for cc in range(8):
                            f = lambda e, bank=bank, off=off, cc=cc: e.matmul(bank[:], lhsT=Wg[:, cc, off + dc * 128:off + (dc + 1) * 128],
                                                                             rhs=z[:, cc, ns * 512:(ns + 1) * 512], start=(cc == 0), stop=(cc == 7))
                            if cc < 7:
                                P.op_nosig(pe, f, deps=(z_evs + [ev_w] + fr) if cc == 0 else ())
                            else:
                                ev_mm = P.op(pe, f)
                    last_mm = ev_mm
                    si = ui % 2
                    ui += 1
                    ev_s = P.op(act, lambda e: e.activation(out=sg[si][:], in_=bg[:], func=AF.Sigmoid), deps=[ev_mm] + sg_free[si])
                    ps["free"][ig] = [ev_s]
                    ev_m = P.op(dve, lambda e: e.tensor_tensor(out=gt[si][:], in0=bv[:], in1=sg[si][:], op=ALU.mult), deps=[ev_s, ev_mm])
                    ps["free"][iv] = [ev_m]
                    sx, xt, xsem, xfr = xc.next()
                    ev_ld = P.dma(sp, xsem, xt[:], xs[:, dc, t0:t0 + 512], deps=list(src_ready) + xfr)
                    ev_r = P.op(dve, lambda e: e.tensor_tensor(out=xt[:], in0=gt[si][:], in1=xt[:], op=ALU.add), deps=[ev_m, ev_ld])
                    sg_free[si] = [ev_r]
                    ev_st = P.dma(sp, st_sems[sx], xs[:, dc, t0:t0 + 512], xt[:], deps=[ev_r])
                    xc.free[sx] = [ev_st]
                    stores.append(ev_st)
            zT_free[qt % 2] = [last_mm]
        return stores[-4:]

    def odd_mixer(self, ps, src_ready):
        P = self.P
        NQ = S // 8
        reset = lambda: ps.__setitem__("free", [[] for _ in ps["banks"]])
        so = contextlib.ExitStack()
        small = {"r8p": P.sbuf([128, 32], F32, "r8p", so), "php": P.sbuf([128, 32], F32, "php", so)}
        with contextlib.ExitStack() as st:
            ev1 = self.odd_O1(st, ps, src_ready)
        P.barrier(ev1)
        reset()
        with contextlib.ExitStack() as st:
            ev2, small = self.odd_tables(st, ps, small)
        reset()
        with contextlib.ExitStack() as st0:
            Ztm = P.sbuf([128, S // 1024, 8, 1024], BF16, "Ztm", st0)
            self.bg()
            with contextlib.ExitStack() as st:
                ev3 = self.odd_O3(st, ps, small, ev1 + ev2, Ztm)
            P.barrier(ev3)
            reset()
            with contextlib.ExitStack() as st:
                ev4 = self.odd_O4(st, ps, Ztm, ev3, src_ready)
            P.barrier(ev4)
            reset()
        so.close()
        return ev4

    def build(self):
        P = self.P
        nc = self.nc
        cfg = self.cfg
        phases = cfg["phases"]
        self.setup_consts()
        self.eps_sb = P.sbuf([128, 1], F32, "eps")
        ev = P.op(P.pool, lambda e: e.memset(self.eps_sb[:], EPS))
        self.const_ev.append(ev)
        self.pi2_sb = P.sbuf([128, 1], F32, "pi2")
        ev = P.op(P.pool, lambda e: e.memset(self.pi2_sb[:], math.pi / 2))
        self.const_ev.append(ev)
        if "even" in phases:
            self.decl_even()
        if "odd" in phases:
            self.decl_odd()
        def conv(ph):
            save = P.phase_sems
            P.phase_sems = None
            if ph.startswith("ffn"):
                self.convert_ffn(int(ph[3:]))
            elif ph == "even":
                self.convert_even()
            elif ph == "odd":
                self.convert_odd()
            P.phase_sems = save
        conv(phases[0])
        banks = [P.psum([128, 512], F32, "bank") for _ in range(8)]
        psf = {"units": [(banks[0], banks[1]), (banks[2], banks[3]), (banks[4], banks[5])], "misc": [banks[6], banks[7]]}
        psm = {"banks": banks, "free": [[] for _ in banks], "i": 0}
        ready = []
        src = self.x_in
        for pi, ph in enumerate(phases):
            P.begin_phase()
            nxt = phases[pi + 1] if pi + 1 < len(phases) else None
            self.bg = (lambda nxt=nxt: conv(nxt)) if nxt is not None else (lambda: None)
            if ph.startswith("ffn"):
                i = int(ph[3:])
                with contextlib.ExitStack() as st:
                    b = self.alloc_ffn(st)
                    ready = self.ffn(b, i, i, src, self.xout, ready, psf)
                P.barrier(ready)
            elif ph == "copy":
                csem = P.newsem("copy")
                ready = [P.dma(P.sp, csem, self.xout, self.x_in)]
                self.bg()
            elif ph == "even":
                ready = self.even_mixer(psm, ready)
            elif ph == "odd":
                ready = self.odd_mixer(psm, ready)
            P.barrier(ready)
            ready = []
            P.end_phase()
            src = self.xout
        P.es.close()
        return nc


POOL_WINDOWS = (2, 4, 8, 16)


def prep_shared(inp, phases):
    m = {}
    g = [inp["ffn_norm"][l, s] for l in range(2) for s in range(2)] + [inp["mix_norm"][0], inp["mix_norm"][1]]
    m["gains"] = np.ascontiguousarray(np.stack([_lay_gain(np.asarray(x, np.float32)) for x in g], axis=1))
    wg, wu, wd = inp["ffn_w_gate"], inp["ffn_w_up"], inp["ffn_w_down"]
    m["wgu_f"] = np.stack([_lay_wgu(wg[l, s], wu[l, s]) for l in range(2) for s in range(2)])
    m["wd_f"] = np.stack([_lay_wd(wd[l, s]) for l in range(2) for s in range(2)])
    if "even" in phases:
        m["win_f"] = np.ascontiguousarray(inp["ev_w_in"][0].reshape(8, 128, 2048).transpose(1, 0, 2)).reshape(1024, 2048)
        m["wout_f"] = np.ascontiguousarray(inp["ev_w_out"][0].reshape(8, 128, 1024).transpose(1, 0, 2)).reshape(512, 2048)
        m["poolw_f"] = np.ascontiguousarray(inp["ev_pool_w"][0].transpose(1, 0, 2)).reshape(128, 512)
        m["pscale"] = np.ascontiguousarray(inp["ev_pool_scale"][0].reshape(4, 128).T)
        m["qkn"] = np.ascontiguousarray(np.stack([inp["ev_q_norm"][0], inp["ev_k_norm"][0]]))
        cst = np.zeros((128, 72), np.float32)
        cst[:, 0:8] = (500000.0 ** (-np.arange(0, 16, 2, dtype=np.float32) / 16.0)).astype(np.float32)[None, :]
        for pc, w in enumerate(POOL_WINDOWS):
            cst[:, 8 + pc * 16:8 + (pc + 1) * 16] = (1.0 / np.minimum(np.arange(1, 17), w)).astype(np.float32)[None, :]
        m["cst"] = cst
    if "odd" in phases:
        f = lambda a: np.asarray(a, np.float32)
        a_re, a_im = f(inp["s5_a_re"][0]), f(inp["s5_a_im"][0])
        m["s5_a1"] = np.ascontiguousarray(np.stack([a_re.reshape(-1), a_im.reshape(-1)]))
        m["s5_a2"] = np.ascontiguousarray(np.stack([np.tile(a_re.T, (2, 1)), np.tile(a_im.T, (2, 1))], axis=1))
        m["s5_ldt"] = np.ascontiguousarray(f(inp["s5_log_dt"][0]))
        b_re, b_im = f(inp["s5_b_re"][0]), f(inp["s5_b_im"][0])
        lay1 = lambda b: np.tile(b.transpose(2, 0, 1).reshape(16, 4096), (8, 1))
        m["s5_b1"] = np.ascontiguousarray(np.stack([lay1(b_re), lay1(b_im)], axis=1))
        lay2 = lambda b: np.tile(b.transpose(1, 0, 2).reshape(64, 1024), (2, 1))
        m["s5_b2"] = np.ascontiguousarray(np.stack([lay2(b_re), lay2(b_im)], axis=1))
        c_re, c_im = f(inp["s5_c_re"][0]), f(inp["s5_c_im"][0])
        lay3 = lambda c: np.tile(c.transpose(2, 0, 1).reshape(64, 1024), (2, 1))
        m["s5_c2"] = np.ascontiguousarray(np.stack([lay3(c_re), lay3(c_im)], axis=1))
        m["s5_dv"] = np.ascontiguousarray(f(inp["s5_d"][0]))
        cst = np.zeros((128, 12), np.float32)
        ii = np.arange(128) // 16
        cst[:, 0] = 7 - ii
        cst[:, 1] = 8 - ii
        cst[:, 2] = (np.arange(128) < 64)
        cst[:, 3] = (np.arange(128) >= 64)
        for s_ in range(8):
            cst[:, 4 + s_] = (ii == s_)
        m["s5_cst"] = cst
        m["s5_iota"] = np.ascontiguousarray(np.tile(np.arange(512, dtype=np.float32)[None, :], (128, 1)))
        m["wglu_f"] = np.ascontiguousarray(f(inp["s5_w_glu"][0]).reshape(8, 128, 2048).transpose(1, 0, 2)).reshape(1024, 2048)
    return m


def prep_core(inp, b, phases):
    m = {"x_in": np.ascontiguousarray(np.asarray(inp["x"][b], np.float32).T)}
    if "even" in phases:
        m["pos"] = np.ascontiguousarray(np.asarray(inp["positions"][b], np.int32).reshape(-1, 128).T)
    return m


PHASES = ["ffn0", "even", "ffn1", "ffn2", "odd", "ffn3"]


_NC_CACHE = {}


def kernel(**inputs):
    inp = {k: np.asarray(v) for k, v in inputs.items()}
    phases = PHASES
    if "nc" not in _NC_CACHE:
        _NC_CACHE["nc"] = Builder({"phases": phases, "S": 4096}).build()
    nc = _NC_CACHE["nc"]
    shared = prep_shared(inp, phases)
    nb = inp["x"].shape[0]
    in_maps = []
    for b in range(nb):
        m = dict(shared)
        m.update(prep_core(inp, b, phases))
        in_maps.append(m)
    res = run_bass_kernel_spmd(nc, in_maps, core_ids=list(range(nb)))
    out = np.stack([np.ascontiguousarray(np.asarray(res.results[b]["xout"]).T) for b in range(nb)], axis=0)
    return out.astype(np.float32)
```
